# Optimizing a Trainium2 kernel written in Bass

```python
import math
import jax, jax.numpy as jnp
from jax import lax
import numpy as np

D_MODEL = 1024
BATCH = 8
SEQ = 2048
DEPTH = 2
DEC_BATCH = 128
DEC_SEQ = 4
PAST_LEN = 16384
PAGE_SIZE = 128

MLSTM_HEADS = 4
MLSTM_DH = D_MODEL // 16
MLSTM_W = MLSTM_HEADS * MLSTM_DH
RET_HEADS = 4
RET_DH = D_MODEL // 8
RET_W = RET_HEADS * RET_DH
LRU_W = D_MODEL // 4
LRU_BLOCKS = 4
LRU_BW = LRU_W // LRU_BLOCKS
CONV_W = 4
LRU_C = 8.0
MIX_W = MLSTM_W + RET_W + LRU_W
IN_W = 4 * MLSTM_W + 2 * MLSTM_HEADS + 4 * RET_W + 2 * LRU_W
D_FF = ((8 * D_MODEL // 3 + 127) // 128) * 128
CHUNK = 64
ROPE_BASE = 10000.0
EPS = 1e-6

kernel_name = 'hybrid_mlstm_retention_rglru_macaron_step'


def rmsnorm(x, g):
    xf = x.astype(jnp.float32)
    y = xf * lax.rsqrt(jnp.mean(xf * xf, axis=-1, keepdims=True) + EPS)
    return (y * g.astype(jnp.float32)).astype(x.dtype)


def head_norm(x, g):
    H, d = x.shape[-2], x.shape[-1]
    xf = x.astype(jnp.float32)
    xc = xf - jnp.mean(xf, axis=-1, keepdims=True)
    var = jnp.mean(xc * xc, axis=-1, keepdims=True)
    return xc * lax.rsqrt(var + EPS) * g.astype(jnp.float32).reshape(H, d)


def swiglu(x, w_gu, w_down):
    gate, up = jnp.split(x @ w_gu, 2, axis=-1)
    return (jax.nn.silu(gate) * up) @ w_down


def rope(x, pos):
    d = x.shape[-1]
    inv_freq = ROPE_BASE ** (-jnp.arange(0, d, 2, dtype=jnp.float32) / d)
    ang = pos[:, None] * inv_freq[None, :]
    cos = jnp.cos(ang)[None, :, None, :]
    sin = jnp.sin(ang)[None, :, None, :]
    x1, x2 = jnp.split(x, 2, axis=-1)
    return jnp.concatenate([x1 * cos - x2 * sin, x1 * sin + x2 * cos], axis=-1)


def to_chunks(a, L):
    B, T, H = a.shape[:3]
    a = a.reshape((B, T // L, L, H) + a.shape[3:])
    return a.transpose((1, 0, 3, 2) + tuple(range(4, a.ndim)))


def from_chunks(a):
    NC, B, H, L, d = a.shape
    return a.transpose(1, 0, 3, 2, 4).reshape(B, NC * L, H, d)


def mlstm_chunkwise(q, k, v, li, lf, C0, n0, m0):
    T = q.shape[1]
    L = math.gcd(T, CHUNK)
    t_idx = jnp.arange(L)
    causal = t_idx[:, None] >= t_idx[None, :]

    def step(carry, inp):
        C, n, m = carry
        qc, kc, vc, lic, lfc = inp
        b = jnp.cumsum(lfc, axis=-1)
        dlog = jnp.where(causal, b[..., :, None] - b[..., None, :] + lic[..., None, :], -jnp.inf)
        inter = b + m[..., None]
        mt = jnp.maximum(inter, jnp.max(dlog, axis=-1))
        s = jnp.einsum('bhtd,bhsd->bhts', qc, kc) * jnp.exp(dlog - mt[..., None])
        iw = jnp.exp(inter - mt)
        num = jnp.einsum('bhts,bhsv->bhtv', s, vc) + iw[..., None] * jnp.einsum('bhtk,bhkv->bhtv', qc, C)
        den = jnp.sum(s, axis=-1) + iw * jnp.einsum('bhtk,bhk->bht', qc, n)
        h = num / jnp.maximum(jnp.abs(den), jnp.exp(-mt))[..., None]
        m_new = mt[..., -1]
        w_s = jnp.exp(b[..., -1:] - b + lic - m_new[..., None])
        f_c = jnp.exp(b[..., -1] + m - m_new)
        C_new = f_c[..., None, None] * C + jnp.einsum('bhs,bhsk,bhsv->bhkv', w_s, kc, vc)
        n_new = f_c[..., None] * n + jnp.einsum('bhs,bhsk->bhk', w_s, kc)
        return (C_new, n_new, m_new), h

    xs = (to_chunks(q, L), to_chunks(k, L), to_chunks(v, L), to_chunks(li, L), to_chunks(lf, L))
    (C1, n1, m1), h = lax.scan(step, (C0, n0, m0), xs)
    return from_chunks(h), C1, n1, m1


def retention_chunkwise(q, k, v, S0):
    T, H = q.shape[1], q.shape[2]
    L = math.gcd(T, CHUNK)
    lg = jnp.log1p(-(2.0 ** (-5.0 - jnp.arange(H, dtype=jnp.float32))))
    idx = jnp.arange(L, dtype=jnp.float32)
    diff = idx[:, None] - idx[None, :]
    dmask = jnp.where(diff >= 0, jnp.exp(jnp.maximum(diff, 0.0)[None] * lg[:, None, None]), 0.0)
    q_dec = jnp.exp((idx[None, :] + 1.0) * lg[:, None])
    k_dec = jnp.exp((L - 1.0 - idx[None, :]) * lg[:, None])
    chunk_dec = jnp.exp(L * lg)

    def step(S, inp):
        qc, kc, vc = inp
        s = jnp.einsum('bhtd,bhsd->bhts', qc, kc) * dmask
        o = jnp.einsum('bhts,bhsv->bhtv', s, vc) + q_dec[:, :, None] * jnp.einsum('bhtk,bhkv->bhtv', qc, S)
        S_new = chunk_dec[:, None, None] * S + jnp.einsum('hs,bhsk,bhsv->bhkv', k_dec, kc, vc)
        return S_new, o

    S1, o = lax.scan(step, S0, (to_chunks(q, L), to_chunks(k, L), to_chunks(v, L)))
    return from_chunks(o), S1


def rglru(xb, yb, buf, h0, w_conv, b_conv, w_a, b_a, w_i, b_i, lam):
    B, T, W = xb.shape
    xpad = jnp.concatenate([buf.astype(jnp.float32), xb.astype(jnp.float32)], axis=1)
    xc = b_conv.astype(jnp.float32)
    for j in range(CONV_W):
        xc = xc + w_conv[j].astype(jnp.float32) * xpad[:, j:j + T]
    blk = xc.reshape(B, T, LRU_BLOCKS, LRU_BW)
    r = jax.nn.sigmoid(jnp.einsum('btnd,nde->btne', blk, w_a.astype(jnp.float32)).reshape(B, T, W) + b_a)
    i = jax.nn.sigmoid(jnp.einsum('btnd,nde->btne', blk, w_i.astype(jnp.float32)).reshape(B, T, W) + b_i)
    log_a = -LRU_C * r * jax.nn.softplus(-lam.astype(jnp.float32))
    a = jnp.exp(log_a)
    u = jnp.sqrt(-jnp.expm1(2.0 * log_a)) * i * xc

    def step(h, au):
        h = au[0] * h + au[1]
        return h, h

    hT, hs = lax.scan(step, h0, (a.transpose(1, 0, 2), u.transpose(1, 0, 2)))
    y = hs.transpose(1, 0, 2) * jax.nn.gelu(yb.astype(jnp.float32))
    return y, hT, xpad[:, T:]


def mixer(h, offset, C0, n0, m0, S0, h0, buf0, w_in, b_if, g_mn, g_rn,
          w_conv, b_conv, w_a, b_a, w_i, b_i, lam, w_out):
    B, T, _ = h.shape
    f32 = jnp.float32
    sizes = (MLSTM_W,) * 4 + (MLSTM_HEADS,) * 2 + (RET_W,) * 4 + (LRU_W,) * 2
    cuts = []
    acc = 0
    for s in sizes[:-1]:
        acc += s
        cuts.append(acc)
    mq, mk, mv, mo, mi, mf, rq, rk, rv, rg, lx, ly = jnp.split((h @ w_in).astype(f32), cuts, axis=-1)

    hm_ = lambda a: a.reshape(B, T, MLSTM_HEADS, MLSTM_DH)
    li = mi + b_if[:MLSTM_HEADS]
    lf = jax.nn.log_sigmoid(mf + b_if[MLSTM_HEADS:])
    hm, C1, n1, m1 = mlstm_chunkwise(hm_(mq), hm_(mk) * MLSTM_DH ** -0.5, hm_(mv), li, lf,
                                     C0.astype(f32), n0.astype(f32), m0.astype(f32))
    ym = head_norm(jax.nn.sigmoid(hm_(mo)) * hm, g_mn).reshape(B, T, MLSTM_W)

    hr_ = lambda a: a.reshape(B, T, RET_HEADS, RET_DH)
    pos = offset + jnp.arange(T, dtype=f32)
    hr, S1 = retention_chunkwise(rope(hr_(rq), pos), rope(hr_(rk), pos) * RET_DH ** -0.5,
                                 hr_(rv), S0.astype(f32))
    yr = head_norm(hr, g_rn).reshape(B, T, RET_W) * jax.nn.silu(rg)

    yl, h1, buf1 = rglru(lx, ly, buf0, h0.astype(f32), w_conv, b_conv, w_a, b_a, w_i, b_i, lam)

    y = jnp.concatenate([ym, yr, yl], axis=-1).astype(h.dtype) @ w_out
    return y, (C1, n1, m1, S1, h1, buf1)


def setup_inputs(seed: int = 0) -> dict:
    key = jax.random.key(seed)
    ks = jax.random.split(key, 32)
    nrm = lambda k, shape, scale: scale * jax.random.normal(k, shape, jnp.float32)
    gain = lambda k, shape: 1.0 + 0.01 * jax.random.normal(k, shape, jnp.float32)
    u = jax.random.uniform(ks[20], (DEPTH, LRU_W), jnp.float32, 0.9, 0.999)
    s = u ** (1.0 / LRU_C)
    b_if = jnp.concatenate([nrm(ks[13], (DEPTH, MLSTM_HEADS), 0.1),
                            jnp.linspace(3.0, 6.0, MLSTM_HEADS, dtype=jnp.float32)[None, :]
                            + nrm(ks[14], (DEPTH, MLSTM_HEADS), 0.01)], axis=-1)
    return {
        'x_prompt': nrm(ks[0], (BATCH, SEQ, D_MODEL), 1.0),
        'x_sample': nrm(ks[1], (DEC_BATCH, DEC_SEQ, D_MODEL), 1.0),
        'state_mlstm_C': nrm(ks[2], (DEPTH, DEC_BATCH, MLSTM_HEADS, MLSTM_DH, MLSTM_DH), 0.3),
        'state_mlstm_n': nrm(ks[3], (DEPTH, DEC_BATCH, MLSTM_HEADS, MLSTM_DH), 0.3),
        'state_mlstm_m': nrm(ks[4], (DEPTH, DEC_BATCH, MLSTM_HEADS), 1.0),
        'state_ret_S': nrm(ks[5], (DEPTH, DEC_BATCH, RET_HEADS, RET_DH, RET_DH), 0.3),
        'state_lru_h': nrm(ks[6], (DEPTH, DEC_BATCH, LRU_W), 0.5),
        'state_lru_conv': nrm(ks[7], (DEPTH, DEC_BATCH, CONV_W - 1, LRU_W), 1.0),
        'norm_ffn1': gain(ks[8], (DEPTH, D_MODEL)),
        'w_ffn1_gu': nrm(ks[9], (DEPTH, D_MODEL, 2 * D_FF), D_MODEL ** -0.5),
        'w_ffn1_down': nrm(ks[10], (DEPTH, D_FF, D_MODEL), D_FF ** -0.5),
        'norm_mix': gain(ks[11], (DEPTH, D_MODEL)),
        'w_in': nrm(ks[12], (DEPTH, D_MODEL, IN_W), D_MODEL ** -0.5),
        'b_mlstm_if': b_if,
        'g_mlstm_norm': gain(ks[15], (DEPTH, MLSTM_W)),
        'g_ret_norm': gain(ks[16], (DEPTH, RET_W)),
        'w_conv': nrm(ks[17], (DEPTH, CONV_W, LRU_W), CONV_W ** -0.5),
        'b_conv': nrm(ks[18], (DEPTH, LRU_W), 0.01),
        'w_lru_a': nrm(ks[19], (DEPTH, LRU_BLOCKS, LRU_BW, LRU_BW), LRU_BW ** -0.5),
        'b_lru_a': nrm(ks[21], (DEPTH, LRU_W), 0.01),
        'w_lru_i': nrm(ks[22], (DEPTH, LRU_BLOCKS, LRU_BW, LRU_BW), LRU_BW ** -0.5),
        'b_lru_i': nrm(ks[23], (DEPTH, LRU_W), 0.01),
        'lru_lambda': jnp.log(s) - jnp.log1p(-s),
        'w_out': nrm(ks[24], (DEPTH, MIX_W, D_MODEL), MIX_W ** -0.5),
        'norm_ffn2': gain(ks[25], (DEPTH, D_MODEL)),
        'w_ffn2_gu': nrm(ks[26], (DEPTH, D_MODEL, 2 * D_FF), D_MODEL ** -0.5),
        'w_ffn2_down': nrm(ks[27], (DEPTH, D_FF, D_MODEL), D_FF ** -0.5),
        'norm_final': gain(ks[28], (D_MODEL,)),
    }


def reference(x_prompt, x_sample, state_mlstm_C, state_mlstm_n, state_mlstm_m, state_ret_S,
              state_lru_h, state_lru_conv, norm_ffn1, w_ffn1_gu, w_ffn1_down, norm_mix, w_in,
              b_mlstm_if, g_mlstm_norm, g_ret_norm, w_conv, b_conv, w_lru_a, b_lru_a, w_lru_i,
              b_lru_i, lru_lambda, w_out, norm_ffn2, w_ffn2_gu, w_ffn2_down, norm_final):
    def run(x, offset, states):
        outs = ([], [], [], [], [], [])
        for l in range(DEPTH):
            x = x + 0.5 * swiglu(rmsnorm(x, norm_ffn1[l]), w_ffn1_gu[l], w_ffn1_down[l])
            y, new = mixer(rmsnorm(x, norm_mix[l]), offset, *[s[l] for s in states],
                           w_in[l], b_mlstm_if[l], g_mlstm_norm[l], g_ret_norm[l],
                           w_conv[l], b_conv[l], w_lru_a[l], b_lru_a[l], w_lru_i[l], b_lru_i[l],
                           lru_lambda[l], w_out[l])
            x = x + y
            x = x + 0.5 * swiglu(rmsnorm(x, norm_ffn2[l]), w_ffn2_gu[l], w_ffn2_down[l])
            for acc, s in zip(outs, new):
                acc.append(s)
        return rmsnorm(x, norm_final), [jnp.stack(acc) for acc in outs]

    Bp = x_prompt.shape[0]
    f32 = jnp.float32
    zero_states = (jnp.zeros((DEPTH, Bp, MLSTM_HEADS, MLSTM_DH, MLSTM_DH), f32),
                   jnp.zeros((DEPTH, Bp, MLSTM_HEADS, MLSTM_DH), f32),
                   jnp.zeros((DEPTH, Bp, MLSTM_HEADS), f32),
                   jnp.zeros((DEPTH, Bp, RET_HEADS, RET_DH, RET_DH), f32),
                   jnp.zeros((DEPTH, Bp, LRU_W), f32),
                   jnp.zeros((DEPTH, Bp, CONV_W - 1, LRU_W), f32))
    y_prompt, (p_C, p_n, p_m, p_S, p_h, p_conv) = run(x_prompt, 0.0, zero_states)
    y_sample, (s_C, s_n, s_m, s_S, s_h, s_conv) = run(
        x_sample, float(PAST_LEN),
        (state_mlstm_C, state_mlstm_n, state_mlstm_m, state_ret_S, state_lru_h, state_lru_conv))
    return (y_prompt, y_sample, p_C, p_n, p_m, p_S, p_h, p_conv, s_C, s_n, s_m, s_S, s_h, s_conv)
```

```python
import contextlib
import numpy as np
import concourse.bass as bass
import concourse.mybir as mybir
from concourse.bass_utils import run_bass_kernel_spmd

F32 = mybir.dt.float32
BF16 = mybir.dt.bfloat16
AF = mybir.ActivationFunctionType
ALU = mybir.AluOpType

D = 1024
KC = 8
DFF = 2816
NFC = 22
NPROMPT = 2048
NSAMP = 64
N = NPROMPT + NSAMP
TT = [(0, 512), (512, 512), (1024, 512), (1536, 512), (2048, 64)]
DEPTH = 2
EPS = 1e-6
NSLOT = 4
FG = 4
SCR = 84544
PARTS = ("lru", "mlstm", "ret")
SKIP = set()
USE_SCOPES = False
POOL_DCS = (3, 7)

COMPUTE = ("pe", "act", "dve", "pool")


class Op:
    __slots__ = ("id", "eng", "fn", "deps", "dma_key", "sig", "sigval", "dma_waits", "scope")

    def __init__(self, id, eng, fn, dma_key):
        self.id = id
        self.eng = eng
        self.fn = fn
        self.deps = []
        self.dma_key = dma_key
        self.sig = False
        self.sigval = 0
        self.dma_waits = {}


def _arena(c):
    return c[0] if isinstance(c, tuple) else c


class Sched:
    def __init__(self, nc):
        self.nc = nc
        self.ops = []
        self.lw = {}
        self.rd = {}
        self.keycnt = {}
        self.extra = {}
        self.arena_touch = {}
        self.scope = None
        self.use_scopes = False

    def op(self, eng, fn, reads=(), writes=(), dma_key=None):
        o = Op(len(self.ops), eng, fn, dma_key)
        o.scope = self.scope
        is_dma = dma_key is not None
        deps = {}

        def add(d, raw):
            if d is None:
                return
            d_dma = d.dma_key is not None
            if (not is_dma) and (not d_dma) and d.eng == eng:
                if not raw or eng == "pe":
                    return
            deps[d.id] = d

        for c in reads:
            add(self.lw.get(c), True)
        for c in writes:
            add(self.lw.get(c), False)
            for r in self.rd.get(c, {}).values():
                add(r, False)
        k = ("dma", o.id) if is_dma else eng
        for c in list(reads) + list(writes):
            ar = _arena(c)
            if ar in self.extra:
                for d in self.extra[ar]:
                    add(d, True)
            if ar in self.arena_touch:
                self.arena_touch[ar][k] = o
        for c in reads:
            self.rd.setdefault(c, {})[k] = o
        for c in writes:
            self.lw[c] = o
            self.rd[c] = {}
        for d in deps.values():
            if d.dma_key is not None:
                o.dma_waits[d.dma_key] = self.keycnt[d.dma_key]
            else:
                o.deps.append(d)
                d.sig = True
        if is_dma:
            self.keycnt[dma_key] = self.keycnt.get(dma_key, 0) + 16
        self.ops.append(o)
        return o

    def track_arena(self, arena):
        self.arena_touch.setdefault(arena, {})

    def alias_barrier(self, arenas):
        if isinstance(arenas, str):
            arenas = [arenas]
        ops = {}
        for ar in arenas:
            for k, o in self.arena_touch.get(ar, {}).items():
                ops[(ar, k)] = o
        lst = list(ops.values())
        for ar in arenas:
            self.extra[ar] = lst
            self.arena_touch[ar] = {}
        for c in [c for c in self.lw if _arena(c) in arenas]:
            del self.lw[c]
        for c in [c for c in self.rd if _arena(c) in arenas]:
            del self.rd[c]

    def emit(self, final_wait_keys=()):
        nc = self.nc
        engs = {"pe": [], "act": [], "dve": [], "pool": [], "sp": []}
        for o in self.ops:
            engs[o.eng].append(o)
        for e, lst in engs.items():
            n = 0
            for o in lst:
                if o.dma_key is None and o.sig:
                    n += 1
                    o.sigval = n
            assert n < 60000, (e, n)
        keys = sorted(self.keycnt.keys(), key=str)
        with contextlib.ExitStack() as st:
            esem = {e: st.enter_context(nc.semaphore("s_" + e)) for e in COMPUTE}
            ksem = {k: st.enter_context(nc.semaphore("k%d" % i)) for i, k in enumerate(keys)}
            block = st.enter_context(nc.Block())

            def run(e, engobj):
                waited = {}
                for o in engs[e]:
                    need = {}
                    for d in o.deps:
                        s = esem[d.eng]
                        need[s] = max(need.get(s, 0), d.sigval)
                    for k, v in o.dma_waits.items():
                        s = ksem[k]
                        need[s] = max(need.get(s, 0), v)
                    for s, v in need.items():
                        if waited.get(s, 0) < v:
                            engobj.wait_ge(s, v)
                            waited[s] = v
                    if self.use_scopes and o.scope is not None:
                        with nc.named_scope(o.scope):
                            ins = o.fn(engobj)
                    else:
                        ins = o.fn(engobj)
                    if o.dma_key is not None:
                        ins.then_inc(ksem[o.dma_key], 16)
                    elif o.sig:
                        ins.then_inc(esem[e], 1)
                if e == "sp":
                    for k in final_wait_keys:
                        if self.keycnt.get(k, 0) > 0:
                            engobj.wait_ge(ksem[k], self.keycnt[k])

            @block.tensor
            def _(eng):
                run("pe", eng)

            @block.scalar
            def _(eng):
                run("act", eng)

            @block.vector
            def _(eng):
                run("dve", eng)

            @block.gpsimd
            def _(eng):
                run("pool", eng)

            @block.sync
            def _(eng):
                run("sp", eng)


def _pv_layout():
    cols = {}
    n = 0

    def add(name, w):
        nonlocal n
        cols[name] = (n, w)
        n += w

    for l in range(DEPTH):
        add(("nf1", l), 8)
        add(("nmx", l), 8)
        add(("nf2", l), 8)
        for nm, w in (("gmn", 2), ("grn", 4), ("wconv", 8), ("bconv", 2), ("ba", 2), ("bi", 2), ("lam", 2), ("bifi", 2), ("biff", 2)):
            add((nm, l), w)
    add("nfin", 8)
    return cols, n


PVC, NPV = _pv_layout()


def _fm(v):
    return np.ascontiguousarray(v.reshape(-1, 128).T)


def _host_prep(inp):
    f = np.float32
    pv = np.zeros((128, NPV), f)

    def put(name, arr):
        o, w = PVC[name]
        pv[:, o:o + w] = arr

    for l in range(DEPTH):
        put(("nf1", l), _fm(inp["norm_ffn1"][l]))
        put(("nmx", l), _fm(inp["norm_mix"][l]))
        put(("nf2", l), _fm(inp["norm_ffn2"][l]))
        put(("gmn", l), _fm(inp["g_mlstm_norm"][l]))
        put(("grn", l), _fm(inp["g_ret_norm"][l]))
        wc = np.zeros((128, 8), f)
        for jj in range(4):
            wc[:, jj * 2:jj * 2 + 2] = _fm(inp["w_conv"][l, jj])
        put(("wconv", l), wc)
        put(("bconv", l), _fm(inp["b_conv"][l]))
        put(("ba", l), _fm(inp["b_lru_a"][l]))
        put(("bi", l), _fm(inp["b_lru_i"][l]))
        put(("lam", l), _fm(inp["lru_lambda"][l]))
        hidx = np.arange(128) // 64
        bi_ = np.zeros((128, 2), f)
        bf_ = np.zeros((128, 2), f)
        for pr in range(2):
            bi_[:, pr] = inp["b_mlstm_if"][l][2 * pr + hidx]
            bf_[:, pr] = inp["b_mlstm_if"][l][4 + 2 * pr + hidx]
        put(("bifi", l), bi_)
        put(("biff", l), bf_)
    put("nfin", _fm(inp["norm_final"]))

    cuts = dict(mq=0, mk=256, mv=512, mo=768, mi=1024, mf=1028, rq=1032, rk=1544, rv=2056, rg=2568, lx=3080, ly=3336)
    ar = np.arange(128)
    sw = np.concatenate([np.arange(64, 128), np.arange(0, 64)])
    units = []
    for c in range(2):
        units += [cuts["lx"] + c * 128 + ar, cuts["ly"] + c * 128 + ar]
    for pr in range(2):
        rep = np.repeat(np.array([2 * pr, 2 * pr + 1]), 64)
        units += [cuts["mi"] + rep, cuts["mf"] + rep, cuts["mq"] + pr * 128 + ar, cuts["mk"] + pr * 128 + ar,
                  cuts["mv"] + pr * 128 + ar, cuts["mo"] + pr * 128 + ar]
    for h in range(4):
        units += [cuts["rq"] + h * 128 + ar, cuts["rq"] + h * 128 + sw, cuts["rk"] + h * 128 + ar, cuts["rk"] + h * 128 + sw,
                  cuts["rv"] + h * 128 + ar, cuts["rg"] + h * 128 + ar]
    cols = np.stack(units)
    win = inp["w_in"][:, :, cols]
    win = win.reshape(DEPTH, KC, 128, 40, 128).transpose(0, 3, 2, 1, 4)
    win = np.ascontiguousarray(win.reshape(DEPTH, 40, 128, 1024))
    wout = np.ascontiguousarray(inp["w_out"].reshape(DEPTH, 8, 128, 1024))
    wlru = np.zeros((DEPTH, 128, 512), f)
    for l in range(DEPTH):
        for gi, W in ((0, inp["w_lru_a"]), (1, inp["w_lru_i"])):
            for c in range(2):
                o = (gi * 2 + c) * 128
                wlru[l, 0:64, o:o + 64] = W[l, 2 * c]
                wlru[l, 64:128, o + 64:o + 128] = W[l, 2 * c + 1]
    cstb = np.zeros((128, 640), f)
    s_ = np.arange(128)
    mc = (s_[:, None] <= s_[None, :]).astype(f)
    cstb[:, 0:128] = mc
    cstb[:, 128:256] = mc
    s6 = np.arange(64)
    mb = ((s6[:, None] // 4 == s6[None, :] // 4) & (s6[:, None] <= s6[None, :])).astype(f)
    cstb[0:64, 256:320] = mb
    cstb[0:64, 320:384] = mb
    cstb[:, 384:512] = (s_[:, None] // 64 == s_[None, :] // 64).astype(f) / 64.0
    cstb[:, 512:640] = 1.0 / 128.0
    cstf = np.zeros((128, 400), f)
    cstf[0:64, 0:16] = (s6[:, None] // 4 == np.arange(16)[None, :]).astype(f)
    for hh in range(4):
        for pr in range(2):
            cstf[hh, 16 + pr * 128:16 + (pr + 1) * 128] = (hh == 2 * pr + s_ // 64).astype(f)
    cstf[:, 272:400] = 1.0
    pos = np.concatenate([np.arange(NPROMPT, dtype=f), np.tile(np.float32(16384.0) + np.arange(4, dtype=f), 16)])
    inv = (np.float32(10000.0) ** (-(np.arange(0, 128, 2, dtype=f) / np.float32(128.0)))).astype(f)
    ang = (pos[:, None] * inv[None, :]).astype(f).astype(np.float64)
    rope = np.zeros((128, 2, N), f)
    rope[0:64, 0] = np.cos(ang).T
    rope[64:128, 0] = np.cos(ang).T
    rope[0:64, 1] = -np.sin(ang).T
    rope[64:128, 1] = np.sin(ang).T
    dect = np.zeros((4, 128, 2, 192), f)
    tl = np.arange(128, dtype=np.float64)
    ts = np.tile(np.arange(4, dtype=np.float64), 16)
    for h in range(4):
        lg = np.log1p(-(2.0 ** (-5.0 - h)))
        dect[h, :, 0, 0:128] = np.exp((tl + 1.0) * lg)
        dect[h, :, 0, 128:192] = np.exp((ts + 1.0) * lg)
        dect[h, :, 1, 0:128] = np.exp(-(tl + 1.0) * lg) * 128.0 ** -0.5
        dect[h, :, 1, 128:192] = np.exp(-(ts + 1.0) * lg) * 128.0 ** -0.5

    def gu_tiles(w):
        L = w.shape[0]
        g = w[:, :, :DFF].reshape(L, KC, 128, NFC, 128)
        u = w[:, :, DFF:].reshape(L, KC, 128, NFC, 128)
        t = np.stack([g, u], axis=4)
        t = t.transpose(0, 3, 2, 1, 4, 5)
        return np.ascontiguousarray(t.reshape(L, NFC, 128, KC * 256))

    def dn_tiles(w):
        L = w.shape[0]
        t = w.reshape(L, NFC // 2, 2, 128, D).transpose(0, 1, 3, 2, 4)
        return np.ascontiguousarray(t.reshape(L, NFC // 2, 128, 2 * D))

    shared = {
        "pv": pv,
        "wgu1": gu_tiles(inp["w_ffn1_gu"]), "wdn1": dn_tiles(inp["w_ffn1_down"]),
        "wgu2": gu_tiles(inp["w_ffn2_gu"]), "wdn2": dn_tiles(inp["w_ffn2_down"]),
        "ident": np.eye(128, dtype=f),
        "win": win, "wout": wout, "wlru": wlru, "cstb": cstb, "cstf": cstf, "rope": rope, "dect": dect,
    }
    per_core = []
    for c in range(8):
        xin = np.concatenate([inp["x_prompt"][c], inp["x_sample"][16 * c:16 * c + 16].reshape(NSAMP, D)], axis=0)
        sl = slice(16 * c, 16 * c + 16)
        per_core.append({
            "xin": np.ascontiguousarray(xin),
            "stC": np.ascontiguousarray(inp["state_mlstm_C"][:, sl]),
            "stn": np.ascontiguousarray(inp["state_mlstm_n"][:, sl].reshape(DEPTH, 16, 256)),
            "stm": np.ascontiguousarray(inp["state_mlstm_m"][:, sl]),
            "stS": np.ascontiguousarray(inp["state_ret_S"][:, sl]),
            "sth": np.ascontiguousarray(inp["state_lru_h"][:, sl]),
            "stcv": np.ascontiguousarray(inp["state_lru_conv"][:, sl].reshape(DEPTH, 48, 256)),
        })
    return shared, per_core


def build():
    nc = bass.Bass("TRN2", target_bir_lowering=False)
    dt_in = lambda name, shape: nc.dram_tensor(name, list(shape), F32, kind="ExternalInput").ap()
    dt_out = lambda name, shape: nc.dram_tensor(name, list(shape), F32, kind="ExternalOutput").ap()
    xin = dt_in("xin", [N, D])
    pv_d = dt_in("pv", [128, NPV])
    ident_d = dt_in("ident", [128, 128])
    wgu_d = {1: dt_in("wgu1", [DEPTH, NFC, 128, 2048]), 2: dt_in("wgu2", [DEPTH, NFC, 128, 2048])}
    wdn_d = {1: dt_in("wdn1", [DEPTH, NFC // 2, 128, 2048]), 2: dt_in("wdn2", [DEPTH, NFC // 2, 128, 2048])}
    y_d = dt_out("y", [N, D])
    win_d = dt_in("win", [DEPTH, 40, 128, 1024])
    wout_d = dt_in("wout", [DEPTH, 8, 128, 1024])
    wlru_d = dt_in("wlru", [DEPTH, 128, 512])
    cstb_d = dt_in("cstb", [128, 640])
    cstf_d = dt_in("cstf", [128, 400])
    rope_d = dt_in("rope", [128, 2, N])
    dect_d = dt_in("dect", [4, 128, 2, 192])
    stC_d = dt_in("stC", [DEPTH, 16, 4, 64, 64])
    stn_d = dt_in("stn", [DEPTH, 16, 256])
    stm_d = dt_in("stm", [DEPTH, 16, 4])
    stS_d = dt_in("stS", [DEPTH, 16, 4, 128, 128])
    sth_d = dt_in("sth", [DEPTH, 16, 256])
    stcv_d = dt_in("stcv", [DEPTH, 48, 256])
    oC_d = dt_out("oC", [DEPTH, 17, 4, 64, 64])
    on_d = dt_out("on", [DEPTH, 17, 256])
    om_d = dt_out("om", [DEPTH, 17, 4])
    oS_d = dt_out("oS", [DEPTH, 17, 4, 128, 128])
    oh_d = dt_out("oh", [DEPTH, 17, 256])
    ocv_d = dt_out("ocv", [DEPTH, 17, 3, 256])

    with contextlib.ExitStack() as st:
        SB = lambda name, shape, dt: st.enter_context(nc.sbuf_tensor(name, list(shape), dt))
        xT = SB("xT", [128, KC, N], F32)
        hT = SB("hT", [128, KC, N], BF16)
        wsl = SB("wsl", [128, NSLOT, 2048], BF16)
        pv = SB("pv_s", [128, NPV], F32)
        ident = SB("ident_s", [128, 128], F32)
        identb = SB("identb", [128, 128], BF16)
        onesb = SB("onesb", [128, 128], BF16)
        nsq = SB("nsq", [128, 2, 512], BF16)
        nrstd = SB("nrstd", [128, 2, 512], F32)
        cstb = SB("cstb_s", [128, 640], BF16)
        cstf = SB("cstf_s", [128, 400], F32)
        maskc = cstb[:, 0:256].rearrange("p (h t) -> p h t", h=2)
        maskb = cstb[:, 256:384].rearrange("p (h t) -> p h t", h=2)
        bones64 = cstb[:, 384:512]
        ones128 = cstb[:, 512:640]
        selb = cstf[:, 0:16]
        sel4 = cstf[:, 16:272].rearrange("p (r q) -> p r q", r=2)
        onesf = cstf[:, 272:400]
        scr = SB("scr", [128, SCR], mybir.dt.uint8)
        psb = [st.enter_context(nc.psum_tensor("ps%d" % i, [128, 512], F32)) for i in range(8)]

        S = Sched(nc)
        S.use_scopes = USE_SCOPES
        S.track_arena("scr")
        S.track_arena("scrA")
        S.track_arena("scrB")
        S.track_arena("w")
        S.track_arena("wh")
        state = {"ps": 0, "w": 0, "wh": 0, "scr_off": 0}

        held = set()
        pools = {None: list(range(8)), "A": [4, 5, 6, 7], "B": [0, 1, 2, 3]}
        pstate = {None: 0, "A": 0, "B": 0}

        def PS(hold=False):
            pool = state.get("pspool")
            banks = pools[pool]
            for _ in range(2 * len(banks)):
                b = banks[pstate[pool] % len(banks)]
                pstate[pool] += 1
                if b not in held:
                    break
            else:
                raise RuntimeError("all PSUM banks of pool held")
            if hold:
                held.add(b)
            return psb[b], ("ps", b)

        def PS_release(cell):
            held.discard(cell[1])

        arena_of = {}
        ARENAS = ["scr", "scrA", "scrB"]
        RB = 28032
        regions = {"scr": (0, SCR), "scrB": (0, RB), "scrA": (RB, SCR)}
        state["off"] = {"scr": 0, "scrA": 0, "scrB": 0}

        def C(name, *idx):
            return (arena_of.get(name, "scr"), name) + idx

        def carve(shape, dt, region="scr"):
            esz = 4 if dt == F32 else 2
            n = 1
            for s_ in shape:
                n *= s_
            base, lim = regions[region]
            off = state["off"][region]
            off = (off + 63) // 64 * 64
            state["off"][region] = off + n * esz
            assert base + off + n * esz <= lim, (region, off + n * esz, lim - base)
            v = scr[:, base + off:base + off + n * esz].bitcast(dt)
            if len(shape) == 2:
                v = v.rearrange("p (a b) -> p a b", a=shape[0])
            elif len(shape) == 3:
                v = v.rearrange("p (a b c) -> p a b c", a=shape[0], b=shape[1])
            return v

        def new_phase(name=None):
            if name is not None:
                S.scope = name
            S.alias_barrier(ARENAS)
            state["off"] = {"scr": 0, "scrA": 0, "scrB": 0}

        wslh = wsl[:, :, :].rearrange("p s (h f) -> p (s h) f", h=2)

        def load_wh(src):
            idx = state["wh"] % (2 * NSLOT)
            state["wh"] += 1
            F = src.shape[-1]
            assert F <= 1024
            S.op("pool", lambda e, idx=idx, src=src, F=F: e.dma_start(out=wslh[:, idx, 0:F], in_=src),
                 writes=[("wh", idx)], dma_key=("wh", idx))
            return idx

        def load_w(src):
            sl = state["w"] % NSLOT
            state["w"] += 1
            F = src.shape[-1]
            S.op("pool", lambda e, sl=sl, src=src, F=F: e.dma_start(out=wsl[:, sl, 0:F], in_=src),
                 writes=[("w", sl)], dma_key=("w", sl))
            return sl

        S.op("sp", lambda e: e.dma_start(out=pv[:], in_=pv_d), writes=["pv"], dma_key="pv")
        S.op("sp", lambda e: e.dma_start(out=ident[:], in_=ident_d), writes=["ident"], dma_key="ident")
        S.op("dve", lambda e: e.tensor_copy(out=identb[:], in_=ident[:]), reads=["ident"], writes=["identb"])
        S.op("dve", lambda e: e.memset(onesb[:], 1.0 / 1024.0), writes=["onesb"])
        S.op("pool", lambda e: e.dma_start(out=cstb[:], in_=cstb_d), writes=["cstb"], dma_key="cstb")
        S.op("sp", lambda e: e.dma_start(out=cstf[:], in_=cstf_d), writes=["cstf"], dma_key="cstf")

        def pvcol(name, j=0, w=1):
            o, _ = PVC[name]
            return pv[:, o + j:o + j + w]

        def load_x(tail=None):
            new_phase("load_x")
            stg = carve([2, 4, D], F32)
            for j, (t0, tn) in enumerate(TT):
                sb = j % 2
                nb = (tn + 127) // 128
                rows = min(128, tn)
                S.op("sp", lambda e, sb=sb, t0=t0, nb=nb, rows=rows: e.dma_start(
                    out=stg[0:rows, sb, 0:nb, :], in_=xin[t0:t0 + nb * rows, :].rearrange("(b p) d -> p b d", p=rows)),
                    writes=[C("stg", sb)], dma_key=("stg", sb))
                for kc in range(KC):
                    ps, psc = PS()
                    for b in range(nb):
                        S.op("pe", lambda e, ps=ps, sb=sb, b=b, kc=kc, rows=rows: e.transpose(
                            out=ps[:, b * rows:(b + 1) * rows], in_=stg[0:rows, sb, b, kc * 128:(kc + 1) * 128],
                            identity=ident[0:rows, 0:rows]),
                            reads=[C("stg", sb), "ident"], writes=[psc])
                    eng = "act" if kc % 2 == 0 else "dve"
                    if eng == "act":
                        S.op("act", lambda e, ps=ps, kc=kc, t0=t0, tn=tn: e.activation(
                            out=xT[:, kc, t0:t0 + tn], in_=ps[:, 0:tn], func=AF.Copy),
                            reads=[psc], writes=[("xT", kc, j)])
                    else:
                        S.op("dve", lambda e, ps=ps, kc=kc, t0=t0, tn=tn: e.tensor_copy(
                            out=xT[:, kc, t0:t0 + tn], in_=ps[:, 0:tn]),
                            reads=[psc], writes=[("xT", kc, j)])
                if tail is not None:
                    tail(j)

        def rms_tile(j, gname, dst_fn, dst_cells):
            sq, rstd = nsq, nrstd
            t0, tn = TT[j]
            ps, psc = PS()
            for kc in range(KC):
                s = kc % 2
                S.op("act", lambda e, s=s, kc=kc: e.activation(out=sq[:, s, 0:tn], in_=xT[:, kc, t0:t0 + tn], func=AF.Square),
                     reads=[("xT", kc, j)], writes=[("nsq", s)])
                S.op("pe", lambda e, s=s, kc=kc, ps=ps: e.matmul(ps[:, 0:tn], lhsT=onesb[:], rhs=sq[:, s, 0:tn],
                                                                  start=(kc == 0), stop=(kc == KC - 1)),
                     reads=[("nsq", s), "onesb"], writes=[psc])
            r = j % 2
            S.op("act", lambda e, ps=ps, r=r: e.activation(out=rstd[:, r, 0:tn], in_=ps[:, 0:tn], func=AF.Ln, bias=EPS, scale=1.0),
                 reads=[psc], writes=[("nrstd", r)])
            S.op("act", lambda e, r=r: e.activation(out=rstd[:, r, 0:tn], in_=rstd[:, r, 0:tn], func=AF.Exp, scale=-0.5),
                 reads=[("nrstd", r)], writes=[("nrstd", r)])
            for kc in range(KC):
                S.op("dve", lambda e, kc=kc, r=r: e.scalar_tensor_tensor(
                    out=dst_fn(kc), in0=xT[:, kc, t0:t0 + tn], scalar=pvcol(gname, kc), in1=rstd[:, r, 0:tn],
                    op0=ALU.mult, op1=ALU.mult),
                    reads=[("xT", kc, j), ("nrstd", r), "pv"], writes=[dst_cells(kc)])

        def norm_hT_tile(gname):
            def f(j):
                t0, tn = TT[j]
                rms_tile(j, gname, lambda kc: hT[:, kc, t0:t0 + tn], lambda kc: ("hT", kc, j))
            return f

        def ffn(l, which, next_norm=None):
            new_phase("ffn%d_l%d" % (which, l))
            S.alias_barrier(["w", "wh"])
            sg = carve([3, 512], F32)
            actb = carve([2 * FG, N], BF16)
            if next_norm == "final":
                next_norm = make_final_tile()
            groups = [list(range(g, min(g + FG, NFC))) for g in range(0, NFC, FG)]
            sgc = [0]

            def gu(fc, aslot, fl):
                sl = load_w(wgu_d[which][l, fc])
                wv = wsl[:, sl, :].rearrange("p (k c) -> p k c", k=KC)
                for j, (t0, tn) in enumerate(TT):
                    pg, pgc = PS()
                    pu, puc = PS()
                    for (pp, ppc, co) in ((pg, pgc, 0), (pu, puc, 128)):
                        for kc in range(KC):
                            S.op("pe", lambda e, pp=pp, kc=kc, co=co, wv=wv, t0=t0, tn=tn: e.matmul(
                                pp[:, 0:tn], lhsT=wv[:, kc, co:co + 128], rhs=hT[:, kc, t0:t0 + tn],
                                start=(kc == 0), stop=(kc == KC - 1)),
                                reads=[("w", sl), ("hT", kc, j)], writes=[ppc])
                    s = sgc[0] % 3
                    sgc[0] += 1
                    S.op("act", lambda e, pg=pg, s=s, tn=tn: e.activation(out=sg[:, s, 0:tn], in_=pg[:, 0:tn], func=AF.Silu),
                         reads=[pgc], writes=[C("sg", s)])
                    S.op("dve", lambda e, pu=pu, s=s, a=aslot * FG + fl, t0=t0, tn=tn: e.tensor_tensor(
                        out=actb[:, a, t0:t0 + tn], in0=pu[:, 0:tn], in1=sg[:, s, 0:tn], op=ALU.mult),
                        reads=[puc, C("sg", s)], writes=[C("act", aslot * FG + fl, j)])

            def down(items, tail=None):
                chunks = []
                for (g, aslot) in items:
                    fcs = groups[g]
                    sls = [load_w(wdn_d[which][l, fc // 2]) for fc in fcs[::2]]
                    for i, fc in enumerate(fcs):
                        chunks.append((sls[i // 2], i % 2, aslot * FG + i))
                for j, (t0, tn) in enumerate(TT):
                    for dc in range(KC):
                        ps, psc = PS()
                        for n_, (sl, half, asl) in enumerate(chunks):
                            wv = wsl[:, sl, :].rearrange("p (f d) -> p f d", f=2)
                            S.op("pe", lambda e, ps=ps, wv=wv, half=half, asl=asl, dc=dc, t0=t0, tn=tn, n_=n_: e.matmul(
                                ps[:, 0:tn], lhsT=wv[:, half, dc * 128:(dc + 1) * 128], rhs=actb[:, asl, t0:t0 + tn],
                                start=(n_ == 0), stop=(n_ == len(chunks) - 1)),
                                reads=[("w", sl), C("act", asl, j)], writes=[psc])
                        S.op("dve", lambda e, ps=ps, dc=dc, t0=t0, tn=tn: e.scalar_tensor_tensor(
                            out=xT[:, dc, t0:t0 + tn], in0=ps[:, 0:tn], scalar=0.5, in1=xT[:, dc, t0:t0 + tn],
                            op0=ALU.mult, op1=ALU.add),
                            reads=[psc, ("xT", dc, j)], writes=[("xT", dc, j)])
                    if tail is not None:
                        tail(j)

            pending = None
            ng = len(groups)
            for g, fcs in enumerate(groups):
                aslot = g % 2
                for fl, fc in enumerate(fcs):
                    gu(fc, aslot, fl)
                    if fl == 0 and pending is not None and g < ng - 1:
                        down([pending])
                        pending = None
                if g < ng - 2:
                    pending = (g, aslot)
            down([(ng - 2, (ng - 2) % 2), (ng - 1, (ng - 1) % 2)], tail=next_norm)

        def make_final_tile():
            hf = carve([2, KC, 512], F32)
            ost = carve([2, D], F32)
            cnt = [0]

            def final_tile(j):
                t0, tn = TT[j]
                hs = j % 2
                rms_tile(j, "nfin", lambda kc: hf[:, hs, kc, 0:tn], lambda kc: C("hf", hs, kc))
                nb = (tn + 127) // 128
                rows = min(128, tn)
                for b in range(nb):
                    os_ = cnt[0] % 2
                    cnt[0] += 1
                    for half in range(2):
                        ps, psc = PS()
                        for q in range(4):
                            kc = half * 4 + q
                            S.op("pe", lambda e, ps=ps, q=q, kc=kc, b=b: e.transpose(
                                out=ps[0:rows, q * 128:(q + 1) * 128], in_=hf[:, hs, kc, b * rows:(b + 1) * rows], identity=ident[:]),
                                reads=[C("hf", hs, kc), "ident"], writes=[psc])
                        if half == 0:
                            S.op("act", lambda e, ps=ps, os_=os_: e.activation(
                                out=ost[0:rows, os_, 0:512], in_=ps[0:rows, :], func=AF.Copy),
                                reads=[psc], writes=[C("ost", os_, 0)])
                        else:
                            S.op("dve", lambda e, ps=ps, os_=os_: e.tensor_copy(
                                out=ost[0:rows, os_, 512:1024], in_=ps[0:rows, :]),
                                reads=[psc], writes=[C("ost", os_, 1)])
                    S.op("sp", lambda e, os_=os_, r0=t0 + b * rows: e.dma_start(
                        out=y_d[r0:r0 + rows, :], in_=ost[0:rows, os_, :]),
                        reads=[C("ost", os_, 0), C("ost", os_, 1)], dma_key=("ost", os_))
            return final_tile

        def cells(name, js=None):
            return [C(name, j) for j in (range(5) if js is None else js)]

        def A_(fn, reads, writes):
            S.op("act", fn, reads, writes)

        def V_(fn, reads, writes):
            S.op("dve", fn, reads, writes)

        def M_(fn, reads, writes):
            S.op("pe", fn, reads, writes)

        def proj_fm(sl, consume):
            wv = wslh[:, sl, 0:1024].rearrange("p (k c) -> p k c", k=KC)
            for j, (t0, tn) in enumerate(TT):
                ps, psc = PS()
                for kc in range(KC):
                    M_(lambda e, ps=ps, kc=kc, t0=t0, tn=tn, wv=wv: e.matmul(
                        ps[:, 0:tn], lhsT=wv[:, kc, :], rhs=hT[:, kc, t0:t0 + tn], start=(kc == 0), stop=(kc == KC - 1)),
                        [("wh", sl), ("hT", kc, j)], [psc])
                consume(j, t0, tn, ps, psc)

        def proj_tok(sl, evac):
            wv = wslh[:, sl, 0:1024].rearrange("p (k c) -> p k c", k=KC)
            for j, (t0, tn) in enumerate(TT):
                ps, psc = PS()
                rows = min(128, tn)
                nb = (tn + 127) // 128
                for b in range(nb):
                    for kc in range(KC):
                        M_(lambda e, ps=ps, kc=kc, b=b, rows=rows, t0=t0, wv=wv: e.matmul(
                            ps[0:rows, b * 128:(b + 1) * 128], lhsT=hT[:, kc, t0 + b * rows:t0 + (b + 1) * rows], rhs=wv[:, kc, :],
                            start=(kc == 0), stop=(kc == KC - 1)),
                            [("wh", sl), ("hT", kc, j)], [psc])
                evac(j, nb, rows, ps, psc)

        def to_tok(srcT, srcname, dst, dstname, scale_fn=None):
            for j, (t0, tn) in enumerate(TT):
                ps, psc = PS()
                psb_ = ps.bitcast(BF16)
                rows = min(128, tn)
                nb = (tn + 127) // 128
                for b in range(nb):
                    M_(lambda e, psb_=psb_, b=b, rows=rows, t0=t0: e.transpose(
                        out=psb_[0:rows, b * 128:(b + 1) * 128], in_=srcT[:, t0 + b * rows:t0 + (b + 1) * rows], identity=identb[:]),
                        [("scr", srcname, j), "identb"], [psc])
                A_(lambda e, psb_=psb_, j=j, nb=nb, rows=rows, sc_=(1.0 if scale_fn is None else scale_fn(j)): e.activation(
                    out=dst[0:rows, 4 * j:4 * j + nb, :], in_=psb_[0:rows, 0:nb * 128].rearrange("p (b c) -> p b c", b=nb), func=AF.Copy, scale=sc_),
                    [psc], [("scr", dstname, j)])

        def out_proj(l, kcs, src, src_cells, tail=None):
            sls = [load_wh(wout_d[l, kc]) for kc in kcs]
            for j, (t0, tn) in enumerate(TT):
                for dc in range(KC):
                    ps, psc = PS()
                    for i in range(len(kcs)):
                        M_(lambda e, ps=ps, i=i, dc=dc, t0=t0, tn=tn: e.matmul(
                            ps[:, 0:tn], lhsT=wslh[:, sls[i], dc * 128:(dc + 1) * 128], rhs=src(i, t0, tn),
                            start=(i == 0), stop=(i == len(kcs) - 1)),
                            [("wh", sls[i])] + src_cells(i, j), [psc])
                    V_(lambda e, ps=ps, dc=dc, t0=t0, tn=tn: e.scalar_tensor_tensor(
                        out=xT[:, dc, t0:t0 + tn], in0=ps[:, 0:tn], scalar=1.0, in1=xT[:, dc, t0:t0 + tn],
                        op0=ALU.mult, op1=ALU.add),
                        [psc, ("xT", dc, j)], [("xT", dc, j)])
                if tail is not None:
                    tail(j)

        def sv(buf):
            return buf[:, NPROMPT:N].rearrange("p (b t) -> p b t", t=4)

        def lru(l):
            for c in range(2):
                lru_chunk(l, c)

        def lru_chunk(l, c):
            if True:
                new_phase("lru%d_l%d" % (c, l))
                yl = carve([2, N], BF16)
                ohst = carve([256], F32)
                ocst = carve([256], F32)
                XPp = carve([NPROMPT + 3], F32)
                XPs = carve([16, 7], F32)
                xc = carve([N], F32)
                xcb = carve([N], BF16)
                Ab = carve([N], F32)
                Ub = carve([N], F32)
                Gb = carve([N], F32)
                HS = carve([N], F32)
                T1 = carve([N], F32)
                cvst = carve([256], F32)
                hst = carve([256], F32)
                h0T = carve([16], F32)
                hl = carve([32], F32)
                cv = carve([17, 3], F32)
                spn = carve([2], F32)
                u0 = 2 * c
                S.op("sp", lambda e: e.dma_start(out=cvst[0:48, :], in_=stcv_d[l]), writes=[C("cvst")], dma_key="cvst")
                S.op("sp", lambda e: e.dma_start(out=hst[0:16, :], in_=sth_d[l]), writes=[C("hst")], dma_key="hst")
                ps, psc = PS()
                M_(lambda e, ps=ps: e.transpose(out=ps[:, 0:48], in_=cvst[0:48, c * 128:(c + 1) * 128], identity=ident[0:48, 0:48]),
                   [C("cvst"), "ident"], [psc])
                M_(lambda e, ps=ps: e.transpose(out=ps[:, 64:80], in_=hst[0:16, c * 128:(c + 1) * 128], identity=ident[0:16, 0:16]),
                   [C("hst"), "ident"], [psc])
                A_(lambda e, ps=ps: e.activation(out=XPs[:, :, 0:3], in_=ps[:, 0:48].rearrange("p (b j) -> p b j", j=3), func=AF.Copy),
                   [psc], [C("XPs")])
                A_(lambda e, ps=ps: e.activation(out=h0T[:, :], in_=ps[:, 64:80], func=AF.Copy), [psc], [C("h0T")])
                V_(lambda e: e.memset(XPp[:, 0:3], 0.0), [], [C("XPp0")])
                A_(lambda e: e.activation(out=spn[:, :], in_=pvcol(("lam", l), 0, 2), func=AF.Exp, scale=-1.0), ["pv"], [C("spn")])
                A_(lambda e: e.activation(out=spn[:, :], in_=spn[:, :], func=AF.Ln, bias=1.0), [C("spn")], [C("spn")])
                V_(lambda e: e.tensor_scalar(out=spn[:, :], in0=spn[:, :], scalar1=-8.0, scalar2=None, op0=ALU.mult),
                   [C("spn")], [C("spn")])
                sl = load_wh(win_d[l, u0 + 0])

                def c_lx(j, t0, tn, ps, psc):
                    if j < 4:
                        A_(lambda e: e.activation(out=XPp[:, 3 + t0:3 + t0 + tn], in_=ps[:, 0:tn], func=AF.Copy), [psc], [C("XP", j)])
                    else:
                        A_(lambda e: e.activation(out=XPs[:, :, 3:7], in_=ps[:, 0:64].rearrange("p (b t) -> p b t", t=4), func=AF.Copy),
                           [psc], [C("XP", 4)])
                proj_fm(sl, c_lx)
                sl = load_wh(win_d[l, u0 + 1])

                def c_ly(j, t0, tn, ps, psc):
                    A_(lambda e: e.activation(out=Gb[:, t0:t0 + tn], in_=ps[:, 0:tn], func=AF.Gelu_apprx_tanh), [psc], [C("G", j)])
                proj_fm(sl, c_ly)
                wc = lambda jj: pvcol(("wconv", l), jj * 2 + c)
                xpr = [C("XP", j) for j in range(4)] + [C("XPp0")]
                V_(lambda e: e.tensor_scalar(out=xc[:, 0:NPROMPT], in0=XPp[:, 0:NPROMPT], scalar1=wc(0), scalar2=pvcol(("bconv", l), c),
                                             op0=ALU.mult, op1=ALU.add), xpr + ["pv"], cells("xc", range(4)))
                for jj in range(1, 4):
                    V_(lambda e, jj=jj: e.scalar_tensor_tensor(out=xc[:, 0:NPROMPT], in0=XPp[:, jj:jj + NPROMPT], scalar=wc(jj),
                                                               in1=xc[:, 0:NPROMPT], op0=ALU.mult, op1=ALU.add),
                       xpr + ["pv"] + cells("xc", range(4)), cells("xc", range(4)))
                V_(lambda e: e.tensor_scalar(out=sv(xc), in0=XPs[:, :, 0:4], scalar1=wc(0), scalar2=pvcol(("bconv", l), c),
                                             op0=ALU.mult, op1=ALU.add), [C("XP", 4), C("XPs"), "pv"], cells("xc", [4]))
                for jj in range(1, 4):
                    V_(lambda e, jj=jj: e.scalar_tensor_tensor(out=sv(xc), in0=XPs[:, :, jj:jj + 4], scalar=wc(jj), in1=sv(xc),
                                                               op0=ALU.mult, op1=ALU.add),
                       [C("XP", 4), C("XPs"), "pv"] + cells("xc", [4]), cells("xc", [4]))
                A_(lambda e: e.activation(out=xcb[:, :], in_=xc[:, :], func=AF.Copy), cells("xc"), cells("xcb"))
                slw = load_wh(wlru_d[l])
                for (gi, dst, dname, bname) in ((0, Ab, "A", "ba"), (1, Ub, "U", "bi")):
                    for j, (t0, tn) in enumerate(TT):
                        ps, psc = PS()
                        M_(lambda e, ps=ps, gi=gi, t0=t0, tn=tn: e.matmul(
                            ps[:, 0:tn], lhsT=wslh[:, slw, (gi * 2 + c) * 128:(gi * 2 + c + 1) * 128], rhs=xcb[:, t0:t0 + tn],
                            start=True, stop=True), [("wh", slw), C("xcb", j)], [psc])
                        A_(lambda e, ps=ps, dst=dst, bname=bname, t0=t0, tn=tn: e.activation(
                            out=dst[:, t0:t0 + tn], in_=ps[:, 0:tn], func=AF.Sigmoid, bias=pvcol((bname, l), c), scale=1.0),
                            [psc, "pv"], [("scr", dname, j)])
                A_(lambda e: e.activation(out=Ab[:, :], in_=Ab[:, :], func=AF.Exp, scale=spn[:, c:c + 1]),
                   cells("A") + [C("spn")], cells("A"))
                V_(lambda e: e.scalar_tensor_tensor(out=T1[:, :], in0=Ab[:, :], scalar=1.0, in1=Ab[:, :], op0=ALU.min, op1=ALU.mult),
                   cells("A"), cells("T1"))
                A_(lambda e: e.activation(out=T1[:, :], in_=T1[:, :], func=AF.Sqrt, bias=1.0, scale=-1.0), cells("T1"), cells("T1"))
                V_(lambda e: e.tensor_tensor(out=Ub[:, :], in0=Ub[:, :], in1=T1[:, :], op=ALU.mult), cells("U") + cells("T1"), cells("U"))
                V_(lambda e: e.tensor_tensor(out=Ub[:, :], in0=Ub[:, :], in1=xc[:, :], op=ALU.mult), cells("U") + cells("xc"), cells("U"))
                V_(lambda e: e.tensor_tensor_scan(out=HS[:, 0:NPROMPT], data0=Ab[:, 0:NPROMPT], data1=Ub[:, 0:NPROMPT], initial=0.0,
                                                  op0=ALU.mult, op1=ALU.add),
                   cells("A", range(4)) + cells("U", range(4)), cells("HS", range(4)))
                for b in range(16):
                    o = NPROMPT + 4 * b
                    V_(lambda e, b=b, o=o: e.tensor_tensor_scan(out=HS[:, o:o + 4], data0=Ab[:, o:o + 4], data1=Ub[:, o:o + 4],
                                                                initial=h0T[:, b:b + 1], op0=ALU.mult, op1=ALU.add),
                       cells("A", [4]) + cells("U", [4]) + [C("h0T")], cells("HS", [4]))
                V_(lambda e: e.tensor_tensor(out=yl[:, c, :], in0=HS[:, :], in1=Gb[:, :], op=ALU.mult),
                   cells("HS") + cells("G"), [C("yl", c, j) for j in range(5)])
                A_(lambda e: e.activation(out=hl[:, 0:1], in_=HS[:, NPROMPT - 1:NPROMPT], func=AF.Copy), cells("HS", [3]), [C("hl")])
                A_(lambda e: e.activation(out=hl[:, 1:17], in_=sv(HS)[:, :, 3], func=AF.Copy), cells("HS", [4]), [C("hl")])
                A_(lambda e: e.activation(out=cv[:, 0, :], in_=XPp[:, NPROMPT:NPROMPT + 3], func=AF.Copy), [C("XP", 3)], [C("cv")])
                A_(lambda e: e.activation(out=cv[:, 1:17, :], in_=XPs[:, :, 4:7], func=AF.Copy), [C("XP", 4)], [C("cv")])
                ps, psc = PS()
                M_(lambda e, ps=ps: e.transpose(out=ps[0:17, 0:128], in_=hl[:, 0:17], identity=ident[:]), [C("hl"), "ident"], [psc])
                M_(lambda e, ps=ps: e.transpose(out=ps[0:51, 128:256], in_=cv[:, :, :].rearrange("p s j -> p (s j)"), identity=ident[:]),
                   [C("cv"), "ident"], [psc])
                A_(lambda e, ps=ps: e.activation(out=ohst[0:17, c * 128:(c + 1) * 128], in_=ps[0:17, 0:128], func=AF.Copy), [psc], [C("ohst", c)])
                A_(lambda e, ps=ps: e.activation(out=ocst[0:51, c * 128:(c + 1) * 128], in_=ps[0:51, 128:256], func=AF.Copy), [psc], [C("ocst", c)])
                if c == 1:
                    S.op("sp", lambda e: e.dma_start(out=oh_d[l], in_=ohst[0:17, :]), reads=[C("ohst", 0), C("ohst", 1)], dma_key="o_h")
                    S.op("sp", lambda e: e.dma_start(out=ocv_d[l].rearrange("s j c -> (s j) c"), in_=ocst[0:51, :]),
                         reads=[C("ocst", 0), C("ocst", 1)], dma_key="o_cv")
                    out_proj(l, [6, 7], lambda i, t0, tn: yl[:, i, t0:t0 + tn], lambda i, j: [C("yl", i, j)])

        def unit_core(nh, qsel, kT, qfsel, Vx, Ktok, St0, Snew, scale_p, scale_s, has_den, post, tmps, chain1=False):
            Cst, Ctmp, Cbf, Pm, Pms, Km = tmps
            dsz = 128 // nh
            ow = 128 // nh
            if "core_p" not in SKIP:
                def emit_P(c):
                    t0 = c * 128
                    psP, psPc = PS(hold=True)
                    for h in range(nh):
                        M_(lambda e, psP=psP, h=h, t0=t0: e.matmul(
                            psP[:, h * 128:(h + 1) * 128], lhsT=kT[:, t0:t0 + 128], rhs=qsel(h)[:, t0:t0 + 128],
                            start=True, stop=True), [C("kT", c // 4), C("qT", c // 4)], [psPc])
                    return psP, psPc

                nextP = emit_P(0)
                for j in range(4):
                    psS, psSc = PS(hold=True)
                    for cc in range(4):
                        c = 4 * j + cc
                        for h in range(nh):
                            M_(lambda e, psS=psS, cc=cc, c=c, h=h: e.matmul(
                                psS[h * dsz:(h + 1) * dsz, cc * 128:(cc + 1) * 128], lhsT=Ktok[:, c, h * dsz:(h + 1) * dsz], rhs=Vx[:, c, h, :],
                                start=True, stop=True), [C("Ktok", j), C("Vx", j)], [psSc])
                    psO, psOc = PS(hold=True)
                    if has_den:
                        psD, psDc = PS(hold=True)
                    else:
                        psD, psDc = None, None
                    for cc in range(4):
                        c = 4 * j + cc
                        t0 = c * 128
                        psP, psPc = nextP
                        if c + 1 < 16:
                            nextP = emit_P(c + 1)
                        pslot = c % 4
                        V_(lambda e, psP=psP, pslot=pslot: e.tensor_tensor(
                            out=Pm[:, pslot, 0:nh, :], in0=psP[:, 0:nh * 128].rearrange("p (h t) -> p h t", h=nh), in1=maskc[:, 0:nh, :], op=ALU.mult),
                            [psPc, "cstb"], [C("Pm", pslot)])
                        PS_release(psPc)
                        slot = c % 3
                        src_ = psS[:, cc * 128:(cc + 1) * 128]
                        if chain1:
                            if c == 0:
                                V_(lambda e, src_=src_, slot=slot: e.tensor_copy(out=Cst[:, slot, :], in_=src_), [psSc], [C("Cst", slot)])
                            else:
                                V_(lambda e, src_=src_, slot=slot, c=c, ps_=(c - 1) % 3: e.scalar_tensor_tensor(
                                    out=Cst[:, slot, :], in0=Cst[:, ps_, :], scalar=scale_p(c), in1=src_, op0=ALU.mult, op1=ALU.add),
                                    [psSc, C("Cst", (c - 1) % 3)], [C("Cst", slot)])
                        else:
                            if c == 0:
                                V_(lambda e, src_=src_: e.tensor_copy(out=Ctmp[:, :], in_=src_), [psSc], [C("Ctmp")])
                            else:
                                V_(lambda e, src_=src_, ps_=(c - 1) % 3: e.tensor_tensor(out=Ctmp[:, :], in0=src_, in1=Cst[:, ps_, :], op=ALU.add),
                                   [psSc, C("Cst", (c - 1) % 3)], [C("Ctmp")])
                            A_(lambda e, c=c, slot=slot: e.activation(out=Cst[:, slot, :], in_=Ctmp[:, :], func=AF.Copy, scale=scale_p(c)),
                               [C("Ctmp")] + cells("eq", range(4)), [C("Cst", slot)])
                        if c < 15:
                            A_(lambda e, c=c, slot=slot: e.activation(out=Cbf[:, c + 1, :], in_=Cst[:, slot, :], func=AF.Copy),
                               [C("Cst", slot)], [C("Cbf", c + 1)])
                        for h in range(nh):
                            outs = [(psO, psOc, 0)] + ([(psD, psDc, 64)] if has_den else [])
                            for (pp, ppc, off) in outs:
                                M_(lambda e, pp=pp, h=h, c=c, cc=cc, pslot=pslot, off=off: e.matmul(
                                    pp[h * ow:(h + 1) * ow, cc * 128:(cc + 1) * 128], lhsT=Vx[:, c, h, off:off + ow], rhs=Pm[:, pslot, h, :],
                                    start=True, stop=(c == 0)), [C("Vx", j), C("Pm", pslot)], [ppc])
                                if c > 0:
                                    M_(lambda e, pp=pp, h=h, c=c, cc=cc, t0=t0, off=off: e.matmul(
                                        pp[h * ow:(h + 1) * ow, cc * 128:(cc + 1) * 128], lhsT=Cbf[:, c, off:off + ow],
                                        rhs=qsel(h)[:, t0:t0 + 128], start=False, stop=True),
                                        [C("Cbf", c), C("qT", j)], [ppc])
                        yield
                    PS_release(psSc)
                    post(j, psO, psOc, psD, psDc)
                    PS_release(psOc)
                    if has_den:
                        PS_release(psDc)
            if "core_s" not in SKIP:
                o = NPROMPT
                psP, psPc = PS()
                for h in range(nh):
                    M_(lambda e, h=h, psP=psP: e.matmul(
                        psP[0:64, h * 64:(h + 1) * 64], lhsT=kT[:, o:o + 64], rhs=qsel(h)[:, o:o + 64],
                        start=True, stop=True), [C("kT", 4), C("qT", 4)], [psPc])
                V_(lambda e, psP=psP: e.tensor_tensor(
                    out=Pms[0:64, 0:nh, :], in0=psP[0:64, 0:nh * 64].rearrange("p (h t) -> p h t", h=nh), in1=maskb[0:64, 0:nh, :], op=ALU.mult),
                    [psPc, "cstb"], [C("Pms")])
                psO, psOc = PS()
                if has_den:
                    psD, psDc = PS()
                else:
                    psD, psDc = None, None
                for h in range(nh):
                    outs = [(psO, psOc, 0)] + ([(psD, psDc, 64)] if has_den else [])
                    for (pp, ppc, off) in outs:
                        M_(lambda e, pp=pp, h=h, off=off: e.matmul(
                            pp[h * ow:(h + 1) * ow, 0:64], lhsT=Vx[0:64, 16, h, off:off + ow], rhs=Pms[0:64, h, :], start=True, stop=False),
                            [C("Vx", 4), C("Pms")], [ppc])
                        for b in range(16):
                            M_(lambda e, pp=pp, h=h, off=off, b=b: e.matmul(
                                pp[h * ow:(h + 1) * ow, 4 * b:4 * b + 4], lhsT=St0[:, b, off:off + ow],
                                rhs=qfsel(h)[:, 4 * b:4 * b + 4], start=False, stop=(b == 15)),
                                [C("St0"), C("qTf")], [ppc])
                post(4, psO, psOc, psD, psDc)
                yield
                for g in range(4):
                    psS, psSc = PS()
                    for bb in range(4):
                        b = 4 * g + bb
                        V_(lambda e, b=b, bb=bb: e.tensor_scalar(out=Km[0:64, bb, :], in0=Ktok[0:64, 16, :], scalar1=selb[0:64, b:b + 1], scalar2=None,
                                                                 op0=ALU.mult), [C("Ktok", 4), "cstf"], [C("Km", bb)])
                        for h in range(nh):
                            M_(lambda e, psS=psS, bb=bb, h=h: e.matmul(
                                psS[h * dsz:(h + 1) * dsz, bb * 128:(bb + 1) * 128], lhsT=Km[0:64, bb, h * dsz:(h + 1) * dsz], rhs=Vx[0:64, 16, h, :],
                                start=True, stop=True), [C("Km", bb), C("Vx", 4)], [psSc])
                    sc = scale_s(g)
                    if chain1:
                        V_(lambda e, psS=psS, g=g, sc=sc: e.scalar_tensor_tensor(
                            out=Snew[:, 4 * g:4 * g + 4, :], in0=St0[:, 4 * g:4 * g + 4, :], scalar=sc,
                            in1=psS[:, :].rearrange("p (b v) -> p b v", b=4), op0=ALU.mult, op1=ALU.add),
                            [psSc, C("St0")], [C("St0")])
                        yield
                        continue
                    V_(lambda e, psS=psS, g=g: e.tensor_tensor(out=Snew[:, 4 * g:4 * g + 4, :], in0=psS[:, :].rearrange("p (b v) -> p b v", b=4),
                                                               in1=St0[:, 4 * g:4 * g + 4, :], op=ALU.add),
                       [psSc, C("St0")], [C("St0")])
                    if isinstance(sc, float):
                        V_(lambda e, g=g, sc=sc: e.tensor_scalar(out=Snew[:, 4 * g:4 * g + 4, :], in0=Snew[:, 4 * g:4 * g + 4, :], scalar1=sc, scalar2=None,
                                                                 op0=ALU.mult), [C("St0")], [C("St0")])
                    else:
                        V_(lambda e, g=g, sc=sc: e.tensor_tensor(out=Snew[:, 4 * g:4 * g + 4, :], in0=Snew[:, 4 * g:4 * g + 4, :], in1=sc, op=ALU.mult),
                           [C("St0")] + cells("eq", [4]), [C("St0")])
            yield

        def carve_core_tmps(nh):
            Cst = carve([3, 128], F32)
            Ctmp = carve([128], F32)
            Cbf = carve([17, 128], BF16)
            Pm = carve([4, nh, 128], BF16)
            Pms = carve([nh, 64], BF16)
            Km = carve([4, 128], BF16)
            return (Cst, Ctmp, Cbf, Pm, Pms, Km)

        def head_norm_tile(j, t0, tn, Z, onesm, gcol, nt, dst_fn, dst_cells, extra_mul=None):
            Zb, Zq, sd = nt
            r = j % 2
            A_(lambda e: e.activation(out=Zb[:, r, 0:tn], in_=Z[:, t0:t0 + tn], func=AF.Copy), [C("Z", j)], [C("Zb", r)])
            psM, psMc = PS()
            M_(lambda e: e.matmul(psM[:, 0:tn], lhsT=onesm, rhs=Zb[:, r, 0:tn], start=True, stop=True), [C("Zb", r), "cstb"], [psMc])
            V_(lambda e: e.tensor_tensor(out=Z[:, t0:t0 + tn], in0=Z[:, t0:t0 + tn], in1=psM[:, 0:tn], op=ALU.subtract),
               [C("Z", j), psMc], [C("Z", j)])
            A_(lambda e: e.activation(out=Zq[:, r, 0:tn], in_=Z[:, t0:t0 + tn], func=AF.Square), [C("Z", j)], [C("Zq", r)])
            psV, psVc = PS()
            M_(lambda e: e.matmul(psV[:, 0:tn], lhsT=onesm, rhs=Zq[:, r, 0:tn], start=True, stop=True), [C("Zq", r), "cstb"], [psVc])
            A_(lambda e: e.activation(out=sd[:, r, 0:tn], in_=psV[:, 0:tn], func=AF.Sqrt, bias=EPS, scale=1.0), [psVc], [C("sd", r)])
            V_(lambda e: e.reciprocal(out=sd[:, r, 0:tn], in_=sd[:, r, 0:tn]), [C("sd", r)], [C("sd", r)])
            if extra_mul is None:
                V_(lambda e: e.scalar_tensor_tensor(out=dst_fn(t0, tn), in0=Z[:, t0:t0 + tn], scalar=gcol, in1=sd[:, r, 0:tn],
                                                    op0=ALU.mult, op1=ALU.mult), [C("Z", j), C("sd", r), "pv"], dst_cells(j))
            else:
                em, emc = extra_mul
                V_(lambda e: e.scalar_tensor_tensor(out=Z[:, t0:t0 + tn], in0=Z[:, t0:t0 + tn], scalar=gcol, in1=sd[:, r, 0:tn],
                                                    op0=ALU.mult, op1=ALU.mult), [C("Z", j), C("sd", r), "pv"], [C("Z", j)])
                V_(lambda e: e.tensor_tensor(out=dst_fn(t0, tn), in0=Z[:, t0:t0 + tn], in1=em, op=ALU.mult),
                   [C("Z", j)] + emc, dst_cells(j))

        def b_phase(l, wunit, kind, Z, onesm, gcol, kc_out, tail=None, bufs=None):
            if bufs is None:
                nsl = 5
                yT = carve([N], BF16)
                Zb = carve([5, 512], BF16)
                Zq = carve([5, 512], BF16)
                sd = carve([5, 512], F32)
                sgt = carve([5, 512], F32)
            else:
                nsl, yT, Zb, Zq, sd, sgt = bufs
            sl = load_wh(win_d[l, wunit])
            slo = load_wh(wout_d[l, kc_out])
            wv = wslh[:, sl, 0:1024].rearrange("p (k c) -> p k c", k=KC)
            gfun = AF.Sigmoid if kind == "ml" else AF.Silu

            def stages(j):
                t0, tn = TT[j]
                sj = j % nsl
                zc = C("Z", j)
                box = {}

                def s0():
                    ps, psc = PS()
                    for kc in range(KC):
                        M_(lambda e, kc=kc: e.matmul(ps[:, 0:tn], lhsT=wv[:, kc, :], rhs=hT[:, kc, t0:t0 + tn],
                                                     start=(kc == 0), stop=(kc == KC - 1)), [("wh", sl), ("hT", kc, j)], [psc])
                    A_(lambda e: e.activation(out=sgt[:, sj, 0:tn], in_=ps[:, 0:tn], func=gfun), [psc], [C("sgt", sj)])

                def s1():
                    if kind == "ml":
                        V_(lambda e: e.tensor_tensor(out=Z[:, t0:t0 + tn], in0=Z[:, t0:t0 + tn], in1=sgt[:, sj, 0:tn], op=ALU.mult),
                           [zc, C("sgt", sj)], [zc])
                    A_(lambda e: e.activation(out=Zb[:, sj, 0:tn], in_=Z[:, t0:t0 + tn], func=AF.Copy), [zc], [C("Zb", sj)])

                def s2():
                    box["m"] = PS(hold=True)
                    psM, psMc = box["m"]
                    M_(lambda e: e.matmul(psM[:, 0:tn], lhsT=onesm, rhs=Zb[:, sj, 0:tn], start=True, stop=True), [C("Zb", sj), "cstb"], [psMc])

                def s3():
                    psM, psMc = box["m"]
                    V_(lambda e: e.tensor_tensor(out=Z[:, t0:t0 + tn], in0=Z[:, t0:t0 + tn], in1=psM[:, 0:tn], op=ALU.subtract), [zc, psMc], [zc])
                    A_(lambda e: e.activation(out=Zq[:, sj, 0:tn], in_=Z[:, t0:t0 + tn], func=AF.Square), [zc], [C("Zq", sj)])
                    PS_release(psMc)

                def s4():
                    box["v"] = PS(hold=True)
                    psV, psVc = box["v"]
                    M_(lambda e: e.matmul(psV[:, 0:tn], lhsT=onesm, rhs=Zq[:, sj, 0:tn], start=True, stop=True), [C("Zq", sj), "cstb"], [psVc])

                def s5():
                    psV, psVc = box["v"]
                    A_(lambda e: e.activation(out=sd[:, sj, 0:tn], in_=psV[:, 0:tn], func=AF.Ln, bias=EPS, scale=1.0), [psVc], [C("sd", sj)])
                    A_(lambda e: e.activation(out=sd[:, sj, 0:tn], in_=sd[:, sj, 0:tn], func=AF.Exp, scale=-0.5), [C("sd", sj)], [C("sd", sj)])
                    PS_release(psVc)

                def s6():
                    if kind == "ml":
                        V_(lambda e: e.scalar_tensor_tensor(out=yT[:, t0:t0 + tn], in0=Z[:, t0:t0 + tn], scalar=gcol, in1=sd[:, sj, 0:tn],
                                                            op0=ALU.mult, op1=ALU.mult), [zc, C("sd", sj), "pv"], [C("yT", j)])
                    else:
                        V_(lambda e: e.scalar_tensor_tensor(out=Z[:, t0:t0 + tn], in0=Z[:, t0:t0 + tn], scalar=gcol, in1=sd[:, sj, 0:tn],
                                                            op0=ALU.mult, op1=ALU.mult), [zc, C("sd", sj), "pv"], [zc])
                        V_(lambda e: e.tensor_tensor(out=yT[:, t0:t0 + tn], in0=Z[:, t0:t0 + tn], in1=sgt[:, sj, 0:tn], op=ALU.mult),
                           [zc, C("sgt", sj)], [C("yT", j)])

                def s7():
                    for dc in range(KC):
                        ps, psc = PS()
                        M_(lambda e, ps=ps, dc=dc: e.matmul(ps[:, 0:tn], lhsT=wslh[:, slo, dc * 128:(dc + 1) * 128], rhs=yT[:, t0:t0 + tn],
                                                            start=True, stop=True), [("wh", slo), C("yT", j)], [psc])
                        if dc in POOL_DCS:
                            A_(lambda e, ps=ps: e.activation(out=sd[:, sj, 0:tn], in_=ps[:, 0:tn], func=AF.Copy), [psc], [C("sd", sj)])
                            S.op("pool", lambda e, dc=dc: e.tensor_tensor(out=xT[:, dc, t0:t0 + tn], in0=xT[:, dc, t0:t0 + tn],
                                                                         in1=sd[:, sj, 0:tn], op=ALU.add),
                                 [C("sd", sj), ("xT", dc, j)], [("xT", dc, j)])
                            continue
                        V_(lambda e, ps=ps, dc=dc: e.scalar_tensor_tensor(
                            out=xT[:, dc, t0:t0 + tn], in0=ps[:, 0:tn], scalar=1.0, in1=xT[:, dc, t0:t0 + tn], op0=ALU.mult, op1=ALU.add),
                            [psc, ("xT", dc, j)], [("xT", dc, j)])
                    if tail is not None:
                        tail(j)

                return [s0, s1, s2, s3, s4, s5, s6, s7]

            sts = [stages(j) for j in range(5)]
            start = []
            for j in range(5):
                s_ = j if j == 0 else start[j - 1] + 1
                if j >= nsl:
                    s_ = max(s_, start[j - nsl] + 8)
                start.append(s_)
            for step in range(start[-1] + 8):
                for j in range(5):
                    s = step - start[j]
                    if 0 <= s < 8:
                        sts[j][s]()
                yield

        def mlstm(l, pr):
            new_phase("mlA%d_l%d" % (pr, l))
            Z = carve([N], F32)
            G1 = carve([N], F32)
            G2 = carve([N], F32)
            G3 = carve([N], F32)
            G4 = Z
            qT = carve([2, N], BF16)
            kT = carve([N], BF16)
            qTf = carve([2, 64], F32)
            Vx = carve([17, 2, 128], BF16)
            Ktok = carve([17, 128], BF16)
            St0 = carve([16, 128], F32)
            Snew = St0
            tmps = carve_core_tmps(2)
            dn = carve([2, 512], F32)
            M0 = carve([16], F32)
            m0T = carve([16], F32)
            msave = carve([32], F32)
            nst = carve([128], F32)
            n0st = carve([128], F32)
            u0 = 4 + 6 * pr
            hs = slice(2 * pr, 2 * pr + 2)
            S.op("sp", lambda e: e.dma_start(out=St0[:, :, 0:64], in_=stC_d[l, :, hs].rearrange("b h k v -> (h k) b v")),
                 writes=[C("St0")], dma_key="St0")
            S.op("sp", lambda e: e.dma_start(out=n0st[0:16, :], in_=stn_d[l, :, 128 * pr:128 * pr + 128]), writes=[C("n0st")], dma_key="n0st")
            S.op("sp", lambda e: e.dma_start(out=m0T[0:4, :], in_=stm_d[l].rearrange("b h -> h b"), allow_slow_non_contiguous=True),
                 writes=[C("m0T")], dma_key="m0T")
            ps, psc = PS()
            M_(lambda e, ps=ps: e.transpose(out=ps[:, 0:16], in_=n0st[0:16, :], identity=ident[0:16, 0:16]), [C("n0st"), "ident"], [psc])
            M_(lambda e, ps=ps: e.matmul(ps[:, 64:80], lhsT=sel4[0:4, pr, :], rhs=m0T[0:4, :], start=True, stop=True),
               [C("m0T"), "cstf"], [psc])
            V_(lambda e, ps=ps: e.tensor_copy(out=St0[:, :, 64:128], in_=ps[:, 0:16].unsqueeze(2).broadcast_to([128, 16, 64])),
               [psc], [C("St0")])
            V_(lambda e, ps=ps: e.tensor_copy(out=M0[:, :], in_=ps[:, 64:80]), [psc], [C("M0")])
            V_(lambda e: e.memset(Vx[:, :, :, 64:128], 1.0), [], cells("Vx"))
            V_(lambda e: e.memset(qT[:, :, :], 0.0), [], cells("qT"))
            V_(lambda e: e.memset(qTf[:, :, :], 0.0), [], [C("qTf")])
            sl = load_wh(win_d[l, u0 + 0])

            def c_li(j, t0, tn, ps, psc):
                A_(lambda e: e.activation(out=G1[:, t0:t0 + tn], in_=ps[:, 0:tn], func=AF.Identity, bias=pvcol(("bifi", l), pr), scale=1.0),
                   [psc, "pv"], [C("G1", j)])
            proj_fm(sl, c_li)
            sl = load_wh(win_d[l, u0 + 1])

            def c_lf(j, t0, tn, ps, psc):
                A_(lambda e: e.activation(out=G2[:, t0:t0 + tn], in_=ps[:, 0:tn], func=AF.Sigmoid, bias=pvcol(("biff", l), pr), scale=1.0),
                   [psc, "pv"], [C("G2", j)])
            proj_fm(sl, c_lf)
            sl = load_wh(win_d[l, u0 + 4])

            def e_v(j, nb, rows, ps, psc):
                A_(lambda e: e.activation(out=Vx[0:rows, 4 * j:4 * j + nb, :, 0:64],
                                          in_=ps[0:rows, 0:nb * 128].rearrange("p (b h v) -> p b h v", b=nb, h=2), func=AF.Copy),
                   [psc], [C("Vx", j)])
            proj_tok(sl, e_v)
            A_(lambda e: e.activation(out=G2[:, :], in_=G2[:, :], func=AF.Ln), cells("G2"), cells("G2"))
            V_(lambda e: e.tensor_tensor_scan(out=G3[:, 0:NPROMPT], data0=G2[:, 0:NPROMPT], data1=G1[:, 0:NPROMPT], initial=0.0,
                                              op0=ALU.add, op1=ALU.max), cells("G2", range(4)) + cells("G1", range(4)), cells("G3", range(4)))
            for t in range(4):
                prev = M0[:, :] if t == 0 else sv(G3)[:, :, t - 1]
                V_(lambda e, t=t, prev=prev: e.tensor_tensor(out=sv(G4)[:, :, t], in0=sv(G2)[:, :, t], in1=prev, op=ALU.add),
                   cells("G2", [4]) + cells("G3", [4]) + [C("M0")], cells("Z", [4]))
                V_(lambda e, t=t: e.tensor_tensor(out=sv(G3)[:, :, t], in0=sv(G4)[:, :, t], in1=sv(G1)[:, :, t], op=ALU.max),
                   cells("Z", [4]) + cells("G1", [4]), cells("G3", [4]))
            V_(lambda e: e.tensor_copy(out=msave[:, 0:1], in_=G3[:, NPROMPT - 1:NPROMPT]), cells("G3", [3]), [C("msave")])
            V_(lambda e: e.tensor_copy(out=msave[:, 1:17], in_=sv(G3)[:, :, 3]), cells("G3", [4]), [C("msave")])
            for c in range(16):
                t0 = c * 128
                V_(lambda e, t0=t0: e.tensor_tensor_scan(out=G4[:, t0:t0 + 128], data0=onesf[:, :], data1=G2[:, t0:t0 + 128], initial=0.0,
                                                         op0=ALU.mult, op1=ALU.add), cells("G2", [c // 4]) + ["cstf"], cells("Z", [c // 4]))
                if c > 0:
                    V_(lambda e, t0=t0: e.tensor_scalar(out=G4[:, t0:t0 + 128], in0=G4[:, t0:t0 + 128], scalar1=G3[:, t0 - 1:t0], scalar2=None,
                                                        op0=ALU.add), cells("Z", [c // 4]) + cells("G3", [(c * 128 - 1) // 512]), cells("Z", [c // 4]))
            V_(lambda e: e.tensor_copy(out=sv(G4)[:, :, 0], in_=sv(G2)[:, :, 0]), cells("G2", [4]), cells("Z", [4]))
            for t in range(1, 4):
                V_(lambda e, t=t: e.tensor_tensor(out=sv(G4)[:, :, t], in0=sv(G4)[:, :, t - 1], in1=sv(G2)[:, :, t], op=ALU.add),
                   cells("Z", [4]) + cells("G2", [4]), cells("Z", [4]))
            V_(lambda e: e.tensor_tensor(out=sv(G4), in0=sv(G4), in1=M0[:, :].unsqueeze(2).broadcast_to([128, 16, 4]), op=ALU.add),
               cells("Z", [4]) + [C("M0")], cells("Z", [4]))
            V_(lambda e: e.tensor_tensor(out=G2[:, :], in0=G4[:, :], in1=G3[:, :], op=ALU.subtract), cells("Z") + cells("G3"), cells("G2"))
            V_(lambda e: e.tensor_tensor(out=G1[:, :], in0=G4[:, :], in1=G1[:, :], op=ALU.subtract), cells("Z") + cells("G1"), cells("G1"))
            A_(lambda e: e.activation(out=G2[:, :], in_=G2[:, :], func=AF.Exp), cells("G2"), cells("G2") + cells("eq"))
            A_(lambda e: e.activation(out=G1[:, :], in_=G1[:, :], func=AF.Exp, scale=-1.0), cells("G1"), cells("G1"))
            A_(lambda e: e.activation(out=G3[:, :], in_=G3[:, :], func=AF.Exp, scale=-1.0), cells("G3") + [C("msave")], cells("G3"))
            sl = load_wh(win_d[l, u0 + 2])

            def c_q(j, t0, tn, ps, psc):
                for hh in range(2):
                    pq = slice(64 * hh, 64 * hh + 64)
                    V_(lambda e, hh=hh, pq=pq: e.tensor_tensor(out=qT[pq, hh, t0:t0 + tn], in0=ps[pq, 0:tn], in1=G2[pq, t0:t0 + tn], op=ALU.mult),
                       [psc, C("G2", j)], [C("qT", j)])
                    if j == 4:
                        V_(lambda e, hh=hh, pq=pq: e.tensor_tensor(out=qTf[pq, hh, :], in0=ps[pq, 0:64], in1=G2[pq, t0:t0 + 64], op=ALU.mult),
                           [psc, C("G2", j)], [C("qTf")])
            proj_fm(sl, c_q)
            sl = load_wh(win_d[l, u0 + 3])

            def c_k(j, t0, tn, ps, psc):
                V_(lambda e: e.scalar_tensor_tensor(out=kT[:, t0:t0 + tn], in0=ps[:, 0:tn], scalar=0.125, in1=G1[:, t0:t0 + tn],
                                                    op0=ALU.mult, op1=ALU.mult), [psc, C("G1", j)], [C("kT", j)])
            proj_fm(sl, c_k)
            to_tok(kT, "kT", Ktok, "Ktok")
            eqs = lambda c: G2[:, c * 128 + 127:c * 128 + 128]
            eqs_s = lambda g: G2[:, NPROMPT + 16 * g:NPROMPT + 16 * g + 16].rearrange("p (b t) -> p b t", t=4)[:, :, 3:4].broadcast_to([128, 4, 128])

            def post(j, psO, psOc, psD, psDc):
                t0, tn = TT[j]
                r = j % 2
                A_(lambda e: e.activation(out=dn[:, r, 0:tn], in_=psD[:, 0:tn], func=AF.Abs), [psDc], [C("dn", r)])
                V_(lambda e: e.tensor_tensor(out=dn[:, r, 0:tn], in0=dn[:, r, 0:tn], in1=G3[:, t0:t0 + tn], op=ALU.max),
                   [C("dn", r), C("G3", j)], [C("dn", r)])
                A_(lambda e: e.activation(out=dn[:, r, 0:tn], in_=dn[:, r, 0:tn], func=AF.Ln), [C("dn", r)], [C("dn", r)])
                A_(lambda e: e.activation(out=dn[:, r, 0:tn], in_=dn[:, r, 0:tn], func=AF.Exp, scale=-1.0), [C("dn", r)], [C("dn", r)])
                V_(lambda e: e.tensor_tensor(out=Z[:, t0:t0 + tn], in0=psO[:, 0:tn], in1=dn[:, r, 0:tn], op=ALU.mult),
                   [psOc, C("dn", r)], [C("Z", j)])
            if "m_core" not in SKIP:
                for _ in unit_core(2, lambda h: qT[:, h, :], kT, lambda h: qTf[:, h, :], Vx, Ktok, St0, Snew, eqs, eqs_s, True, post, tmps):
                    pass
            if "m_out" not in SKIP:
                Cst = tmps[0]
                fin = 15 % 3
                S.op("sp", lambda e: e.dma_start(out=oC_d[l, 0, hs].rearrange("h k v -> (h k) v"), in_=Cst[:, fin, 0:64]),
                     reads=[C("Cst", fin)], dma_key="o_C")
                S.op("sp", lambda e: e.dma_start(out=oC_d[l, 1:17, hs].rearrange("b h k v -> (h k) b v"), in_=Snew[:, :, 0:64]),
                     reads=[C("St0")], dma_key="o_C")
                ps, psc = PS()
                M_(lambda e, ps=ps: e.transpose(out=ps[0:32, 0:128], in_=Cst[:, fin, 64:96], identity=ident[:]), [C("Cst", fin), "ident"], [psc])
                M_(lambda e, ps=ps: e.transpose(out=ps[0:16, 128:256], in_=Snew[:, :, 64], identity=ident[:]),
                   [C("St0")] + ["ident"], [psc])
                A_(lambda e, ps=ps: e.activation(out=nst[0:1, :], in_=ps[0:1, 0:128], func=AF.Copy), [psc], [C("nst")])
                A_(lambda e, ps=ps: e.activation(out=n0st[0:16, :], in_=ps[0:16, 128:256], func=AF.Copy), [psc], [C("n0st")])
                S.op("sp", lambda e: e.dma_start(out=on_d[l, 0:1, 128 * pr:128 * pr + 128], in_=nst[0:1, :]), reads=[C("nst")], dma_key="o_n")
                S.op("sp", lambda e: e.dma_start(out=on_d[l, 1:17, 128 * pr:128 * pr + 128], in_=n0st[0:16, :]), reads=[C("n0st")], dma_key="o_n")
                for h in range(2):
                    S.op("sp", lambda e, h=h: e.dma_start(out=om_d[l, :, 2 * pr + h:2 * pr + h + 1].rearrange("s o -> o s"),
                                                          in_=msave[64 * h:64 * h + 1, 0:17], allow_slow_non_contiguous=True),
                         reads=[C("msave")], dma_key="o_m")
            new_phase("mlB%d_l%d" % (pr, l))
            Z = carve([N], F32)
            for _ in b_phase(l, u0 + 5, "ml", Z, bones64[:, :], pvcol(("gmn", l), pr), pr):
                pass

        def ret_A(l, h, A):
            (Z, cs, dec, T1, T2, qT, kT, qTf, Vx, Ktok, St0, tmps) = A
            Snew = St0
            u0 = 16 + 6 * h
            gam = 1.0 - 2.0 ** (-5.0 - h)
            S.op("sp", lambda e: e.dma_start(out=dec[:, :, :], in_=dect_d[h]), writes=[C("dec")], dma_key="dec")
            S.op("sp", lambda e: e.dma_start(out=St0[:, :, :], in_=stS_d[l, :, h].rearrange("b k v -> k b v")), writes=[C("St0")], dma_key="St0r")
            for (which, dstT, dname, uq) in ((0, qT, "qT", 0), (1, kT, "kT", 2)):
                sla = load_wh(win_d[l, u0 + uq])
                slb = load_wh(win_d[l, u0 + uq + 1])
                wva = wslh[:, sla, 0:1024].rearrange("p (k c) -> p k c", k=KC)
                wvb = wslh[:, slb, 0:1024].rearrange("p (k c) -> p k c", k=KC)
                for j, (t0, tn) in enumerate(TT):
                    pa, pac = PS()
                    pb, pbc = PS()
                    for (pp, ppc, wv_, sl_) in ((pa, pac, wva, sla), (pb, pbc, wvb, slb)):
                        for kc in range(KC):
                            M_(lambda e, pp=pp, kc=kc, t0=t0, tn=tn, wv_=wv_: e.matmul(
                                pp[:, 0:tn], lhsT=wv_[:, kc, :], rhs=hT[:, kc, t0:t0 + tn], start=(kc == 0), stop=(kc == KC - 1)),
                                [("wh", sl_), ("hT", kc, j)], [ppc])
                    V_(lambda e, pa=pa, t0=t0, tn=tn: e.tensor_tensor(out=T1[:, 0:tn], in0=pa[:, 0:tn], in1=cs[:, 0, t0:t0 + tn], op=ALU.mult),
                       [pac, C("cs")], [C("T1")])
                    V_(lambda e, pb=pb, t0=t0, tn=tn: e.tensor_tensor(out=T2[:, 0:tn], in0=pb[:, 0:tn], in1=cs[:, 1, t0:t0 + tn], op=ALU.mult),
                       [pbc, C("cs")], [C("T2")])
                    V_(lambda e, tn=tn: e.tensor_tensor(out=T1[:, 0:tn], in0=T1[:, 0:tn], in1=T2[:, 0:tn], op=ALU.add),
                       [C("T1"), C("T2")], [C("T1")])
                    if j < 4:
                        V_(lambda e, t0=t0, dstT=dstT, which=which: e.tensor_tensor(
                            out=dstT[:, t0:t0 + 512].rearrange("p (c t) -> p c t", c=4), in0=T1[:, :].rearrange("p (c t) -> p c t", c=4),
                            in1=dec[:, which, 0:128].unsqueeze(1).broadcast_to([128, 4, 128]), op=ALU.mult),
                            [C("T1"), C("dec")], [C(dname, j)])
                    else:
                        V_(lambda e, t0=t0, dstT=dstT, which=which: e.tensor_tensor(
                            out=dstT[:, t0:t0 + 64], in0=T1[:, 0:64], in1=dec[:, which, 128:192], op=ALU.mult),
                            [C("T1"), C("dec")], [C(dname, j)])
                        if which == 0:
                            V_(lambda e: e.tensor_tensor(out=qTf[:, :], in0=T1[:, 0:64], in1=dec[:, 0, 128:192], op=ALU.mult),
                               [C("T1"), C("dec")], [C("qTf")])
                    yield
            sl = load_wh(win_d[l, u0 + 4])

            def e_v(j, nb, rows, ps, psc):
                A_(lambda e: e.activation(out=Vx[0:rows, 4 * j:4 * j + nb, 0, :],
                                          in_=ps[0:rows, 0:nb * 128].rearrange("p (b v) -> p b v", b=nb), func=AF.Copy),
                   [psc], [C("Vx", j)])
            proj_tok(sl, e_v)
            yield
            to_tok(kT, "kT", Ktok, "Ktok", scale_fn=lambda j: float(gam ** 128) if j < 4 else float(gam ** 4))
            yield

            def post(j, psO, psOc, psD, psDc):
                t0, tn = TT[j]
                A_(lambda e: e.activation(out=Z[:, t0:t0 + tn], in_=psO[:, 0:tn], func=AF.Copy), [psOc], [C("Z", j)])
            yield from unit_core(1, lambda h_: qT, kT, lambda h_: qTf, Vx, Ktok, St0, Snew,
                                 lambda c: float(gam ** 128), lambda g: float(gam ** 4), False, post, tmps, chain1=True)
            Cst = tmps[0]
            fin = 15 % 3
            S.op("sp", lambda e: e.dma_start(out=oS_d[l, 0, h], in_=Cst[:, fin, :]), reads=[C("Cst", fin)], dma_key="o_S")
            S.op("sp", lambda e: e.dma_start(out=oS_d[l, 1:17, h].rearrange("b k v -> k b v"), in_=Snew[:, :, :]),
                 reads=[C("St0")], dma_key="o_S")
            yield

        def retention_section(l, tail=None):
            new_phase("ret_l%d" % l)
            for n_ in ("Z", "yT", "Zb", "Zq", "sd", "sgt"):
                arena_of[n_] = "scrB"
            for n_ in ("cs", "dec", "T1", "T2", "qT", "kT", "qTf", "Vx", "Ktok", "St0", "Cst", "Ctmp", "Cbf", "Pm", "Pms", "Km"):
                arena_of[n_] = "scrA"
            Z = carve([N], F32, "scrB")
            yT = carve([N], BF16, "scrB")
            Bb = (3, yT, carve([3, 512], BF16, "scrB"), carve([3, 512], BF16, "scrB"), carve([3, 512], F32, "scrB"),
                  carve([3, 512], BF16, "scrB"))
            cs = carve([2, N], F32, "scrA")
            dec = carve([2, 192], F32, "scrA")
            T1 = carve([512], F32, "scrA")
            T2 = carve([512], F32, "scrA")
            qT = carve([N], BF16, "scrA")
            kT = carve([N], BF16, "scrA")
            qTf = carve([64], F32, "scrA")
            Vx = carve([17, 1, 128], BF16, "scrA")
            Ktok = carve([17, 128], BF16, "scrA")
            St0 = carve([16, 128], F32, "scrA")
            tmps = (carve([3, 128], F32, "scrA"), carve([128], F32, "scrA"), carve([16, 128], BF16, "scrA"),
                    carve([4, 1, 128], BF16, "scrA"), carve([1, 64], BF16, "scrA"), carve([4, 128], BF16, "scrA"))
            A = (Z, cs, dec, T1, T2, qT, kT, qTf, Vx, Ktok, St0, tmps)
            S.op("sp", lambda e: e.dma_start(out=cs[:, :, :], in_=rope_d), writes=[C("cs")], dma_key="cs")

            def drive(gens):
                live = [[g, nm, pl] for g, nm, pl in gens if g is not None]
                while live:
                    for it in list(live):
                        S.scope = it[1]
                        state["pspool"] = it[2] if len(live) > 1 else None
                        try:
                            next(it[0])
                        except StopIteration:
                            live.remove(it)

            drive([(ret_A(l, 0, A), "rtA0_l%d" % l, None)])
            for h in range(4):
                gb = b_phase(l, 16 + 6 * h + 5, "rt", Z, ones128[:, :], pvcol(("grn", l), h), 2 + h,
                             tail=(tail if h == 3 else None), bufs=Bb)
                ga = ret_A(l, h + 1, A) if h < 3 else None
                drive([(gb, "rtB%d_l%d" % (h, l), "B"), (ga, "rtA%d_l%d" % (h + 1, l), "A")])
            state["pspool"] = None
            arena_of.clear()

        def mixer(l, next_norm=None):
            S.alias_barrier(["w", "wh"])
            if "lru" in PARTS:
                lru(l)
            if "mlstm" in PARTS:
                for pr in range(2):
                    mlstm(l, pr)
            if "ret" in PARTS:
                retention_section(l, tail=next_norm)

        load_x(tail=norm_hT_tile(("nf1", 0)))
        for l in range(DEPTH):
            ffn(l, 1, next_norm=norm_hT_tile(("nmx", l)))
            mixer(l, next_norm=norm_hT_tile(("nf2", l)))
            ffn(l, 2, next_norm=(norm_hT_tile(("nf1", l + 1)) if l + 1 < DEPTH else "final"))
        S.emit(final_wait_keys=[("ost", 0), ("ost", 1), "o_h", "o_cv", "o_C", "o_n", "o_m", "o_S"])
    return nc


_NC_CACHE = {}


def kernel(**inputs):
    inp = {k: np.asarray(v) for k, v in inputs.items()}
    shared, per_core = _host_prep(inp)
    if "nc" not in _NC_CACHE:
        _NC_CACHE["nc"] = build()
    nc = _NC_CACHE["nc"]
    in_maps = [dict(shared, **pc) for pc in per_core]
    res = run_bass_kernel_spmd(nc, in_maps, core_ids=list(range(8)))
    r = res.results
    y = np.stack([r[c]["y"] for c in range(8)])
    y_prompt = np.ascontiguousarray(y[:, :NPROMPT, :])
    y_sample = np.ascontiguousarray(y[:, NPROMPT:, :].reshape(128, 4, D))

    def split(name, tail):
        a = np.stack([r[c][name] for c in range(8)])
        p = np.ascontiguousarray(a[:, :, 0].transpose((1, 0) + tuple(range(2, a.ndim - 1)))).reshape((DEPTH, 8) + tail)
        s = a[:, :, 1:17].transpose((1, 0, 2) + tuple(range(3, a.ndim)))
        s = np.ascontiguousarray(s).reshape((DEPTH, 128) + tail)
        return p, s

    pC, sC = split("oC", (4, 64, 64))
    pn, sn = split("on", (4, 64))
    pm, sm = split("om", (4,))
    pS, sS = split("oS", (4, 128, 128))
    ph, sh = split("oh", (256,))
    pcv, scv = split("ocv", (3, 256))
    return (y_prompt, y_sample, pC, pn, pm, pS, ph, pcv, sC, sn, sm, sS, sh, scv)
```

```python
import contextlib
import numpy as np
import concourse.bass as bass
import concourse.mybir as mybir
from concourse.bass_utils import run_bass_kernel_spmd

F32 = mybir.dt.float32
BF16 = mybir.dt.bfloat16
AF = mybir.ActivationFunctionType
ALU = mybir.AluOpType

D = 1024
KC = 8
DFF = 2816
NFC = 22
NPROMPT = 2048
NSAMP = 64
N = NPROMPT + NSAMP
TT = [(0, 512), (512, 512), (1024, 512), (1536, 512), (2048, 64)]
DEPTH = 2
EPS = 1e-6
NSLOT = 4
FG = 4
SCR = 84544
PARTS = ("lru", "mlstm", "ret")
SKIP = set()
USE_SCOPES = False
POOL_DCS = (3, 7)

COMPUTE = ("pe", "act", "dve", "pool")


class Op:
    __slots__ = ("id", "eng", "fn", "deps", "dma_key", "sig", "sigval", "dma_waits", "scope")

    def __init__(self, id, eng, fn, dma_key):
        self.id = id
        self.eng = eng
        self.fn = fn
        self.deps = []
        self.dma_key = dma_key
        self.sig = False
        self.sigval = 0
        self.dma_waits = {}


def _arena(c):
    return c[0] if isinstance(c, tuple) else c


class Sched:
    def __init__(self, nc):
        self.nc = nc
        self.ops = []
        self.lw = {}
        self.rd = {}
        self.keycnt = {}
        self.extra = {}
        self.arena_touch = {}
        self.scope = None
        self.use_scopes = False

    def op(self, eng, fn, reads=(), writes=(), dma_key=None):
        o = Op(len(self.ops), eng, fn, dma_key)
        o.scope = self.scope
        is_dma = dma_key is not None
        deps = {}

        def add(d, raw):
            if d is None:
                return
            d_dma = d.dma_key is not None
            if (not is_dma) and (not d_dma) and d.eng == eng:
                if not raw or eng == "pe":
                    return
            deps[d.id] = d

        for c in reads:
            add(self.lw.get(c), True)
        for c in writes:
            add(self.lw.get(c), False)
            for r in self.rd.get(c, {}).values():
                add(r, False)
        k = ("dma", o.id) if is_dma else eng
        for c in list(reads) + list(writes):
            ar = _arena(c)
            if ar in self.extra:
                for d in self.extra[ar]:
                    add(d, True)
            if ar in self.arena_touch:
                self.arena_touch[ar][k] = o
        for c in reads:
            self.rd.setdefault(c, {})[k] = o
        for c in writes:
            self.lw[c] = o
            self.rd[c] = {}
        for d in deps.values():
            if d.dma_key is not None:
                o.dma_waits[d.dma_key] = self.keycnt[d.dma_key]
            else:
                o.deps.append(d)
                d.sig = True
        if is_dma:
            self.keycnt[dma_key] = self.keycnt.get(dma_key, 0) + 16
        self.ops.append(o)
        return o

    def track_arena(self, arena):
        self.arena_touch.setdefault(arena, {})

    def alias_barrier(self, arenas):
        if isinstance(arenas, str):
            arenas = [arenas]
        ops = {}
        for ar in arenas:
            for k, o in self.arena_touch.get(ar, {}).items():
                ops[(ar, k)] = o
        lst = list(ops.values())
        for ar in arenas:
            self.extra[ar] = lst
            self.arena_touch[ar] = {}
        for c in [c for c in self.lw if _arena(c) in arenas]:
            del self.lw[c]
        for c in [c for c in self.rd if _arena(c) in arenas]:
            del self.rd[c]

    def emit(self, final_wait_keys=()):
        nc = self.nc
        engs = {"pe": [], "act": [], "dve": [], "pool": [], "sp": []}
        for o in self.ops:
            engs[o.eng].append(o)
        for e, lst in engs.items():
            n = 0
            for o in lst:
                if o.dma_key is None and o.sig:
                    n += 1
                    o.sigval = n
            assert n < 60000, (e, n)
        keys = sorted(self.keycnt.keys(), key=str)
        with contextlib.ExitStack() as st:
            esem = {e: st.enter_context(nc.semaphore("s_" + e)) for e in COMPUTE}
            ksem = {k: st.enter_context(nc.semaphore("k%d" % i)) for i, k in enumerate(keys)}
            block = st.enter_context(nc.Block())

            def run(e, engobj):
                waited = {}
                for o in engs[e]:
                    need = {}
                    for d in o.deps:
                        s = esem[d.eng]
                        need[s] = max(need.get(s, 0), d.sigval)
                    for k, v in o.dma_waits.items():
                        s = ksem[k]
                        need[s] = max(need.get(s, 0), v)
                    for s, v in need.items():
                        if waited.get(s, 0) < v:
                            engobj.wait_ge(s, v)
                            waited[s] = v
                    if self.use_scopes and o.scope is not None:
                        with nc.named_scope(o.scope):
                            ins = o.fn(engobj)
                    else:
                        ins = o.fn(engobj)
                    if o.dma_key is not None:
                        ins.then_inc(ksem[o.dma_key], 16)
                    elif o.sig:
                        ins.then_inc(esem[e], 1)
                if e == "sp":
                    for k in final_wait_keys:
                        if self.keycnt.get(k, 0) > 0:
                            engobj.wait_ge(ksem[k], self.keycnt[k])

            @block.tensor
            def _(eng):
                run("pe", eng)

            @block.scalar
            def _(eng):
                run("act", eng)

            @block.vector
            def _(eng):
                run("dve", eng)

            @block.gpsimd
            def _(eng):
                run("pool", eng)

            @block.sync
            def _(eng):
                run("sp", eng)


def _pv_layout():
    cols = {}
    n = 0

    def add(name, w):
        nonlocal n
        cols[name] = (n, w)
        n += w

    for l in range(DEPTH):
        add(("nf1", l), 8)
        add(("nmx", l), 8)
        add(("nf2", l), 8)
        for nm, w in (("gmn", 2), ("grn", 4), ("wconv", 8), ("bconv", 2), ("ba", 2), ("bi", 2), ("lam", 2), ("bifi", 2), ("biff", 2)):
            add((nm, l), w)
    add("nfin", 8)
    return cols, n


PVC, NPV = _pv_layout()


def _fm(v):
    return np.ascontiguousarray(v.reshape(-1, 128).T)


def _host_prep(inp):
    f = np.float32
    pv = np.zeros((128, NPV), f)

    def put(name, arr):
        o, w = PVC[name]
        pv[:, o:o + w] = arr

    for l in range(DEPTH):
        put(("nf1", l), _fm(inp["norm_ffn1"][l]))
        put(("nmx", l), _fm(inp["norm_mix"][l]))
        put(("nf2", l), _fm(inp["norm_ffn2"][l]))
        put(("gmn", l), _fm(inp["g_mlstm_norm"][l]))
        put(("grn", l), _fm(inp["g_ret_norm"][l]))
        wc = np.zeros((128, 8), f)
        for jj in range(4):
            wc[:, jj * 2:jj * 2 + 2] = _fm(inp["w_conv"][l, jj])
        put(("wconv", l), wc)
        put(("bconv", l), _fm(inp["b_conv"][l]))
        put(("ba", l), _fm(inp["b_lru_a"][l]))
        put(("bi", l), _fm(inp["b_lru_i"][l]))
        put(("lam", l), _fm(inp["lru_lambda"][l]))
        hidx = np.arange(128) // 64
        bi_ = np.zeros((128, 2), f)
        bf_ = np.zeros((128, 2), f)
        for pr in range(2):
            bi_[:, pr] = inp["b_mlstm_if"][l][2 * pr + hidx]
            bf_[:, pr] = inp["b_mlstm_if"][l][4 + 2 * pr + hidx]
        put(("bifi", l), bi_)
        put(("biff", l), bf_)
    put("nfin", _fm(inp["norm_final"]))

    cuts = dict(mq=0, mk=256, mv=512, mo=768, mi=1024, mf=1028, rq=1032, rk=1544, rv=2056, rg=2568, lx=3080, ly=3336)
    ar = np.arange(128)
    sw = np.concatenate([np.arange(64, 128), np.arange(0, 64)])
    units = []
    for c in range(2):
        units += [cuts["lx"] + c * 128 + ar, cuts["ly"] + c * 128 + ar]
    for pr in range(2):
        rep = np.repeat(np.array([2 * pr, 2 * pr + 1]), 64)
        units += [cuts["mi"] + rep, cuts["mf"] + rep, cuts["mq"] + pr * 128 + ar, cuts["mk"] + pr * 128 + ar,
                  cuts["mv"] + pr * 128 + ar, cuts["mo"] + pr * 128 + ar]
    for h in range(4):
        units += [cuts["rq"] + h * 128 + ar, cuts["rq"] + h * 128 + sw, cuts["rk"] + h * 128 + ar, cuts["rk"] + h * 128 + sw,
                  cuts["rv"] + h * 128 + ar, cuts["rg"] + h * 128 + ar]
    cols = np.stack(units)
    win = inp["w_in"][:, :, cols]
    win = win.reshape(DEPTH, KC, 128, 40, 128).transpose(0, 3, 2, 1, 4)
    win = np.ascontiguousarray(win.reshape(DEPTH, 40, 128, 1024))
    wout = np.ascontiguousarray(inp["w_out"].reshape(DEPTH, 8, 128, 1024))
    wlru = np.zeros((DEPTH, 128, 512), f)
    for l in range(DEPTH):
        for gi, W in ((0, inp["w_lru_a"]), (1, inp["w_lru_i"])):
            for c in range(2):
                o = (gi * 2 + c) * 128
                wlru[l, 0:64, o:o + 64] = W[l, 2 * c]
                wlru[l, 64:128, o + 64:o + 128] = W[l, 2 * c + 1]
    cstb = np.zeros((128, 640), f)
    s_ = np.arange(128)
    mc = (s_[:, None] <= s_[None, :]).astype(f)
    cstb[:, 0:128] = mc
    cstb[:, 128:256] = mc
    s6 = np.arange(64)
    mb = ((s6[:, None] // 4 == s6[None, :] // 4) & (s6[:, None] <= s6[None, :])).astype(f)
    cstb[0:64, 256:320] = mb
    cstb[0:64, 320:384] = mb
    cstb[:, 384:512] = (s_[:, None] // 64 == s_[None, :] // 64).astype(f) / 64.0
    cstb[:, 512:640] = 1.0 / 128.0
    cstf = np.zeros((128, 400), f)
    cstf[0:64, 0:16] = (s6[:, None] // 4 == np.arange(16)[None, :]).astype(f)
    for hh in range(4):
        for pr in range(2):
            cstf[hh, 16 + pr * 128:16 + (pr + 1) * 128] = (hh == 2 * pr + s_ // 64).astype(f)
    cstf[:, 272:400] = 1.0
    pos = np.concatenate([np.arange(NPROMPT, dtype=f), np.tile(np.float32(16384.0) + np.arange(4, dtype=f), 16)])
    inv = (np.float32(10000.0) ** (-(np.arange(0, 128, 2, dtype=f) / np.float32(128.0)))).astype(f)
    ang = (pos[:, None] * inv[None, :]).astype(f).astype(np.float64)
    rope = np.zeros((128, 2, N), f)
    rope[0:64, 0] = np.cos(ang).T
    rope[64:128, 0] = np.cos(ang).T
    rope[0:64, 1] = -np.sin(ang).T
    rope[64:128, 1] = np.sin(ang).T
    dect = np.zeros((4, 128, 2, 192), f)
    tl = np.arange(128, dtype=np.float64)
    ts = np.tile(np.arange(4, dtype=np.float64), 16)
    for h in range(4):
        lg = np.log1p(-(2.0 ** (-5.0 - h)))
        dect[h, :, 0, 0:128] = np.exp((tl + 1.0) * lg)
        dect[h, :, 0, 128:192] = np.exp((ts + 1.0) * lg)
        dect[h, :, 1, 0:128] = np.exp(-(tl + 1.0) * lg) * 128.0 ** -0.5
        dect[h, :, 1, 128:192] = np.exp(-(ts + 1.0) * lg) * 128.0 ** -0.5

    def gu_tiles(w):
        L = w.shape[0]
        g = w[:, :, :DFF].reshape(L, KC, 128, NFC, 128)
        u = w[:, :, DFF:].reshape(L, KC, 128, NFC, 128)
        t = np.stack([g, u], axis=4)
        t = t.transpose(0, 3, 2, 1, 4, 5)
        return np.ascontiguousarray(t.reshape(L, NFC, 128, KC * 256))

    def dn_tiles(w):
        L = w.shape[0]
        t = w.reshape(L, NFC // 2, 2, 128, D).transpose(0, 1, 3, 2, 4)
        return np.ascontiguousarray(t.reshape(L, NFC // 2, 128, 2 * D))

    shared = {
        "pv": pv,
        "wgu1": gu_tiles(inp["w_ffn1_gu"]), "wdn1": dn_tiles(inp["w_ffn1_down"]),
        "wgu2": gu_tiles(inp["w_ffn2_gu"]), "wdn2": dn_tiles(inp["w_ffn2_down"]),
        "ident": np.eye(128, dtype=f),
        "win": win, "wout": wout, "wlru": wlru, "cstb": cstb, "cstf": cstf, "rope": rope, "dect": dect,
    }
    per_core = []
    for c in range(8):
        xin = np.concatenate([inp["x_prompt"][c], inp["x_sample"][16 * c:16 * c + 16].reshape(NSAMP, D)], axis=0)
        sl = slice(16 * c, 16 * c + 16)
        per_core.append({
            "xin": np.ascontiguousarray(xin),
            "stC": np.ascontiguousarray(inp["state_mlstm_C"][:, sl]),
            "stn": np.ascontiguousarray(inp["state_mlstm_n"][:, sl].reshape(DEPTH, 16, 256)),
            "stm": np.ascontiguousarray(inp["state_mlstm_m"][:, sl]),
            "stS": np.ascontiguousarray(inp["state_ret_S"][:, sl]),
            "sth": np.ascontiguousarray(inp["state_lru_h"][:, sl]),
            "stcv": np.ascontiguousarray(inp["state_lru_conv"][:, sl].reshape(DEPTH, 48, 256)),
        })
    return shared, per_core


def build():
    nc = bass.Bass("TRN2", target_bir_lowering=False)
    dt_in = lambda name, shape: nc.dram_tensor(name, list(shape), F32, kind="ExternalInput").ap()
    dt_out = lambda name, shape: nc.dram_tensor(name, list(shape), F32, kind="ExternalOutput").ap()
    xin = dt_in("xin", [N, D])
    pv_d = dt_in("pv", [128, NPV])
    ident_d = dt_in("ident", [128, 128])
    wgu_d = {1: dt_in("wgu1", [DEPTH, NFC, 128, 2048]), 2: dt_in("wgu2", [DEPTH, NFC, 128, 2048])}
    wdn_d = {1: dt_in("wdn1", [DEPTH, NFC // 2, 128, 2048]), 2: dt_in("wdn2", [DEPTH, NFC // 2, 128, 2048])}
    y_d = dt_out("y", [N, D])
    win_d = dt_in("win", [DEPTH, 40, 128, 1024])
    wout_d = dt_in("wout", [DEPTH, 8, 128, 1024])
    wlru_d = dt_in("wlru", [DEPTH, 128, 512])
    cstb_d = dt_in("cstb", [128, 640])
    cstf_d = dt_in("cstf", [128, 400])
    rope_d = dt_in("rope", [128, 2, N])
    dect_d = dt_in("dect", [4, 128, 2, 192])
    stC_d = dt_in("stC", [DEPTH, 16, 4, 64, 64])
    stn_d = dt_in("stn", [DEPTH, 16, 256])
    stm_d = dt_in("stm", [DEPTH, 16, 4])
    stS_d = dt_in("stS", [DEPTH, 16, 4, 128, 128])
    sth_d = dt_in("sth", [DEPTH, 16, 256])
    stcv_d = dt_in("stcv", [DEPTH, 48, 256])
    oC_d = dt_out("oC", [DEPTH, 17, 4, 64, 64])
    on_d = dt_out("on", [DEPTH, 17, 256])
    om_d = dt_out("om", [DEPTH, 17, 4])
    oS_d = dt_out("oS", [DEPTH, 17, 4, 128, 128])
    oh_d = dt_out("oh", [DEPTH, 17, 256])
    ocv_d = dt_out("ocv", [DEPTH, 17, 3, 256])

    with contextlib.ExitStack() as st:
        SB = lambda name, shape, dt: st.enter_context(nc.sbuf_tensor(name, list(shape), dt))
        xT = SB("xT", [128, KC, N], F32)
        hT = SB("hT", [128, KC, N], BF16)
        wsl = SB("wsl", [128, NSLOT, 2048], BF16)
        pv = SB("pv_s", [128, NPV], F32)
        ident = SB("ident_s", [128, 128], F32)
        identb = SB("identb", [128, 128], BF16)
        onesb = SB("onesb", [128, 128], BF16)
        nsq = SB("nsq", [128, 2, 512], BF16)
        nrstd = SB("nrstd", [128, 2, 512], F32)
        cstb = SB("cstb_s", [128, 640], BF16)
        cstf = SB("cstf_s", [128, 400], F32)
        maskc = cstb[:, 0:256].rearrange("p (h t) -> p h t", h=2)
        maskb = cstb[:, 256:384].rearrange("p (h t) -> p h t", h=2)
        bones64 = cstb[:, 384:512]
        ones128 = cstb[:, 512:640]
        selb = cstf[:, 0:16]
        sel4 = cstf[:, 16:272].rearrange("p (r q) -> p r q", r=2)
        onesf = cstf[:, 272:400]
        scr = SB("scr", [128, SCR], mybir.dt.uint8)
        psb = [st.enter_context(nc.psum_tensor("ps%d" % i, [128, 512], F32)) for i in range(8)]

        S = Sched(nc)
        S.use_scopes = USE_SCOPES
        S.track_arena("scr")
        S.track_arena("scrA")
        S.track_arena("scrB")
        S.track_arena("w")
        S.track_arena("wh")
        state = {"ps": 0, "w": 0, "wh": 0, "scr_off": 0}

        held = set()
        pools = {None: list(range(8)), "A": [4, 5, 6, 7], "B": [0, 1, 2, 3]}
        pstate = {None: 0, "A": 0, "B": 0}

        def PS(hold=False):
            pool = state.get("pspool")
            banks = pools[pool]
            for _ in range(2 * len(banks)):
                b = banks[pstate[pool] % len(banks)]
                pstate[pool] += 1
                if b not in held:
                    break
            else:
                raise RuntimeError("all PSUM banks of pool held")
            if hold:
                held.add(b)
            return psb[b], ("ps", b)

        def PS_release(cell):
            held.discard(cell[1])

        arena_of = {}
        ARENAS = ["scr", "scrA", "scrB"]
        RB = 28032
        regions = {"scr": (0, SCR), "scrB": (0, RB), "scrA": (RB, SCR)}
        state["off"] = {"scr": 0, "scrA": 0, "scrB": 0}

        def C(name, *idx):
            return (arena_of.get(name, "scr"), name) + idx

        def carve(shape, dt, region="scr"):
            esz = 4 if dt == F32 else 2
            n = 1
            for s_ in shape:
                n *= s_
            base, lim = regions[region]
            off = state["off"][region]
            off = (off + 63) // 64 * 64
            state["off"][region] = off + n * esz
            assert base + off + n * esz <= lim, (region, off + n * esz, lim - base)
            v = scr[:, base + off:base + off + n * esz].bitcast(dt)
            if len(shape) == 2:
                v = v.rearrange("p (a b) -> p a b", a=shape[0])
            elif len(shape) == 3:
                v = v.rearrange("p (a b c) -> p a b c", a=shape[0], b=shape[1])
            return v

        def new_phase(name=None):
            if name is not None:
                S.scope = name
            S.alias_barrier(ARENAS)
            state["off"] = {"scr": 0, "scrA": 0, "scrB": 0}

        wslh = wsl[:, :, :].rearrange("p s (h f) -> p (s h) f", h=2)

        def load_wh(src):
            idx = state["wh"] % (2 * NSLOT)
            state["wh"] += 1
            F = src.shape[-1]
            assert F <= 1024
            S.op("pool", lambda e, idx=idx, src=src, F=F: e.dma_start(out=wslh[:, idx, 0:F], in_=src),
                 writes=[("wh", idx)], dma_key=("wh", idx))
            return idx

        def load_w(src):
            sl = state["w"] % NSLOT
            state["w"] += 1
            F = src.shape[-1]
            S.op("pool", lambda e, sl=sl, src=src, F=F: e.dma_start(out=wsl[:, sl, 0:F], in_=src),
                 writes=[("w", sl)], dma_key=("w", sl))
            return sl

        S.op("sp", lambda e: e.dma_start(out=pv[:], in_=pv_d), writes=["pv"], dma_key="pv")
        S.op("sp", lambda e: e.dma_start(out=ident[:], in_=ident_d), writes=["ident"], dma_key="ident")
        S.op("dve", lambda e: e.tensor_copy(out=identb[:], in_=ident[:]), reads=["ident"], writes=["identb"])
        S.op("dve", lambda e: e.memset(onesb[:], 1.0 / 1024.0), writes=["onesb"])
        S.op("pool", lambda e: e.dma_start(out=cstb[:], in_=cstb_d), writes=["cstb"], dma_key="cstb")
        S.op("sp", lambda e: e.dma_start(out=cstf[:], in_=cstf_d), writes=["cstf"], dma_key="cstf")

        def pvcol(name, j=0, w=1):
            o, _ = PVC[name]
            return pv[:, o + j:o + j + w]

        def load_x(tail=None):
            new_phase("load_x")
            stg = carve([2, 4, D], F32)
            for j, (t0, tn) in enumerate(TT):
                sb = j % 2
                nb = (tn + 127) // 128
                rows = min(128, tn)
                S.op("sp", lambda e, sb=sb, t0=t0, nb=nb, rows=rows: e.dma_start(
                    out=stg[0:rows, sb, 0:nb, :], in_=xin[t0:t0 + nb * rows, :].rearrange("(b p) d -> p b d", p=rows)),
                    writes=[C("stg", sb)], dma_key=("stg", sb))
                for kc in range(KC):
                    ps, psc = PS()
                    for b in range(nb):
                        S.op("pe", lambda e, ps=ps, sb=sb, b=b, kc=kc, rows=rows: e.transpose(
                            out=ps[:, b * rows:(b + 1) * rows], in_=stg[0:rows, sb, b, kc * 128:(kc + 1) * 128],
                            identity=ident[0:rows, 0:rows]),
                            reads=[C("stg", sb), "ident"], writes=[psc])
                    eng = "act" if kc % 2 == 0 else "dve"
                    if eng == "act":
                        S.op("act", lambda e, ps=ps, kc=kc, t0=t0, tn=tn: e.activation(
                            out=xT[:, kc, t0:t0 + tn], in_=ps[:, 0:tn], func=AF.Copy),
                            reads=[psc], writes=[("xT", kc, j)])
                    else:
                        S.op("dve", lambda e, ps=ps, kc=kc, t0=t0, tn=tn: e.tensor_copy(
                            out=xT[:, kc, t0:t0 + tn], in_=ps[:, 0:tn]),
                            reads=[psc], writes=[("xT", kc, j)])
                if tail is not None:
                    tail(j)

        def rms_stages(j, gname, dst_fn, dst_cells):
            sq, rstd = nsq, nrstd
            t0, tn = TT[j]
            r = j % 2
            box = {}

            def st_a():
                ps, psc = PS(hold=True)
                box["p"] = (ps, psc)
                for kc in range(KC):
                    s = kc % 2
                    S.op("act", lambda e, s=s, kc=kc: e.activation(out=sq[:, s, 0:tn], in_=xT[:, kc, t0:t0 + tn], func=AF.Square),
                         reads=[("xT", kc, j)], writes=[("nsq", s)])
                    S.op("pe", lambda e, s=s, kc=kc, ps=ps: e.matmul(ps[:, 0:tn], lhsT=onesb[:], rhs=sq[:, s, 0:tn],
                                                                      start=(kc == 0), stop=(kc == KC - 1)),
                         reads=[("nsq", s), "onesb"], writes=[psc])

            def st_b():
                ps, psc = box["p"]
                S.op("act", lambda e: e.activation(out=rstd[:, r, 0:tn], in_=ps[:, 0:tn], func=AF.Ln, bias=EPS, scale=1.0),
                     reads=[psc], writes=[("nrstd", r)])
                S.op("act", lambda e: e.activation(out=rstd[:, r, 0:tn], in_=rstd[:, r, 0:tn], func=AF.Exp, scale=-0.5),
                     reads=[("nrstd", r)], writes=[("nrstd", r)])
                PS_release(psc)

            def st_c():
                for kc in range(KC):
                    S.op("dve", lambda e, kc=kc: e.scalar_tensor_tensor(
                        out=dst_fn(kc), in0=xT[:, kc, t0:t0 + tn], scalar=pvcol(gname, kc), in1=rstd[:, r, 0:tn],
                        op0=ALU.mult, op1=ALU.mult),
                        reads=[("xT", kc, j), ("nrstd", r), "pv"], writes=[dst_cells(kc)])

            return [st_a, st_b, st_c]

        def rms_tile(j, gname, dst_fn, dst_cells):
            for st_ in rms_stages(j, gname, dst_fn, dst_cells):
                st_()

        def norm_hT_tile(gname):
            def f(j):
                t0, tn = TT[j]
                rms_tile(j, gname, lambda kc: hT[:, kc, t0:t0 + tn], lambda kc: ("hT", kc, j))

            def stages(j):
                t0, tn = TT[j]
                return rms_stages(j, gname, lambda kc: hT[:, kc, t0:t0 + tn], lambda kc: ("hT", kc, j))
            f.stages = stages
            return f

        def ffn(l, which, next_norm=None):
            new_phase("ffn%d_l%d" % (which, l))
            S.alias_barrier(["w", "wh"])
            sg = carve([3, 512], F32)
            actb = carve([2 * FG, N], BF16)
            if next_norm == "final":
                next_norm = make_final_tile()
            groups = [list(range(g, min(g + FG, NFC))) for g in range(0, NFC, FG)]
            sgc = [0]

            def gu(fc, aslot, fl):
                sl = load_w(wgu_d[which][l, fc])
                wv = wsl[:, sl, :].rearrange("p (k c) -> p k c", k=KC)
                for j, (t0, tn) in enumerate(TT):
                    pg, pgc = PS()
                    pu, puc = PS()
                    for (pp, ppc, co) in ((pg, pgc, 0), (pu, puc, 128)):
                        for kc in range(KC):
                            S.op("pe", lambda e, pp=pp, kc=kc, co=co, wv=wv, t0=t0, tn=tn: e.matmul(
                                pp[:, 0:tn], lhsT=wv[:, kc, co:co + 128], rhs=hT[:, kc, t0:t0 + tn],
                                start=(kc == 0), stop=(kc == KC - 1)),
                                reads=[("w", sl), ("hT", kc, j)], writes=[ppc])
                    s = sgc[0] % 3
                    sgc[0] += 1
                    S.op("act", lambda e, pg=pg, s=s, tn=tn: e.activation(out=sg[:, s, 0:tn], in_=pg[:, 0:tn], func=AF.Silu),
                         reads=[pgc], writes=[C("sg", s)])
                    S.op("dve", lambda e, pu=pu, s=s, a=aslot * FG + fl, t0=t0, tn=tn: e.tensor_tensor(
                        out=actb[:, a, t0:t0 + tn], in0=pu[:, 0:tn], in1=sg[:, s, 0:tn], op=ALU.mult),
                        reads=[puc, C("sg", s)], writes=[C("act", aslot * FG + fl, j)])

            pend_norm = []

            def down(items, tail=None):
                chunks = []
                for (g, aslot) in items:
                    fcs = groups[g]
                    sls = [load_w(wdn_d[which][l, fc // 2]) for fc in fcs[::2]]
                    for i, fc in enumerate(fcs):
                        chunks.append((sls[i // 2], i % 2, aslot * FG + i))
                for j, (t0, tn) in enumerate(TT):
                    for dc in range(KC):
                        ps, psc = PS()
                        for n_, (sl, half, asl) in enumerate(chunks):
                            wv = wsl[:, sl, :].rearrange("p (f d) -> p f d", f=2)
                            S.op("pe", lambda e, ps=ps, wv=wv, half=half, asl=asl, dc=dc, t0=t0, tn=tn, n_=n_: e.matmul(
                                ps[:, 0:tn], lhsT=wv[:, half, dc * 128:(dc + 1) * 128], rhs=actb[:, asl, t0:t0 + tn],
                                start=(n_ == 0), stop=(n_ == len(chunks) - 1)),
                                reads=[("w", sl), C("act", asl, j)], writes=[psc])
                        S.op("dve", lambda e, ps=ps, dc=dc, t0=t0, tn=tn: e.scalar_tensor_tensor(
                            out=xT[:, dc, t0:t0 + tn], in0=ps[:, 0:tn], scalar=0.5, in1=xT[:, dc, t0:t0 + tn],
                            op0=ALU.mult, op1=ALU.add),
                            reads=[psc, ("xT", dc, j)], writes=[("xT", dc, j)])
                    if tail is not None:
                        if hasattr(tail, "stages"):
                            st_ = tail.stages(j)
                            st_[0]()
                            for f_ in pend_norm:
                                f_()
                            del pend_norm[:]
                            pend_norm.extend(st_[1:])
                        else:
                            tail(j)
                for f_ in pend_norm:
                    f_()
                del pend_norm[:]

            pending = None
            ng = len(groups)
            for g, fcs in enumerate(groups):
                aslot = g % 2
                for fl, fc in enumerate(fcs):
                    gu(fc, aslot, fl)
                    if fl == 0 and pending is not None and g < ng - 1:
                        down([pending])
                        pending = None
                if g < ng - 2:
                    pending = (g, aslot)
            down([(ng - 2, (ng - 2) % 2), (ng - 1, (ng - 1) % 2)], tail=next_norm)

        def make_final_tile():
            hf = carve([2, KC, 512], F32)
            ost = carve([2, D], F32)
            cnt = [0]

            def final_tile(j):
                t0, tn = TT[j]
                hs = j % 2
                rms_tile(j, "nfin", lambda kc: hf[:, hs, kc, 0:tn], lambda kc: C("hf", hs, kc))
                nb = (tn + 127) // 128
                rows = min(128, tn)
                for b in range(nb):
                    os_ = cnt[0] % 2
                    cnt[0] += 1
                    for half in range(2):
                        ps, psc = PS()
                        for q in range(4):
                            kc = half * 4 + q
                            S.op("pe", lambda e, ps=ps, q=q, kc=kc, b=b: e.transpose(
                                out=ps[0:rows, q * 128:(q + 1) * 128], in_=hf[:, hs, kc, b * rows:(b + 1) * rows], identity=ident[:]),
                                reads=[C("hf", hs, kc), "ident"], writes=[psc])
                        if half == 0:
                            S.op("act", lambda e, ps=ps, os_=os_: e.activation(
                                out=ost[0:rows, os_, 0:512], in_=ps[0:rows, :], func=AF.Copy),
                                reads=[psc], writes=[C("ost", os_, 0)])
                        else:
                            S.op("dve", lambda e, ps=ps, os_=os_: e.tensor_copy(
                                out=ost[0:rows, os_, 512:1024], in_=ps[0:rows, :]),
                                reads=[psc], writes=[C("ost", os_, 1)])
                    S.op("sp", lambda e, os_=os_, r0=t0 + b * rows: e.dma_start(
                        out=y_d[r0:r0 + rows, :], in_=ost[0:rows, os_, :]),
                        reads=[C("ost", os_, 0), C("ost", os_, 1)], dma_key=("ost", os_))
            return final_tile

        def cells(name, js=None):
            return [C(name, j) for j in (range(5) if js is None else js)]

        def A_(fn, reads, writes):
            S.op("act", fn, reads, writes)

        def V_(fn, reads, writes):
            S.op("dve", fn, reads, writes)

        def M_(fn, reads, writes):
            S.op("pe", fn, reads, writes)

        def proj_fm(sl, consume):
            wv = wslh[:, sl, 0:1024].rearrange("p (k c) -> p k c", k=KC)
            for j, (t0, tn) in enumerate(TT):
                ps, psc = PS()
                for kc in range(KC):
                    M_(lambda e, ps=ps, kc=kc, t0=t0, tn=tn, wv=wv: e.matmul(
                        ps[:, 0:tn], lhsT=wv[:, kc, :], rhs=hT[:, kc, t0:t0 + tn], start=(kc == 0), stop=(kc == KC - 1)),
                        [("wh", sl), ("hT", kc, j)], [psc])
                consume(j, t0, tn, ps, psc)

        def proj_tok(sl, evac):
            wv = wslh[:, sl, 0:1024].rearrange("p (k c) -> p k c", k=KC)
            for j, (t0, tn) in enumerate(TT):
                ps, psc = PS()
                rows = min(128, tn)
                nb = (tn + 127) // 128
                for b in range(nb):
                    for kc in range(KC):
                        M_(lambda e, ps=ps, kc=kc, b=b, rows=rows, t0=t0, wv=wv: e.matmul(
                            ps[0:rows, b * 128:(b + 1) * 128], lhsT=hT[:, kc, t0 + b * rows:t0 + (b + 1) * rows], rhs=wv[:, kc, :],
                            start=(kc == 0), stop=(kc == KC - 1)),
                            [("wh", sl), ("hT", kc, j)], [psc])
                evac(j, nb, rows, ps, psc)

        def to_tok(srcT, srcname, dst, dstname, scale_fn=None):
            for j, (t0, tn) in enumerate(TT):
                ps, psc = PS()
                psb_ = ps.bitcast(BF16)
                rows = min(128, tn)
                nb = (tn + 127) // 128
                for b in range(nb):
                    M_(lambda e, psb_=psb_, b=b, rows=rows, t0=t0: e.transpose(
                        out=psb_[0:rows, b * 128:(b + 1) * 128], in_=srcT[:, t0 + b * rows:t0 + (b + 1) * rows], identity=identb[:]),
                        [("scr", srcname, j), "identb"], [psc])
                A_(lambda e, psb_=psb_, j=j, nb=nb, rows=rows, sc_=(1.0 if scale_fn is None else scale_fn(j)): e.activation(
                    out=dst[0:rows, 4 * j:4 * j + nb, :], in_=psb_[0:rows, 0:nb * 128].rearrange("p (b c) -> p b c", b=nb), func=AF.Copy, scale=sc_),
                    [psc], [("scr", dstname, j)])

        def out_proj(l, kcs, src, src_cells, tail=None):
            sls = [load_wh(wout_d[l, kc]) for kc in kcs]
            for j, (t0, tn) in enumerate(TT):
                for dc in range(KC):
                    ps, psc = PS()
                    for i in range(len(kcs)):
                        M_(lambda e, ps=ps, i=i, dc=dc, t0=t0, tn=tn: e.matmul(
                            ps[:, 0:tn], lhsT=wslh[:, sls[i], dc * 128:(dc + 1) * 128], rhs=src(i, t0, tn),
                            start=(i == 0), stop=(i == len(kcs) - 1)),
                            [("wh", sls[i])] + src_cells(i, j), [psc])
                    V_(lambda e, ps=ps, dc=dc, t0=t0, tn=tn: e.scalar_tensor_tensor(
                        out=xT[:, dc, t0:t0 + tn], in0=ps[:, 0:tn], scalar=1.0, in1=xT[:, dc, t0:t0 + tn],
                        op0=ALU.mult, op1=ALU.add),
                        [psc, ("xT", dc, j)], [("xT", dc, j)])
                if tail is not None:
                    tail(j)

        def sv(buf):
            return buf[:, NPROMPT:N].rearrange("p (b t) -> p b t", t=4)

        def lru(l):
            for c in range(2):
                lru_chunk(l, c)

        def lru_chunk(l, c):
            if True:
                new_phase("lru%d_l%d" % (c, l))
                yl = carve([2, N], BF16)
                ohst = carve([256], F32)
                ocst = carve([256], F32)
                XPp = carve([NPROMPT + 3], F32)
                XPs = carve([16, 7], F32)
                xc = carve([N], F32)
                xcb = carve([N], BF16)
                Ab = carve([N], F32)
                Ub = carve([N], F32)
                Gb = carve([N], F32)
                HS = carve([N], F32)
                T1 = carve([N], F32)
                cvst = carve([256], F32)
                hst = carve([256], F32)
                h0T = carve([16], F32)
                hl = carve([32], F32)
                cv = carve([17, 3], F32)
                spn = carve([2], F32)
                u0 = 2 * c
                S.op("sp", lambda e: e.dma_start(out=cvst[0:48, :], in_=stcv_d[l]), writes=[C("cvst")], dma_key="cvst")
                S.op("sp", lambda e: e.dma_start(out=hst[0:16, :], in_=sth_d[l]), writes=[C("hst")], dma_key="hst")
                ps, psc = PS()
                M_(lambda e, ps=ps: e.transpose(out=ps[:, 0:48], in_=cvst[0:48, c * 128:(c + 1) * 128], identity=ident[0:48, 0:48]),
                   [C("cvst"), "ident"], [psc])
                M_(lambda e, ps=ps: e.transpose(out=ps[:, 64:80], in_=hst[0:16, c * 128:(c + 1) * 128], identity=ident[0:16, 0:16]),
                   [C("hst"), "ident"], [psc])
                A_(lambda e, ps=ps: e.activation(out=XPs[:, :, 0:3], in_=ps[:, 0:48].rearrange("p (b j) -> p b j", j=3), func=AF.Copy),
                   [psc], [C("XPs")])
                A_(lambda e, ps=ps: e.activation(out=h0T[:, :], in_=ps[:, 64:80], func=AF.Copy), [psc], [C("h0T")])
                V_(lambda e: e.memset(XPp[:, 0:3], 0.0), [], [C("XPp0")])
                A_(lambda e: e.activation(out=spn[:, :], in_=pvcol(("lam", l), 0, 2), func=AF.Exp, scale=-1.0), ["pv"], [C("spn")])
                A_(lambda e: e.activation(out=spn[:, :], in_=spn[:, :], func=AF.Ln, bias=1.0), [C("spn")], [C("spn")])
                V_(lambda e: e.tensor_scalar(out=spn[:, :], in0=spn[:, :], scalar1=-8.0, scalar2=None, op0=ALU.mult),
                   [C("spn")], [C("spn")])
                sl = load_wh(win_d[l, u0 + 0])

                def c_lx(j, t0, tn, ps, psc):
                    if j < 4:
                        A_(lambda e: e.activation(out=XPp[:, 3 + t0:3 + t0 + tn], in_=ps[:, 0:tn], func=AF.Copy), [psc], [C("XP", j)])
                    else:
                        A_(lambda e: e.activation(out=XPs[:, :, 3:7], in_=ps[:, 0:64].rearrange("p (b t) -> p b t", t=4), func=AF.Copy),
                           [psc], [C("XP", 4)])
                proj_fm(sl, c_lx)
                sl = load_wh(win_d[l, u0 + 1])

                def c_ly(j, t0, tn, ps, psc):
                    A_(lambda e: e.activation(out=Gb[:, t0:t0 + tn], in_=ps[:, 0:tn], func=AF.Gelu_apprx_tanh), [psc], [C("G", j)])
                proj_fm(sl, c_ly)
                wc = lambda jj: pvcol(("wconv", l), jj * 2 + c)
                xpr = [C("XP", j) for j in range(4)] + [C("XPp0")]
                V_(lambda e: e.tensor_scalar(out=xc[:, 0:NPROMPT], in0=XPp[:, 0:NPROMPT], scalar1=wc(0), scalar2=pvcol(("bconv", l), c),
                                             op0=ALU.mult, op1=ALU.add), xpr + ["pv"], cells("xc", range(4)))
                for jj in range(1, 4):
                    V_(lambda e, jj=jj: e.scalar_tensor_tensor(out=xc[:, 0:NPROMPT], in0=XPp[:, jj:jj + NPROMPT], scalar=wc(jj),
                                                               in1=xc[:, 0:NPROMPT], op0=ALU.mult, op1=ALU.add),
                       xpr + ["pv"] + cells("xc", range(4)), cells("xc", range(4)))
                V_(lambda e: e.tensor_scalar(out=sv(xc), in0=XPs[:, :, 0:4], scalar1=wc(0), scalar2=pvcol(("bconv", l), c),
                                             op0=ALU.mult, op1=ALU.add), [C("XP", 4), C("XPs"), "pv"], cells("xc", [4]))
                for jj in range(1, 4):
                    V_(lambda e, jj=jj: e.scalar_tensor_tensor(out=sv(xc), in0=XPs[:, :, jj:jj + 4], scalar=wc(jj), in1=sv(xc),
                                                               op0=ALU.mult, op1=ALU.add),
                       [C("XP", 4), C("XPs"), "pv"] + cells("xc", [4]), cells("xc", [4]))
                A_(lambda e: e.activation(out=xcb[:, :], in_=xc[:, :], func=AF.Copy), cells("xc"), cells("xcb"))
                slw = load_wh(wlru_d[l])
                for (gi, dst, dname, bname) in ((0, Ab, "A", "ba"), (1, Ub, "U", "bi")):
                    for j, (t0, tn) in enumerate(TT):
                        ps, psc = PS()
                        M_(lambda e, ps=ps, gi=gi, t0=t0, tn=tn: e.matmul(
                            ps[:, 0:tn], lhsT=wslh[:, slw, (gi * 2 + c) * 128:(gi * 2 + c + 1) * 128], rhs=xcb[:, t0:t0 + tn],
                            start=True, stop=True), [("wh", slw), C("xcb", j)], [psc])
                        A_(lambda e, ps=ps, dst=dst, bname=bname, t0=t0, tn=tn: e.activation(
                            out=dst[:, t0:t0 + tn], in_=ps[:, 0:tn], func=AF.Sigmoid, bias=pvcol((bname, l), c), scale=1.0),
                            [psc, "pv"], [("scr", dname, j)])
                A_(lambda e: e.activation(out=Ab[:, :], in_=Ab[:, :], func=AF.Exp, scale=spn[:, c:c + 1]),
                   cells("A") + [C("spn")], cells("A"))
                V_(lambda e: e.scalar_tensor_tensor(out=T1[:, :], in0=Ab[:, :], scalar=1.0, in1=Ab[:, :], op0=ALU.min, op1=ALU.mult),
                   cells("A"), cells("T1"))
                A_(lambda e: e.activation(out=T1[:, :], in_=T1[:, :], func=AF.Sqrt, bias=1.0, scale=-1.0), cells("T1"), cells("T1"))
                V_(lambda e: e.tensor_tensor(out=Ub[:, :], in0=Ub[:, :], in1=T1[:, :], op=ALU.mult), cells("U") + cells("T1"), cells("U"))
                V_(lambda e: e.tensor_tensor(out=Ub[:, :], in0=Ub[:, :], in1=xc[:, :], op=ALU.mult), cells("U") + cells("xc"), cells("U"))
                V_(lambda e: e.tensor_tensor_scan(out=HS[:, 0:NPROMPT], data0=Ab[:, 0:NPROMPT], data1=Ub[:, 0:NPROMPT], initial=0.0,
                                                  op0=ALU.mult, op1=ALU.add),
                   cells("A", range(4)) + cells("U", range(4)), cells("HS", range(4)))
                for b in range(16):
                    o = NPROMPT + 4 * b
                    V_(lambda e, b=b, o=o: e.tensor_tensor_scan(out=HS[:, o:o + 4], data0=Ab[:, o:o + 4], data1=Ub[:, o:o + 4],
                                                                initial=h0T[:, b:b + 1], op0=ALU.mult, op1=ALU.add),
                       cells("A", [4]) + cells("U", [4]) + [C("h0T")], cells("HS", [4]))
                V_(lambda e: e.tensor_tensor(out=yl[:, c, :], in0=HS[:, :], in1=Gb[:, :], op=ALU.mult),
                   cells("HS") + cells("G"), [C("yl", c, j) for j in range(5)])
                A_(lambda e: e.activation(out=hl[:, 0:1], in_=HS[:, NPROMPT - 1:NPROMPT], func=AF.Copy), cells("HS", [3]), [C("hl")])
                A_(lambda e: e.activation(out=hl[:, 1:17], in_=sv(HS)[:, :, 3], func=AF.Copy), cells("HS", [4]), [C("hl")])
                A_(lambda e: e.activation(out=cv[:, 0, :], in_=XPp[:, NPROMPT:NPROMPT + 3], func=AF.Copy), [C("XP", 3)], [C("cv")])
                A_(lambda e: e.activation(out=cv[:, 1:17, :], in_=XPs[:, :, 4:7], func=AF.Copy), [C("XP", 4)], [C("cv")])
                ps, psc = PS()
                M_(lambda e, ps=ps: e.transpose(out=ps[0:17, 0:128], in_=hl[:, 0:17], identity=ident[:]), [C("hl"), "ident"], [psc])
                M_(lambda e, ps=ps: e.transpose(out=ps[0:51, 128:256], in_=cv[:, :, :].rearrange("p s j -> p (s j)"), identity=ident[:]),
                   [C("cv"), "ident"], [psc])
                A_(lambda e, ps=ps: e.activation(out=ohst[0:17, c * 128:(c + 1) * 128], in_=ps[0:17, 0:128], func=AF.Copy), [psc], [C("ohst", c)])
                A_(lambda e, ps=ps: e.activation(out=ocst[0:51, c * 128:(c + 1) * 128], in_=ps[0:51, 128:256], func=AF.Copy), [psc], [C("ocst", c)])
                if c == 1:
                    S.op("sp", lambda e: e.dma_start(out=oh_d[l], in_=ohst[0:17, :]), reads=[C("ohst", 0), C("ohst", 1)], dma_key="o_h")
                    S.op("sp", lambda e: e.dma_start(out=ocv_d[l].rearrange("s j c -> (s j) c"), in_=ocst[0:51, :]),
                         reads=[C("ocst", 0), C("ocst", 1)], dma_key="o_cv")
                    out_proj(l, [6, 7], lambda i, t0, tn: yl[:, i, t0:t0 + tn], lambda i, j: [C("yl", i, j)])

        def unit_core(nh, qsel, kT, qfsel, Vx, Ktok, St0, Snew, scale_p, scale_s, has_den, post, tmps, chain1=False):
            Cst, Ctmp, Cbf, Pm, Pms, Km = tmps
            dsz = 128 // nh
            ow = 128 // nh
            if "core_p" not in SKIP:
                def emit_P(c):
                    t0 = c * 128
                    psP, psPc = PS(hold=True)
                    for h in range(nh):
                        M_(lambda e, psP=psP, h=h, t0=t0: e.matmul(
                            psP[:, h * 128:(h + 1) * 128], lhsT=kT[:, t0:t0 + 128], rhs=qsel(h)[:, t0:t0 + 128],
                            start=True, stop=True), [C("kT", c // 4), C("qT", c // 4)], [psPc])
                    return psP, psPc

                nextP = emit_P(0)
                for j in range(4):
                    psS, psSc = PS(hold=True)
                    for cc in range(4):
                        c = 4 * j + cc
                        for h in range(nh):
                            M_(lambda e, psS=psS, cc=cc, c=c, h=h: e.matmul(
                                psS[h * dsz:(h + 1) * dsz, cc * 128:(cc + 1) * 128], lhsT=Ktok[:, c, h * dsz:(h + 1) * dsz], rhs=Vx[:, c, h, :],
                                start=True, stop=True), [C("Ktok", j), C("Vx", j)], [psSc])
                    psO, psOc = PS(hold=True)
                    if has_den:
                        psD, psDc = PS(hold=True)
                    else:
                        psD, psDc = None, None
                    for cc in range(4):
                        c = 4 * j + cc
                        t0 = c * 128
                        psP, psPc = nextP
                        if c + 1 < 16:
                            nextP = emit_P(c + 1)
                        pslot = c % 4
                        V_(lambda e, psP=psP, pslot=pslot: e.tensor_tensor(
                            out=Pm[:, pslot, 0:nh, :], in0=psP[:, 0:nh * 128].rearrange("p (h t) -> p h t", h=nh), in1=maskc[:, 0:nh, :], op=ALU.mult),
                            [psPc, "cstb"], [C("Pm", pslot)])
                        PS_release(psPc)
                        slot = c % 3
                        src_ = psS[:, cc * 128:(cc + 1) * 128]
                        if chain1:
                            if c == 0:
                                V_(lambda e, src_=src_, slot=slot: e.tensor_copy(out=Cst[:, slot, :], in_=src_), [psSc], [C("Cst", slot)])
                            else:
                                V_(lambda e, src_=src_, slot=slot, c=c, ps_=(c - 1) % 3: e.scalar_tensor_tensor(
                                    out=Cst[:, slot, :], in0=Cst[:, ps_, :], scalar=scale_p(c), in1=src_, op0=ALU.mult, op1=ALU.add),
                                    [psSc, C("Cst", (c - 1) % 3)], [C("Cst", slot)])
                        else:
                            if c == 0:
                                V_(lambda e, src_=src_: e.tensor_copy(out=Ctmp[:, :], in_=src_), [psSc], [C("Ctmp")])
                            else:
                                V_(lambda e, src_=src_, ps_=(c - 1) % 3: e.tensor_tensor(out=Ctmp[:, :], in0=src_, in1=Cst[:, ps_, :], op=ALU.add),
                                   [psSc, C("Cst", (c - 1) % 3)], [C("Ctmp")])
                            A_(lambda e, c=c, slot=slot: e.activation(out=Cst[:, slot, :], in_=Ctmp[:, :], func=AF.Copy, scale=scale_p(c)),
                               [C("Ctmp")] + cells("eq", range(4)), [C("Cst", slot)])
                        if c < 15:
                            A_(lambda e, c=c, slot=slot: e.activation(out=Cbf[:, c + 1, :], in_=Cst[:, slot, :], func=AF.Copy),
                               [C("Cst", slot)], [C("Cbf", c + 1)])
                        for h in range(nh):
                            outs = [(psO, psOc, 0)] + ([(psD, psDc, 64)] if has_den else [])
                            for (pp, ppc, off) in outs:
                                M_(lambda e, pp=pp, h=h, c=c, cc=cc, pslot=pslot, off=off: e.matmul(
                                    pp[h * ow:(h + 1) * ow, cc * 128:(cc + 1) * 128], lhsT=Vx[:, c, h, off:off + ow], rhs=Pm[:, pslot, h, :],
                                    start=True, stop=(c == 0)), [C("Vx", j), C("Pm", pslot)], [ppc])
                                if c > 0:
                                    M_(lambda e, pp=pp, h=h, c=c, cc=cc, t0=t0, off=off: e.matmul(
                                        pp[h * ow:(h + 1) * ow, cc * 128:(cc + 1) * 128], lhsT=Cbf[:, c, off:off + ow],
                                        rhs=qsel(h)[:, t0:t0 + 128], start=False, stop=True),
                                        [C("Cbf", c), C("qT", j)], [ppc])
                        yield
                    PS_release(psSc)
                    post(j, psO, psOc, psD, psDc)
                    PS_release(psOc)
                    if has_den:
                        PS_release(psDc)
            if "core_s" not in SKIP:
                o = NPROMPT
                psP, psPc = PS()
                for h in range(nh):
                    M_(lambda e, h=h, psP=psP: e.matmul(
                        psP[0:64, h * 64:(h + 1) * 64], lhsT=kT[:, o:o + 64], rhs=qsel(h)[:, o:o + 64],
                        start=True, stop=True), [C("kT", 4), C("qT", 4)], [psPc])
                V_(lambda e, psP=psP: e.tensor_tensor(
                    out=Pms[0:64, 0:nh, :], in0=psP[0:64, 0:nh * 64].rearrange("p (h t) -> p h t", h=nh), in1=maskb[0:64, 0:nh, :], op=ALU.mult),
                    [psPc, "cstb"], [C("Pms")])
                psO, psOc = PS()
                if has_den:
                    psD, psDc = PS()
                else:
                    psD, psDc = None, None
                for h in range(nh):
                    outs = [(psO, psOc, 0)] + ([(psD, psDc, 64)] if has_den else [])
                    for (pp, ppc, off) in outs:
                        M_(lambda e, pp=pp, h=h, off=off: e.matmul(
                            pp[h * ow:(h + 1) * ow, 0:64], lhsT=Vx[0:64, 16, h, off:off + ow], rhs=Pms[0:64, h, :], start=True, stop=False),
                            [C("Vx", 4), C("Pms")], [ppc])
                        for b in range(16):
                            M_(lambda e, pp=pp, h=h, off=off, b=b: e.matmul(
                                pp[h * ow:(h + 1) * ow, 4 * b:4 * b + 4], lhsT=St0[:, b, off:off + ow],
                                rhs=qfsel(h)[:, 4 * b:4 * b + 4], start=False, stop=(b == 15)),
                                [C("St0"), C("qTf")], [ppc])
                post(4, psO, psOc, psD, psDc)
                yield
                for g in range(4):
                    psS, psSc = PS()
                    for bb in range(4):
                        b = 4 * g + bb
                        V_(lambda e, b=b, bb=bb: e.tensor_scalar(out=Km[0:64, bb, :], in0=Ktok[0:64, 16, :], scalar1=selb[0:64, b:b + 1], scalar2=None,
                                                                 op0=ALU.mult), [C("Ktok", 4), "cstf"], [C("Km", bb)])
                        for h in range(nh):
                            M_(lambda e, psS=psS, bb=bb, h=h: e.matmul(
                                psS[h * dsz:(h + 1) * dsz, bb * 128:(bb + 1) * 128], lhsT=Km[0:64, bb, h * dsz:(h + 1) * dsz], rhs=Vx[0:64, 16, h, :],
                                start=True, stop=True), [C("Km", bb), C("Vx", 4)], [psSc])
                    sc = scale_s(g)
                    if chain1:
                        V_(lambda e, psS=psS, g=g, sc=sc: e.scalar_tensor_tensor(
                            out=Snew[:, 4 * g:4 * g + 4, :], in0=St0[:, 4 * g:4 * g + 4, :], scalar=sc,
                            in1=psS[:, :].rearrange("p (b v) -> p b v", b=4), op0=ALU.mult, op1=ALU.add),
                            [psSc, C("St0")], [C("St0")])
                        yield
                        continue
                    V_(lambda e, psS=psS, g=g: e.tensor_tensor(out=Snew[:, 4 * g:4 * g + 4, :], in0=psS[:, :].rearrange("p (b v) -> p b v", b=4),
                                                               in1=St0[:, 4 * g:4 * g + 4, :], op=ALU.add),
                       [psSc, C("St0")], [C("St0")])
                    if isinstance(sc, float):
                        V_(lambda e, g=g, sc=sc: e.tensor_scalar(out=Snew[:, 4 * g:4 * g + 4, :], in0=Snew[:, 4 * g:4 * g + 4, :], scalar1=sc, scalar2=None,
                                                                 op0=ALU.mult), [C("St0")], [C("St0")])
                    else:
                        V_(lambda e, g=g, sc=sc: e.tensor_tensor(out=Snew[:, 4 * g:4 * g + 4, :], in0=Snew[:, 4 * g:4 * g + 4, :], in1=sc, op=ALU.mult),
                           [C("St0")] + cells("eq", [4]), [C("St0")])
            yield

        def carve_core_tmps(nh):
            Cst = carve([3, 128], F32)
            Ctmp = carve([128], F32)
            Cbf = carve([17, 128], BF16)
            Pm = carve([4, nh, 128], BF16)
            Pms = carve([nh, 64], BF16)
            Km = carve([4, 128], BF16)
            return (Cst, Ctmp, Cbf, Pm, Pms, Km)

        def head_norm_tile(j, t0, tn, Z, onesm, gcol, nt, dst_fn, dst_cells, extra_mul=None):
            Zb, Zq, sd = nt
            r = j % 2
            A_(lambda e: e.activation(out=Zb[:, r, 0:tn], in_=Z[:, t0:t0 + tn], func=AF.Copy), [C("Z", j)], [C("Zb", r)])
            psM, psMc = PS()
            M_(lambda e: e.matmul(psM[:, 0:tn], lhsT=onesm, rhs=Zb[:, r, 0:tn], start=True, stop=True), [C("Zb", r), "cstb"], [psMc])
            V_(lambda e: e.tensor_tensor(out=Z[:, t0:t0 + tn], in0=Z[:, t0:t0 + tn], in1=psM[:, 0:tn], op=ALU.subtract),
               [C("Z", j), psMc], [C("Z", j)])
            A_(lambda e: e.activation(out=Zq[:, r, 0:tn], in_=Z[:, t0:t0 + tn], func=AF.Square), [C("Z", j)], [C("Zq", r)])
            psV, psVc = PS()
            M_(lambda e: e.matmul(psV[:, 0:tn], lhsT=onesm, rhs=Zq[:, r, 0:tn], start=True, stop=True), [C("Zq", r), "cstb"], [psVc])
            A_(lambda e: e.activation(out=sd[:, r, 0:tn], in_=psV[:, 0:tn], func=AF.Sqrt, bias=EPS, scale=1.0), [psVc], [C("sd", r)])
            V_(lambda e: e.reciprocal(out=sd[:, r, 0:tn], in_=sd[:, r, 0:tn]), [C("sd", r)], [C("sd", r)])
            if extra_mul is None:
                V_(lambda e: e.scalar_tensor_tensor(out=dst_fn(t0, tn), in0=Z[:, t0:t0 + tn], scalar=gcol, in1=sd[:, r, 0:tn],
                                                    op0=ALU.mult, op1=ALU.mult), [C("Z", j), C("sd", r), "pv"], dst_cells(j))
            else:
                em, emc = extra_mul
                V_(lambda e: e.scalar_tensor_tensor(out=Z[:, t0:t0 + tn], in0=Z[:, t0:t0 + tn], scalar=gcol, in1=sd[:, r, 0:tn],
                                                    op0=ALU.mult, op1=ALU.mult), [C("Z", j), C("sd", r), "pv"], [C("Z", j)])
                V_(lambda e: e.tensor_tensor(out=dst_fn(t0, tn), in0=Z[:, t0:t0 + tn], in1=em, op=ALU.mult),
                   [C("Z", j)] + emc, dst_cells(j))

        def b_phase(l, wunit, kind, Z, onesm, gcol, kc_out, tail=None, bufs=None):
            if bufs is None:
                nsl = 5
                yT = carve([N], BF16)
                Zb = carve([5, 512], BF16)
                Zq = carve([5, 512], BF16)
                sd = carve([5, 512], F32)
                sgt = carve([5, 512], F32)
            else:
                nsl, yT, Zb, Zq, sd, sgt = bufs
            sl = load_wh(win_d[l, wunit])
            slo = load_wh(wout_d[l, kc_out])
            wv = wslh[:, sl, 0:1024].rearrange("p (k c) -> p k c", k=KC)
            gfun = AF.Sigmoid if kind == "ml" else AF.Silu

            def stages(j):
                t0, tn = TT[j]
                sj = j % nsl
                zc = C("Z", j)
                box = {}

                def s0():
                    ps, psc = PS()
                    for kc in range(KC):
                        M_(lambda e, kc=kc: e.matmul(ps[:, 0:tn], lhsT=wv[:, kc, :], rhs=hT[:, kc, t0:t0 + tn],
                                                     start=(kc == 0), stop=(kc == KC - 1)), [("wh", sl), ("hT", kc, j)], [psc])
                    A_(lambda e: e.activation(out=sgt[:, sj, 0:tn], in_=ps[:, 0:tn], func=gfun), [psc], [C("sgt", sj)])

                def s1():
                    if kind == "ml":
                        V_(lambda e: e.tensor_tensor(out=Z[:, t0:t0 + tn], in0=Z[:, t0:t0 + tn], in1=sgt[:, sj, 0:tn], op=ALU.mult),
                           [zc, C("sgt", sj)], [zc])
                    A_(lambda e: e.activation(out=Zb[:, sj, 0:tn], in_=Z[:, t0:t0 + tn], func=AF.Copy), [zc], [C("Zb", sj)])

                def s2():
                    box["m"] = PS(hold=True)
                    psM, psMc = box["m"]
                    M_(lambda e: e.matmul(psM[:, 0:tn], lhsT=onesm, rhs=Zb[:, sj, 0:tn], start=True, stop=True), [C("Zb", sj), "cstb"], [psMc])

                def s3():
                    psM, psMc = box["m"]
                    V_(lambda e: e.tensor_tensor(out=Z[:, t0:t0 + tn], in0=Z[:, t0:t0 + tn], in1=psM[:, 0:tn], op=ALU.subtract), [zc, psMc], [zc])
                    A_(lambda e: e.activation(out=Zq[:, sj, 0:tn], in_=Z[:, t0:t0 + tn], func=AF.Square), [zc], [C("Zq", sj)])
                    PS_release(psMc)

                def s4():
                    box["v"] = PS(hold=True)
                    psV, psVc = box["v"]
                    M_(lambda e: e.matmul(psV[:, 0:tn], lhsT=onesm, rhs=Zq[:, sj, 0:tn], start=True, stop=True), [C("Zq", sj), "cstb"], [psVc])

                def s5():
                    psV, psVc = box["v"]
                    A_(lambda e: e.activation(out=sd[:, sj, 0:tn], in_=psV[:, 0:tn], func=AF.Ln, bias=EPS, scale=1.0), [psVc], [C("sd", sj)])
                    A_(lambda e: e.activation(out=sd[:, sj, 0:tn], in_=sd[:, sj, 0:tn], func=AF.Exp, scale=-0.5), [C("sd", sj)], [C("sd", sj)])
                    PS_release(psVc)

                def s6():
                    if kind == "ml":
                        V_(lambda e: e.scalar_tensor_tensor(out=yT[:, t0:t0 + tn], in0=Z[:, t0:t0 + tn], scalar=gcol, in1=sd[:, sj, 0:tn],
                                                            op0=ALU.mult, op1=ALU.mult), [zc, C("sd", sj), "pv"], [C("yT", j)])
                    else:
                        V_(lambda e: e.scalar_tensor_tensor(out=Z[:, t0:t0 + tn], in0=Z[:, t0:t0 + tn], scalar=gcol, in1=sd[:, sj, 0:tn],
                                                            op0=ALU.mult, op1=ALU.mult), [zc, C("sd", sj), "pv"], [zc])
                        V_(lambda e: e.tensor_tensor(out=yT[:, t0:t0 + tn], in0=Z[:, t0:t0 + tn], in1=sgt[:, sj, 0:tn], op=ALU.mult),
                           [zc, C("sgt", sj)], [C("yT", j)])

                def s7():
                    for dc in range(KC):
                        ps, psc = PS()
                        M_(lambda e, ps=ps, dc=dc: e.matmul(ps[:, 0:tn], lhsT=wslh[:, slo, dc * 128:(dc + 1) * 128], rhs=yT[:, t0:t0 + tn],
                                                            start=True, stop=True), [("wh", slo), C("yT", j)], [psc])
                        if dc in POOL_DCS:
                            A_(lambda e, ps=ps: e.activation(out=sd[:, sj, 0:tn], in_=ps[:, 0:tn], func=AF.Copy), [psc], [C("sd", sj)])
                            S.op("pool", lambda e, dc=dc: e.tensor_tensor(out=xT[:, dc, t0:t0 + tn], in0=xT[:, dc, t0:t0 + tn],
                                                                         in1=sd[:, sj, 0:tn], op=ALU.add),
                                 [C("sd", sj), ("xT", dc, j)], [("xT", dc, j)])
                            continue
                        V_(lambda e, ps=ps, dc=dc: e.scalar_tensor_tensor(
                            out=xT[:, dc, t0:t0 + tn], in0=ps[:, 0:tn], scalar=1.0, in1=xT[:, dc, t0:t0 + tn], op0=ALU.mult, op1=ALU.add),
                            [psc, ("xT", dc, j)], [("xT", dc, j)])
                    if tail is not None and not hasattr(tail, "stages"):
                        tail(j)

                lst = [s0, s1, s2, s3, s4, s5, s6, s7]
                if tail is not None and hasattr(tail, "stages"):
                    lst += tail.stages(j)
                return lst

            sts = [stages(j) for j in range(5)]
            start = []
            for j in range(5):
                s_ = j if j == 0 else start[j - 1] + 1
                if j >= nsl:
                    s_ = max(s_, start[j - nsl] + 8)
                start.append(s_)
            nst = len(sts[0])
            for step in range(start[-1] + nst):
                for j in range(5):
                    s = step - start[j]
                    if 0 <= s < nst:
                        sts[j][s]()
                yield

        def mlstm(l, pr):
            new_phase("mlA%d_l%d" % (pr, l))
            Z = carve([N], F32)
            G1 = carve([N], F32)
            G2 = carve([N], F32)
            G3 = carve([N], F32)
            G4 = Z
            qT = carve([2, N], BF16)
            kT = carve([N], BF16)
            qTf = carve([2, 64], F32)
            Vx = carve([17, 2, 128], BF16)
            Ktok = carve([17, 128], BF16)
            St0 = carve([16, 128], F32)
            Snew = St0
            tmps = carve_core_tmps(2)
            dn = carve([2, 512], F32)
            M0 = carve([16], F32)
            m0T = carve([16], F32)
            msave = carve([32], F32)
            nst = carve([128], F32)
            n0st = carve([128], F32)
            u0 = 4 + 6 * pr
            hs = slice(2 * pr, 2 * pr + 2)
            S.op("sp", lambda e: e.dma_start(out=St0[:, :, 0:64], in_=stC_d[l, :, hs].rearrange("b h k v -> (h k) b v")),
                 writes=[C("St0")], dma_key="St0")
            S.op("sp", lambda e: e.dma_start(out=n0st[0:16, :], in_=stn_d[l, :, 128 * pr:128 * pr + 128]), writes=[C("n0st")], dma_key="n0st")
            S.op("sp", lambda e: e.dma_start(out=m0T[0:4, :], in_=stm_d[l].rearrange("b h -> h b"), allow_slow_non_contiguous=True),
                 writes=[C("m0T")], dma_key="m0T")
            ps, psc = PS()
            M_(lambda e, ps=ps: e.transpose(out=ps[:, 0:16], in_=n0st[0:16, :], identity=ident[0:16, 0:16]), [C("n0st"), "ident"], [psc])
            M_(lambda e, ps=ps: e.matmul(ps[:, 64:80], lhsT=sel4[0:4, pr, :], rhs=m0T[0:4, :], start=True, stop=True),
               [C("m0T"), "cstf"], [psc])
            V_(lambda e, ps=ps: e.tensor_copy(out=St0[:, :, 64:128], in_=ps[:, 0:16].unsqueeze(2).broadcast_to([128, 16, 64])),
               [psc], [C("St0")])
            V_(lambda e, ps=ps: e.tensor_copy(out=M0[:, :], in_=ps[:, 64:80]), [psc], [C("M0")])
            V_(lambda e: e.memset(Vx[:, :, :, 64:128], 1.0), [], cells("Vx"))
            V_(lambda e: e.memset(qT[:, :, :], 0.0), [], cells("qT"))
            V_(lambda e: e.memset(qTf[:, :, :], 0.0), [], [C("qTf")])
            sl = load_wh(win_d[l, u0 + 0])

            def c_li(j, t0, tn, ps, psc):
                A_(lambda e: e.activation(out=G1[:, t0:t0 + tn], in_=ps[:, 0:tn], func=AF.Identity, bias=pvcol(("bifi", l), pr), scale=1.0),
                   [psc, "pv"], [C("G1", j)])
            proj_fm(sl, c_li)
            sl = load_wh(win_d[l, u0 + 1])

            def c_lf(j, t0, tn, ps, psc):
                A_(lambda e: e.activation(out=G2[:, t0:t0 + tn], in_=ps[:, 0:tn], func=AF.Sigmoid, bias=pvcol(("biff", l), pr), scale=1.0),
                   [psc, "pv"], [C("G2", j)])
            proj_fm(sl, c_lf)
            sl = load_wh(win_d[l, u0 + 4])

            def e_v(j, nb, rows, ps, psc):
                A_(lambda e: e.activation(out=Vx[0:rows, 4 * j:4 * j + nb, :, 0:64],
                                          in_=ps[0:rows, 0:nb * 128].rearrange("p (b h v) -> p b h v", b=nb, h=2), func=AF.Copy),
                   [psc], [C("Vx", j)])
            proj_tok(sl, e_v)
            A_(lambda e: e.activation(out=G2[:, :], in_=G2[:, :], func=AF.Ln), cells("G2"), cells("G2"))
            V_(lambda e: e.tensor_tensor_scan(out=G3[:, 0:NPROMPT], data0=G2[:, 0:NPROMPT], data1=G1[:, 0:NPROMPT], initial=0.0,
                                              op0=ALU.add, op1=ALU.max), cells("G2", range(4)) + cells("G1", range(4)), cells("G3", range(4)))
            for t in range(4):
                prev = M0[:, :] if t == 0 else sv(G3)[:, :, t - 1]
                V_(lambda e, t=t, prev=prev: e.tensor_tensor(out=sv(G4)[:, :, t], in0=sv(G2)[:, :, t], in1=prev, op=ALU.add),
                   cells("G2", [4]) + cells("G3", [4]) + [C("M0")], cells("Z", [4]))
                V_(lambda e, t=t: e.tensor_tensor(out=sv(G3)[:, :, t], in0=sv(G4)[:, :, t], in1=sv(G1)[:, :, t], op=ALU.max),
                   cells("Z", [4]) + cells("G1", [4]), cells("G3", [4]))
            V_(lambda e: e.tensor_copy(out=msave[:, 0:1], in_=G3[:, NPROMPT - 1:NPROMPT]), cells("G3", [3]), [C("msave")])
            V_(lambda e: e.tensor_copy(out=msave[:, 1:17], in_=sv(G3)[:, :, 3]), cells("G3", [4]), [C("msave")])
            for c in range(16):
                t0 = c * 128
                V_(lambda e, t0=t0: e.tensor_tensor_scan(out=G4[:, t0:t0 + 128], data0=onesf[:, :], data1=G2[:, t0:t0 + 128], initial=0.0,
                                                         op0=ALU.mult, op1=ALU.add), cells("G2", [c // 4]) + ["cstf"], cells("Z", [c // 4]))
                if c > 0:
                    V_(lambda e, t0=t0: e.tensor_scalar(out=G4[:, t0:t0 + 128], in0=G4[:, t0:t0 + 128], scalar1=G3[:, t0 - 1:t0], scalar2=None,
                                                        op0=ALU.add), cells("Z", [c // 4]) + cells("G3", [(c * 128 - 1) // 512]), cells("Z", [c // 4]))
            V_(lambda e: e.tensor_copy(out=sv(G4)[:, :, 0], in_=sv(G2)[:, :, 0]), cells("G2", [4]), cells("Z", [4]))
            for t in range(1, 4):
                V_(lambda e, t=t: e.tensor_tensor(out=sv(G4)[:, :, t], in0=sv(G4)[:, :, t - 1], in1=sv(G2)[:, :, t], op=ALU.add),
                   cells("Z", [4]) + cells("G2", [4]), cells("Z", [4]))
            V_(lambda e: e.tensor_tensor(out=sv(G4), in0=sv(G4), in1=M0[:, :].unsqueeze(2).broadcast_to([128, 16, 4]), op=ALU.add),
               cells("Z", [4]) + [C("M0")], cells("Z", [4]))
            V_(lambda e: e.tensor_tensor(out=G2[:, :], in0=G4[:, :], in1=G3[:, :], op=ALU.subtract), cells("Z") + cells("G3"), cells("G2"))
            V_(lambda e: e.tensor_tensor(out=G1[:, :], in0=G4[:, :], in1=G1[:, :], op=ALU.subtract), cells("Z") + cells("G1"), cells("G1"))
            A_(lambda e: e.activation(out=G2[:, :], in_=G2[:, :], func=AF.Exp), cells("G2"), cells("G2") + cells("eq"))
            A_(lambda e: e.activation(out=G1[:, :], in_=G1[:, :], func=AF.Exp, scale=-1.0), cells("G1"), cells("G1"))
            A_(lambda e: e.activation(out=G3[:, :], in_=G3[:, :], func=AF.Exp, scale=-1.0), cells("G3") + [C("msave")], cells("G3"))
            sl = load_wh(win_d[l, u0 + 2])

            def c_q(j, t0, tn, ps, psc):
                for hh in range(2):
                    pq = slice(64 * hh, 64 * hh + 64)
                    V_(lambda e, hh=hh, pq=pq: e.tensor_tensor(out=qT[pq, hh, t0:t0 + tn], in0=ps[pq, 0:tn], in1=G2[pq, t0:t0 + tn], op=ALU.mult),
                       [psc, C("G2", j)], [C("qT", j)])
                    if j == 4:
                        V_(lambda e, hh=hh, pq=pq: e.tensor_tensor(out=qTf[pq, hh, :], in0=ps[pq, 0:64], in1=G2[pq, t0:t0 + 64], op=ALU.mult),
                           [psc, C("G2", j)], [C("qTf")])
            proj_fm(sl, c_q)
            sl = load_wh(win_d[l, u0 + 3])

            def c_k(j, t0, tn, ps, psc):
                V_(lambda e: e.scalar_tensor_tensor(out=kT[:, t0:t0 + tn], in0=ps[:, 0:tn], scalar=0.125, in1=G1[:, t0:t0 + tn],
                                                    op0=ALU.mult, op1=ALU.mult), [psc, C("G1", j)], [C("kT", j)])
            proj_fm(sl, c_k)
            to_tok(kT, "kT", Ktok, "Ktok")
            eqs = lambda c: G2[:, c * 128 + 127:c * 128 + 128]
            eqs_s = lambda g: G2[:, NPROMPT + 16 * g:NPROMPT + 16 * g + 16].rearrange("p (b t) -> p b t", t=4)[:, :, 3:4].broadcast_to([128, 4, 128])

            def post(j, psO, psOc, psD, psDc):
                t0, tn = TT[j]
                r = j % 2
                A_(lambda e: e.activation(out=dn[:, r, 0:tn], in_=psD[:, 0:tn], func=AF.Abs), [psDc], [C("dn", r)])
                V_(lambda e: e.tensor_tensor(out=dn[:, r, 0:tn], in0=dn[:, r, 0:tn], in1=G3[:, t0:t0 + tn], op=ALU.max),
                   [C("dn", r), C("G3", j)], [C("dn", r)])
                A_(lambda e: e.activation(out=dn[:, r, 0:tn], in_=dn[:, r, 0:tn], func=AF.Ln), [C("dn", r)], [C("dn", r)])
                A_(lambda e: e.activation(out=dn[:, r, 0:tn], in_=dn[:, r, 0:tn], func=AF.Exp, scale=-1.0), [C("dn", r)], [C("dn", r)])
                V_(lambda e: e.tensor_tensor(out=Z[:, t0:t0 + tn], in0=psO[:, 0:tn], in1=dn[:, r, 0:tn], op=ALU.mult),
                   [psOc, C("dn", r)], [C("Z", j)])
            if "m_core" not in SKIP:
                for _ in unit_core(2, lambda h: qT[:, h, :], kT, lambda h: qTf[:, h, :], Vx, Ktok, St0, Snew, eqs, eqs_s, True, post, tmps):
                    pass
            if "m_out" not in SKIP:
                Cst = tmps[0]
                fin = 15 % 3
                S.op("sp", lambda e: e.dma_start(out=oC_d[l, 0, hs].rearrange("h k v -> (h k) v"), in_=Cst[:, fin, 0:64]),
                     reads=[C("Cst", fin)], dma_key="o_C")
                S.op("sp", lambda e: e.dma_start(out=oC_d[l, 1:17, hs].rearrange("b h k v -> (h k) b v"), in_=Snew[:, :, 0:64]),
                     reads=[C("St0")], dma_key="o_C")
                ps, psc = PS()
                M_(lambda e, ps=ps: e.transpose(out=ps[0:32, 0:128], in_=Cst[:, fin, 64:96], identity=ident[:]), [C("Cst", fin), "ident"], [psc])
                M_(lambda e, ps=ps: e.transpose(out=ps[0:16, 128:256], in_=Snew[:, :, 64], identity=ident[:]),
                   [C("St0")] + ["ident"], [psc])
                A_(lambda e, ps=ps: e.activation(out=nst[0:1, :], in_=ps[0:1, 0:128], func=AF.Copy), [psc], [C("nst")])
                A_(lambda e, ps=ps: e.activation(out=n0st[0:16, :], in_=ps[0:16, 128:256], func=AF.Copy), [psc], [C("n0st")])
                S.op("sp", lambda e: e.dma_start(out=on_d[l, 0:1, 128 * pr:128 * pr + 128], in_=nst[0:1, :]), reads=[C("nst")], dma_key="o_n")
                S.op("sp", lambda e: e.dma_start(out=on_d[l, 1:17, 128 * pr:128 * pr + 128], in_=n0st[0:16, :]), reads=[C("n0st")], dma_key="o_n")
                for h in range(2):
                    S.op("sp", lambda e, h=h: e.dma_start(out=om_d[l, :, 2 * pr + h:2 * pr + h + 1].rearrange("s o -> o s"),
                                                          in_=msave[64 * h:64 * h + 1, 0:17], allow_slow_non_contiguous=True),
                         reads=[C("msave")], dma_key="o_m")
            new_phase("mlB%d_l%d" % (pr, l))
            Z = carve([N], F32)
            for _ in b_phase(l, u0 + 5, "ml", Z, bones64[:, :], pvcol(("gmn", l), pr), pr):
                pass

        def ret_A(l, h, A):
            (Z, cs, dec, T1, T2, qT, kT, qTf, Vx, Ktok, St0, tmps) = A
            Snew = St0
            u0 = 16 + 6 * h
            gam = 1.0 - 2.0 ** (-5.0 - h)
            S.op("sp", lambda e: e.dma_start(out=dec[:, :, :], in_=dect_d[h]), writes=[C("dec")], dma_key="dec")
            S.op("sp", lambda e: e.dma_start(out=St0[:, :, :], in_=stS_d[l, :, h].rearrange("b k v -> k b v")), writes=[C("St0")], dma_key="St0r")
            for (which, dstT, dname, uq) in ((0, qT, "qT", 0), (1, kT, "kT", 2)):
                sla = load_wh(win_d[l, u0 + uq])
                slb = load_wh(win_d[l, u0 + uq + 1])
                wva = wslh[:, sla, 0:1024].rearrange("p (k c) -> p k c", k=KC)
                wvb = wslh[:, slb, 0:1024].rearrange("p (k c) -> p k c", k=KC)
                for j, (t0, tn) in enumerate(TT):
                    pa, pac = PS()
                    pb, pbc = PS()
                    for (pp, ppc, wv_, sl_) in ((pa, pac, wva, sla), (pb, pbc, wvb, slb)):
                        for kc in range(KC):
                            M_(lambda e, pp=pp, kc=kc, t0=t0, tn=tn, wv_=wv_: e.matmul(
                                pp[:, 0:tn], lhsT=wv_[:, kc, :], rhs=hT[:, kc, t0:t0 + tn], start=(kc == 0), stop=(kc == KC - 1)),
                                [("wh", sl_), ("hT", kc, j)], [ppc])
                    V_(lambda e, pa=pa, t0=t0, tn=tn: e.tensor_tensor(out=T1[:, 0:tn], in0=pa[:, 0:tn], in1=cs[:, 0, t0:t0 + tn], op=ALU.mult),
                       [pac, C("cs")], [C("T1")])
                    V_(lambda e, pb=pb, t0=t0, tn=tn: e.tensor_tensor(out=T2[:, 0:tn], in0=pb[:, 0:tn], in1=cs[:, 1, t0:t0 + tn], op=ALU.mult),
                       [pbc, C("cs")], [C("T2")])
                    V_(lambda e, tn=tn: e.tensor_tensor(out=T1[:, 0:tn], in0=T1[:, 0:tn], in1=T2[:, 0:tn], op=ALU.add),
                       [C("T1"), C("T2")], [C("T1")])
                    if j < 4:
                        V_(lambda e, t0=t0, dstT=dstT, which=which: e.tensor_tensor(
                            out=dstT[:, t0:t0 + 512].rearrange("p (c t) -> p c t", c=4), in0=T1[:, :].rearrange("p (c t) -> p c t", c=4),
                            in1=dec[:, which, 0:128].unsqueeze(1).broadcast_to([128, 4, 128]), op=ALU.mult),
                            [C("T1"), C("dec")], [C(dname, j)])
                    else:
                        V_(lambda e, t0=t0, dstT=dstT, which=which: e.tensor_tensor(
                            out=dstT[:, t0:t0 + 64], in0=T1[:, 0:64], in1=dec[:, which, 128:192], op=ALU.mult),
                            [C("T1"), C("dec")], [C(dname, j)])
                        if which == 0:
                            V_(lambda e: e.tensor_tensor(out=qTf[:, :], in0=T1[:, 0:64], in1=dec[:, 0, 128:192], op=ALU.mult),
                               [C("T1"), C("dec")], [C("qTf")])
                    yield
            sl = load_wh(win_d[l, u0 + 4])

            def e_v(j, nb, rows, ps, psc):
                A_(lambda e: e.activation(out=Vx[0:rows, 4 * j:4 * j + nb, 0, :],
                                          in_=ps[0:rows, 0:nb * 128].rearrange("p (b v) -> p b v", b=nb), func=AF.Copy),
                   [psc], [C("Vx", j)])
            proj_tok(sl, e_v)
            yield
            to_tok(kT, "kT", Ktok, "Ktok", scale_fn=lambda j: float(gam ** 128) if j < 4 else float(gam ** 4))
            yield

            def post(j, psO, psOc, psD, psDc):
                t0, tn = TT[j]
                A_(lambda e: e.activation(out=Z[:, t0:t0 + tn], in_=psO[:, 0:tn], func=AF.Copy), [psOc], [C("Z", j)])
            yield from unit_core(1, lambda h_: qT, kT, lambda h_: qTf, Vx, Ktok, St0, Snew,
                                 lambda c: float(gam ** 128), lambda g: float(gam ** 4), False, post, tmps, chain1=True)
            Cst = tmps[0]
            fin = 15 % 3
            S.op("sp", lambda e: e.dma_start(out=oS_d[l, 0, h], in_=Cst[:, fin, :]), reads=[C("Cst", fin)], dma_key="o_S")
            S.op("sp", lambda e: e.dma_start(out=oS_d[l, 1:17, h].rearrange("b k v -> k b v"), in_=Snew[:, :, :]),
                 reads=[C("St0")], dma_key="o_S")
            yield

        def retention_section(l, tail=None):
            new_phase("ret_l%d" % l)
            for n_ in ("Z", "yT", "Zb", "Zq", "sd", "sgt"):
                arena_of[n_] = "scrB"
            for n_ in ("cs", "dec", "T1", "T2", "qT", "kT", "qTf", "Vx", "Ktok", "St0", "Cst", "Ctmp", "Cbf", "Pm", "Pms", "Km"):
                arena_of[n_] = "scrA"
            Z = carve([N], F32, "scrB")
            yT = carve([N], BF16, "scrB")
            Bb = (3, yT, carve([3, 512], BF16, "scrB"), carve([3, 512], BF16, "scrB"), carve([3, 512], F32, "scrB"),
                  carve([3, 512], BF16, "scrB"))
            cs = carve([2, N], F32, "scrA")
            dec = carve([2, 192], F32, "scrA")
            T1 = carve([512], F32, "scrA")
            T2 = carve([512], F32, "scrA")
            qT = carve([N], BF16, "scrA")
            kT = carve([N], BF16, "scrA")
            qTf = carve([64], F32, "scrA")
            Vx = carve([17, 1, 128], BF16, "scrA")
            Ktok = carve([17, 128], BF16, "scrA")
            St0 = carve([16, 128], F32, "scrA")
            tmps = (carve([3, 128], F32, "scrA"), carve([128], F32, "scrA"), carve([16, 128], BF16, "scrA"),
                    carve([4, 1, 128], BF16, "scrA"), carve([1, 64], BF16, "scrA"), carve([4, 128], BF16, "scrA"))
            A = (Z, cs, dec, T1, T2, qT, kT, qTf, Vx, Ktok, St0, tmps)
            S.op("sp", lambda e: e.dma_start(out=cs[:, :, :], in_=rope_d), writes=[C("cs")], dma_key="cs")

            def drive(gens):
                live = [[g, nm, pl] for g, nm, pl in gens if g is not None]
                while live:
                    for it in list(live):
                        S.scope = it[1]
                        state["pspool"] = it[2] if len(live) > 1 else None
                        try:
                            next(it[0])
                        except StopIteration:
                            live.remove(it)

            drive([(ret_A(l, 0, A), "rtA0_l%d" % l, None)])
            for h in range(4):
                gb = b_phase(l, 16 + 6 * h + 5, "rt", Z, ones128[:, :], pvcol(("grn", l), h), 2 + h,
                             tail=(tail if h == 3 else None), bufs=Bb)
                ga = ret_A(l, h + 1, A) if h < 3 else None
                drive([(gb, "rtB%d_l%d" % (h, l), "B"), (ga, "rtA%d_l%d" % (h + 1, l), "A")])
            state["pspool"] = None
            arena_of.clear()

        def mixer(l, next_norm=None):
            S.alias_barrier(["w", "wh"])
            if "lru" in PARTS:
                lru(l)
            if "mlstm" in PARTS:
                for pr in range(2):
                    mlstm(l, pr)
            if "ret" in PARTS:
                retention_section(l, tail=next_norm)

        load_x(tail=norm_hT_tile(("nf1", 0)))
        for l in range(DEPTH):
            ffn(l, 1, next_norm=norm_hT_tile(("nmx", l)))
            mixer(l, next_norm=norm_hT_tile(("nf2", l)))
            ffn(l, 2, next_norm=(norm_hT_tile(("nf1", l + 1)) if l + 1 < DEPTH else "final"))
        S.emit(final_wait_keys=[("ost", 0), ("ost", 1), "o_h", "o_cv", "o_C", "o_n", "o_m", "o_S"])
    return nc


_NC_CACHE = {}


def kernel(**inputs):
    inp = {k: np.asarray(v) for k, v in inputs.items()}
    shared, per_core = _host_prep(inp)
    if "nc" not in _NC_CACHE:
        _NC_CACHE["nc"] = build()
    nc = _NC_CACHE["nc"]
    in_maps = [dict(shared, **pc) for pc in per_core]
    res = run_bass_kernel_spmd(nc, in_maps, core_ids=list(range(8)))
    r = res.results
    y = np.stack([r[c]["y"] for c in range(8)])
    y_prompt = np.ascontiguousarray(y[:, :NPROMPT, :])
    y_sample = np.ascontiguousarray(y[:, NPROMPT:, :].reshape(128, 4, D))

    def split(name, tail):
        a = np.stack([r[c][name] for c in range(8)])
        p = np.ascontiguousarray(a[:, :, 0].transpose((1, 0) + tuple(range(2, a.ndim - 1)))).reshape((DEPTH, 8) + tail)
        s = a[:, :, 1:17].transpose((1, 0, 2) + tuple(range(3, a.ndim)))
        s = np.ascontiguousarray(s).reshape((DEPTH, 128) + tail)
        return p, s

    pC, sC = split("oC", (4, 64, 64))
    pn, sn = split("on", (4, 64))
    pm, sm = split("om", (4,))
    pS, sS = split("oS", (4, 128, 128))
    ph, sh = split("oh", (256,))
    pcv, scv = split("ocv", (3, 256))
    return (y_prompt, y_sample, pC, pn, pm, pS, ph, pcv, sC, sn, sm, sS, sh, scv)
```

```python
import contextlib
import numpy as np
import concourse.bass as bass
import concourse.mybir as mybir
from concourse.bass_utils import run_bass_kernel_spmd

F32 = mybir.dt.float32
BF16 = mybir.dt.bfloat16
AF = mybir.ActivationFunctionType
ALU = mybir.AluOpType

D = 1024
KC = 8
DFF = 2816
NFC = 22
NPROMPT = 2048
NSAMP = 64
N = NPROMPT + NSAMP
TT = [(0, 512), (512, 512), (1024, 512), (1536, 512), (2048, 64)]
DEPTH = 2
EPS = 1e-6
NSLOT = 4
FG = 4
SCR = 84544
PARTS = ("lru", "mlstm", "ret")
SKIP = set()
USE_SCOPES = False
POOL_DCS = (3, 7)

COMPUTE = ("pe", "act", "dve", "pool")


class Op:
    __slots__ = ("id", "eng", "fn", "deps", "dma_key", "sig", "sigval", "dma_waits", "scope")

    def __init__(self, id, eng, fn, dma_key):
        self.id = id
        self.eng = eng
        self.fn = fn
        self.deps = []
        self.dma_key = dma_key
        self.sig = False
        self.sigval = 0
        self.dma_waits = {}


def _arena(c):
    return c[0] if isinstance(c, tuple) else c


class Sched:
    def __init__(self, nc):
        self.nc = nc
        self.ops = []
        self.lw = {}
        self.rd = {}
        self.keycnt = {}
        self.extra = {}
        self.arena_touch = {}
        self.scope = None
        self.use_scopes = False

    def op(self, eng, fn, reads=(), writes=(), dma_key=None):
        o = Op(len(self.ops), eng, fn, dma_key)
        o.scope = self.scope
        is_dma = dma_key is not None
        deps = {}

        def add(d, raw):
            if d is None:
                return
            d_dma = d.dma_key is not None
            if (not is_dma) and (not d_dma) and d.eng == eng:
                if not raw or eng == "pe":
                    return
            deps[d.id] = d

        for c in reads:
            add(self.lw.get(c), True)
        for c in writes:
            add(self.lw.get(c), False)
            for r in self.rd.get(c, {}).values():
                add(r, False)
        k = ("dma", o.id) if is_dma else eng
        for c in list(reads) + list(writes):
            ar = _arena(c)
            if ar in self.extra:
                for d in self.extra[ar]:
                    add(d, True)
            if ar in self.arena_touch:
                self.arena_touch[ar][k] = o
        for c in reads:
            self.rd.setdefault(c, {})[k] = o
        for c in writes:
            self.lw[c] = o
            self.rd[c] = {}
        for d in deps.values():
            if d.dma_key is not None:
                o.dma_waits[d.dma_key] = self.keycnt[d.dma_key]
            else:
                o.deps.append(d)
                d.sig = True
        if is_dma:
            self.keycnt[dma_key] = self.keycnt.get(dma_key, 0) + 16
        self.ops.append(o)
        return o

    def track_arena(self, arena):
        self.arena_touch.setdefault(arena, {})

    def alias_barrier(self, arenas):
        if isinstance(arenas, str):
            arenas = [arenas]
        ops = {}
        for ar in arenas:
            for k, o in self.arena_touch.get(ar, {}).items():
                ops[(ar, k)] = o
        lst = list(ops.values())
        for ar in arenas:
            self.extra[ar] = lst
            self.arena_touch[ar] = {}
        for c in [c for c in self.lw if _arena(c) in arenas]:
            del self.lw[c]
        for c in [c for c in self.rd if _arena(c) in arenas]:
            del self.rd[c]

    def emit(self, final_wait_keys=()):
        nc = self.nc
        engs = {"pe": [], "act": [], "dve": [], "pool": [], "sp": []}
        for o in self.ops:
            engs[o.eng].append(o)
        for e, lst in engs.items():
            n = 0
            for o in lst:
                if o.dma_key is None and o.sig:
                    n += 1
                    o.sigval = n
            assert n < 60000, (e, n)
        keys = sorted(self.keycnt.keys(), key=str)
        with contextlib.ExitStack() as st:
            esem = {e: st.enter_context(nc.semaphore("s_" + e)) for e in COMPUTE}
            ksem = {k: st.enter_context(nc.semaphore("k%d" % i)) for i, k in enumerate(keys)}
            block = st.enter_context(nc.Block())

            def run(e, engobj):
                waited = {}
                for o in engs[e]:
                    need = {}
                    for d in o.deps:
                        s = esem[d.eng]
                        need[s] = max(need.get(s, 0), d.sigval)
                    for k, v in o.dma_waits.items():
                        s = ksem[k]
                        need[s] = max(need.get(s, 0), v)
                    for s, v in need.items():
                        if waited.get(s, 0) < v:
                            engobj.wait_ge(s, v)
                            waited[s] = v
                    if self.use_scopes and o.scope is not None:
                        with nc.named_scope(o.scope):
                            ins = o.fn(engobj)
                    else:
                        ins = o.fn(engobj)
                    if o.dma_key is not None:
                        ins.then_inc(ksem[o.dma_key], 16)
                    elif o.sig:
                        ins.then_inc(esem[e], 1)
                if e == "sp":
                    for k in final_wait_keys:
                        if self.keycnt.get(k, 0) > 0:
                            engobj.wait_ge(ksem[k], self.keycnt[k])

            @block.tensor
            def _(eng):
                run("pe", eng)

            @block.scalar
            def _(eng):
                run("act", eng)

            @block.vector
            def _(eng):
                run("dve", eng)

            @block.gpsimd
            def _(eng):
                run("pool", eng)

            @block.sync
            def _(eng):
                run("sp", eng)


def _pv_layout():
    cols = {}
    n = 0

    def add(name, w):
        nonlocal n
        cols[name] = (n, w)
        n += w

    for l in range(DEPTH):
        add(("nf1", l), 8)
        add(("nmx", l), 8)
        add(("nf2", l), 8)
        for nm, w in (("gmn", 2), ("grn", 4), ("wconv", 8), ("bconv", 2), ("ba", 2), ("bi", 2), ("lam", 2), ("bifi", 2), ("biff", 2)):
            add((nm, l), w)
    add("nfin", 8)
    return cols, n


PVC, NPV = _pv_layout()


def _fm(v):
    return np.ascontiguousarray(v.reshape(-1, 128).T)


def _host_prep(inp):
    f = np.float32
    pv = np.zeros((128, NPV), f)

    def put(name, arr):
        o, w = PVC[name]
        pv[:, o:o + w] = arr

    for l in range(DEPTH):
        put(("nf1", l), _fm(inp["norm_ffn1"][l]))
        put(("nmx", l), _fm(inp["norm_mix"][l]))
        put(("nf2", l), _fm(inp["norm_ffn2"][l]))
        put(("gmn", l), _fm(inp["g_mlstm_norm"][l]))
        put(("grn", l), _fm(inp["g_ret_norm"][l]))
        wc = np.zeros((128, 8), f)
        for jj in range(4):
            wc[:, jj * 2:jj * 2 + 2] = _fm(inp["w_conv"][l, jj])
        put(("wconv", l), wc)
        put(("bconv", l), _fm(inp["b_conv"][l]))
        put(("ba", l), _fm(inp["b_lru_a"][l]))
        put(("bi", l), _fm(inp["b_lru_i"][l]))
        put(("lam", l), _fm(inp["lru_lambda"][l]))
        hidx = np.arange(128) // 64
        bi_ = np.zeros((128, 2), f)
        bf_ = np.zeros((128, 2), f)
        for pr in range(2):
            bi_[:, pr] = inp["b_mlstm_if"][l][2 * pr + hidx]
            bf_[:, pr] = inp["b_mlstm_if"][l][4 + 2 * pr + hidx]
        put(("bifi", l), bi_)
        put(("biff", l), bf_)
    put("nfin", _fm(inp["norm_final"]))

    cuts = dict(mq=0, mk=256, mv=512, mo=768, mi=1024, mf=1028, rq=1032, rk=1544, rv=2056, rg=2568, lx=3080, ly=3336)
    ar = np.arange(128)
    sw = np.concatenate([np.arange(64, 128), np.arange(0, 64)])
    units = []
    for c in range(2):
        units += [cuts["lx"] + c * 128 + ar, cuts["ly"] + c * 128 + ar]
    for pr in range(2):
        rep = np.repeat(np.array([2 * pr, 2 * pr + 1]), 64)
        units += [cuts["mi"] + rep, cuts["mf"] + rep, cuts["mq"] + pr * 128 + ar, cuts["mk"] + pr * 128 + ar,
                  cuts["mv"] + pr * 128 + ar, cuts["mo"] + pr * 128 + ar]
    for h in range(4):
        units += [cuts["rq"] + h * 128 + ar, cuts["rq"] + h * 128 + sw, cuts["rk"] + h * 128 + ar, cuts["rk"] + h * 128 + sw,
                  cuts["rv"] + h * 128 + ar, cuts["rg"] + h * 128 + ar]
    cols = np.stack(units)
    win = inp["w_in"][:, :, cols]
    win = win.reshape(DEPTH, KC, 128, 40, 128).transpose(0, 3, 2, 1, 4)
    win = np.ascontiguousarray(win.reshape(DEPTH, 40, 128, 1024))
    wout = np.ascontiguousarray(inp["w_out"].reshape(DEPTH, 8, 128, 1024))
    wlru = np.zeros((DEPTH, 128, 512), f)
    for l in range(DEPTH):
        for gi, W in ((0, inp["w_lru_a"]), (1, inp["w_lru_i"])):
            for c in range(2):
                o = (gi * 2 + c) * 128
                wlru[l, 0:64, o:o + 64] = W[l, 2 * c]
                wlru[l, 64:128, o + 64:o + 128] = W[l, 2 * c + 1]
    cstb = np.zeros((128, 640), f)
    s_ = np.arange(128)
    mc = (s_[:, None] <= s_[None, :]).astype(f)
    cstb[:, 0:128] = mc
    cstb[:, 128:256] = mc
    s6 = np.arange(64)
    mb = ((s6[:, None] // 4 == s6[None, :] // 4) & (s6[:, None] <= s6[None, :])).astype(f)
    cstb[0:64, 256:320] = mb
    cstb[0:64, 320:384] = mb
    cstb[:, 384:512] = (s_[:, None] // 64 == s_[None, :] // 64).astype(f) / 64.0
    cstb[:, 512:640] = 1.0 / 128.0
    cstf = np.zeros((128, 400), f)
    cstf[0:64, 0:16] = (s6[:, None] // 4 == np.arange(16)[None, :]).astype(f)
    for hh in range(4):
        for pr in range(2):
            cstf[hh, 16 + pr * 128:16 + (pr + 1) * 128] = (hh == 2 * pr + s_ // 64).astype(f)
    cstf[:, 272:400] = 1.0
    pos = np.concatenate([np.arange(NPROMPT, dtype=f), np.tile(np.float32(16384.0) + np.arange(4, dtype=f), 16)])
    inv = (np.float32(10000.0) ** (-(np.arange(0, 128, 2, dtype=f) / np.float32(128.0)))).astype(f)
    ang = (pos[:, None] * inv[None, :]).astype(f).astype(np.float64)
    rope = np.zeros((128, 2, N), f)
    rope[0:64, 0] = np.cos(ang).T
    rope[64:128, 0] = np.cos(ang).T
    rope[0:64, 1] = -np.sin(ang).T
    rope[64:128, 1] = np.sin(ang).T
    dect = np.zeros((4, 128, 2, 192), f)
    tl = np.arange(128, dtype=np.float64)
    ts = np.tile(np.arange(4, dtype=np.float64), 16)
    for h in range(4):
        lg = np.log1p(-(2.0 ** (-5.0 - h)))
        dect[h, :, 0, 0:128] = np.exp((tl + 1.0) * lg)
        dect[h, :, 0, 128:192] = np.exp((ts + 1.0) * lg)
        dect[h, :, 1, 0:128] = np.exp(-(tl + 1.0) * lg) * 128.0 ** -0.5
        dect[h, :, 1, 128:192] = np.exp(-(ts + 1.0) * lg) * 128.0 ** -0.5

    def gu_tiles(w):
        L = w.shape[0]
        g = w[:, :, :DFF].reshape(L, KC, 128, NFC, 128)
        u = w[:, :, DFF:].reshape(L, KC, 128, NFC, 128)
        t = np.stack([g, u], axis=4)
        t = t.transpose(0, 3, 2, 1, 4, 5)
        return np.ascontiguousarray(t.reshape(L, NFC, 128, KC * 256))

    def dn_tiles(w):
        L = w.shape[0]
        t = w.reshape(L, NFC // 2, 2, 128, D).transpose(0, 1, 3, 2, 4)
        return np.ascontiguousarray(t.reshape(L, NFC // 2, 128, 2 * D))

    shared = {
        "pv": pv,
        "wgu1": gu_tiles(inp["w_ffn1_gu"]), "wdn1": dn_tiles(inp["w_ffn1_down"]),
        "wgu2": gu_tiles(inp["w_ffn2_gu"]), "wdn2": dn_tiles(inp["w_ffn2_down"]),
        "ident": np.eye(128, dtype=f),
        "win": win, "wout": wout, "wlru": wlru, "cstb": cstb, "cstf": cstf, "rope": rope, "dect": dect,
    }
    per_core = []
    for c in range(8):
        xin = np.concatenate([inp["x_prompt"][c], inp["x_sample"][16 * c:16 * c + 16].reshape(NSAMP, D)], axis=0)
        sl = slice(16 * c, 16 * c + 16)
        per_core.append({
            "xin": np.ascontiguousarray(xin),
            "stC": np.ascontiguousarray(inp["state_mlstm_C"][:, sl]),
            "stn": np.ascontiguousarray(inp["state_mlstm_n"][:, sl].reshape(DEPTH, 16, 256)),
            "stm": np.ascontiguousarray(inp["state_mlstm_m"][:, sl]),
            "stS": np.ascontiguousarray(inp["state_ret_S"][:, sl]),
            "sth": np.ascontiguousarray(inp["state_lru_h"][:, sl]),
            "stcv": np.ascontiguousarray(inp["state_lru_conv"][:, sl].reshape(DEPTH, 48, 256)),
        })
    return shared, per_core


def build():
    nc = bass.Bass("TRN2", target_bir_lowering=False)
    dt_in = lambda name, shape: nc.dram_tensor(name, list(shape), F32, kind="ExternalInput").ap()
    dt_out = lambda name, shape: nc.dram_tensor(name, list(shape), F32, kind="ExternalOutput").ap()
    xin = dt_in("xin", [N, D])
    pv_d = dt_in("pv", [128, NPV])
    ident_d = dt_in("ident", [128, 128])
    wgu_d = {1: dt_in("wgu1", [DEPTH, NFC, 128, 2048]), 2: dt_in("wgu2", [DEPTH, NFC, 128, 2048])}
    wdn_d = {1: dt_in("wdn1", [DEPTH, NFC // 2, 128, 2048]), 2: dt_in("wdn2", [DEPTH, NFC // 2, 128, 2048])}
    y_d = dt_out("y", [N, D])
    win_d = dt_in("win", [DEPTH, 40, 128, 1024])
    wout_d = dt_in("wout", [DEPTH, 8, 128, 1024])
    wlru_d = dt_in("wlru", [DEPTH, 128, 512])
    cstb_d = dt_in("cstb", [128, 640])
    cstf_d = dt_in("cstf", [128, 400])
    rope_d = dt_in("rope", [128, 2, N])
    dect_d = dt_in("dect", [4, 128, 2, 192])
    stC_d = dt_in("stC", [DEPTH, 16, 4, 64, 64])
    stn_d = dt_in("stn", [DEPTH, 16, 256])
    stm_d = dt_in("stm", [DEPTH, 16, 4])
    stS_d = dt_in("stS", [DEPTH, 16, 4, 128, 128])
    sth_d = dt_in("sth", [DEPTH, 16, 256])
    stcv_d = dt_in("stcv", [DEPTH, 48, 256])
    oC_d = dt_out("oC", [DEPTH, 17, 4, 64, 64])
    on_d = dt_out("on", [DEPTH, 17, 256])
    om_d = dt_out("om", [DEPTH, 17, 4])
    oS_d = dt_out("oS", [DEPTH, 17, 4, 128, 128])
    oh_d = dt_out("oh", [DEPTH, 17, 256])
    ocv_d = dt_out("ocv", [DEPTH, 17, 3, 256])

    with contextlib.ExitStack() as st:
        SB = lambda name, shape, dt: st.enter_context(nc.sbuf_tensor(name, list(shape), dt))
        xT = SB("xT", [128, KC, N], F32)
        hT = SB("hT", [128, KC, N], BF16)
        wsl = SB("wsl", [128, NSLOT, 2048], BF16)
        pv = SB("pv_s", [128, NPV], F32)
        ident = SB("ident_s", [128, 128], F32)
        identb = SB("identb", [128, 128], BF16)
        onesb = SB("onesb", [128, 128], BF16)
        nsq = SB("nsq", [128, 2, 512], BF16)
        nrstd = SB("nrstd", [128, 2, 512], F32)
        cstb = SB("cstb_s", [128, 640], BF16)
        cstf = SB("cstf_s", [128, 400], F32)
        maskc = cstb[:, 0:256].rearrange("p (h t) -> p h t", h=2)
        maskb = cstb[:, 256:384].rearrange("p (h t) -> p h t", h=2)
        bones64 = cstb[:, 384:512]
        ones128 = cstb[:, 512:640]
        selb = cstf[:, 0:16]
        sel4 = cstf[:, 16:272].rearrange("p (r q) -> p r q", r=2)
        onesf = cstf[:, 272:400]
        scr = SB("scr", [128, SCR], mybir.dt.uint8)
        psb = [st.enter_context(nc.psum_tensor("ps%d" % i, [128, 512], F32)) for i in range(8)]

        S = Sched(nc)
        S.use_scopes = USE_SCOPES
        S.track_arena("scr")
        S.track_arena("scrA")
        S.track_arena("scrB")
        S.track_arena("w")
        S.track_arena("wh")
        state = {"ps": 0, "w": 0, "wh": 0, "scr_off": 0}

        held = set()
        pools = {None: list(range(8)), "A": [4, 5, 6, 7], "B": [0, 1, 2, 3]}
        pstate = {None: 0, "A": 0, "B": 0}

        def PS(hold=False):
            pool = state.get("pspool")
            banks = pools[pool]
            for _ in range(2 * len(banks)):
                b = banks[pstate[pool] % len(banks)]
                pstate[pool] += 1
                if b not in held:
                    break
            else:
                raise RuntimeError("all PSUM banks of pool held")
            if hold:
                held.add(b)
            return psb[b], ("ps", b)

        def PS_release(cell):
            held.discard(cell[1])

        arena_of = {}
        ARENAS = ["scr", "scrA", "scrB"]
        RB = 28032
        regions = {"scr": (0, SCR), "scrB": (0, RB), "scrA": (RB, SCR)}
        state["off"] = {"scr": 0, "scrA": 0, "scrB": 0}

        def C(name, *idx):
            return (arena_of.get(name, "scr"), name) + idx

        def carve(shape, dt, region="scr"):
            esz = 4 if dt == F32 else 2
            n = 1
            for s_ in shape:
                n *= s_
            base, lim = regions[region]
            off = state["off"][region]
            off = (off + 63) // 64 * 64
            state["off"][region] = off + n * esz
            assert base + off + n * esz <= lim, (region, off + n * esz, lim - base)
            v = scr[:, base + off:base + off + n * esz].bitcast(dt)
            if len(shape) == 2:
                v = v.rearrange("p (a b) -> p a b", a=shape[0])
            elif len(shape) == 3:
                v = v.rearrange("p (a b c) -> p a b c", a=shape[0], b=shape[1])
            return v

        def new_phase(name=None):
            if name is not None:
                S.scope = name
            S.alias_barrier(ARENAS)
            state["off"] = {"scr": 0, "scrA": 0, "scrB": 0}

        wslh = wsl[:, :, :].rearrange("p s (h f) -> p (s h) f", h=2)

        def load_wh(src):
            idx = state["wh"] % (2 * NSLOT)
            state["wh"] += 1
            F = src.shape[-1]
            assert F <= 1024
            S.op("pool", lambda e, idx=idx, src=src, F=F: e.dma_start(out=wslh[:, idx, 0:F], in_=src),
                 writes=[("wh", idx)], dma_key=("wh", idx))
            return idx

        def load_w(src):
            sl = state["w"] % NSLOT
            state["w"] += 1
            F = src.shape[-1]
            S.op("pool", lambda e, sl=sl, src=src, F=F: e.dma_start(out=wsl[:, sl, 0:F], in_=src),
                 writes=[("w", sl)], dma_key=("w", sl))
            return sl

        S.op("sp", lambda e: e.dma_start(out=pv[:], in_=pv_d), writes=["pv"], dma_key="pv")
        S.op("sp", lambda e: e.dma_start(out=ident[:], in_=ident_d), writes=["ident"], dma_key="ident")
        S.op("dve", lambda e: e.tensor_copy(out=identb[:], in_=ident[:]), reads=["ident"], writes=["identb"])
        S.op("dve", lambda e: e.memset(onesb[:], 1.0 / 1024.0), writes=["onesb"])
        S.op("pool", lambda e: e.dma_start(out=cstb[:], in_=cstb_d), writes=["cstb"], dma_key="cstb")
        S.op("sp", lambda e: e.dma_start(out=cstf[:], in_=cstf_d), writes=["cstf"], dma_key="cstf")

        def pvcol(name, j=0, w=1):
            o, _ = PVC[name]
            return pv[:, o + j:o + j + w]

        def load_x(tail=None):
            new_phase("load_x")
            pend_lx = []
            stg = carve([2, 4, D], F32)
            for j, (t0, tn) in enumerate(TT):
                sb = j % 2
                nb = (tn + 127) // 128
                rows = min(128, tn)
                S.op("sp", lambda e, sb=sb, t0=t0, nb=nb, rows=rows: e.dma_start(
                    out=stg[0:rows, sb, 0:nb, :], in_=xin[t0:t0 + nb * rows, :].rearrange("(b p) d -> p b d", p=rows)),
                    writes=[C("stg", sb)], dma_key=("stg", sb))
                for kc in range(KC):
                    ps, psc = PS()
                    for b in range(nb):
                        S.op("pe", lambda e, ps=ps, sb=sb, b=b, kc=kc, rows=rows: e.transpose(
                            out=ps[:, b * rows:(b + 1) * rows], in_=stg[0:rows, sb, b, kc * 128:(kc + 1) * 128],
                            identity=ident[0:rows, 0:rows]),
                            reads=[C("stg", sb), "ident"], writes=[psc])
                    eng = "act" if kc % 2 == 0 else "dve"
                    if eng == "act":
                        S.op("act", lambda e, ps=ps, kc=kc, t0=t0, tn=tn: e.activation(
                            out=xT[:, kc, t0:t0 + tn], in_=ps[:, 0:tn], func=AF.Copy),
                            reads=[psc], writes=[("xT", kc, j)])
                    else:
                        S.op("dve", lambda e, ps=ps, kc=kc, t0=t0, tn=tn: e.tensor_copy(
                            out=xT[:, kc, t0:t0 + tn], in_=ps[:, 0:tn]),
                            reads=[psc], writes=[("xT", kc, j)])
                if tail is not None:
                    if hasattr(tail, "stages"):
                        st_ = tail.stages(j)
                        st_[0]()
                        for f_ in pend_lx:
                            f_()
                        del pend_lx[:]
                        pend_lx.extend(st_[1:])
                    else:
                        tail(j)
            for f_ in pend_lx:
                f_()

        def rms_stages(j, gname, dst_fn, dst_cells):
            sq, rstd = nsq, nrstd
            t0, tn = TT[j]
            r = j % 2
            box = {}

            def st_a():
                ps, psc = PS(hold=True)
                box["p"] = (ps, psc)
                for kc in range(KC):
                    s = kc % 2
                    S.op("act", lambda e, s=s, kc=kc: e.activation(out=sq[:, s, 0:tn], in_=xT[:, kc, t0:t0 + tn], func=AF.Square),
                         reads=[("xT", kc, j)], writes=[("nsq", s)])
                    S.op("pe", lambda e, s=s, kc=kc, ps=ps: e.matmul(ps[:, 0:tn], lhsT=onesb[:], rhs=sq[:, s, 0:tn],
                                                                      start=(kc == 0), stop=(kc == KC - 1)),
                         reads=[("nsq", s), "onesb"], writes=[psc])

            def st_b():
                ps, psc = box["p"]
                S.op("act", lambda e: e.activation(out=rstd[:, r, 0:tn], in_=ps[:, 0:tn], func=AF.Ln, bias=EPS, scale=1.0),
                     reads=[psc], writes=[("nrstd", r)])
                S.op("act", lambda e: e.activation(out=rstd[:, r, 0:tn], in_=rstd[:, r, 0:tn], func=AF.Exp, scale=-0.5),
                     reads=[("nrstd", r)], writes=[("nrstd", r)])
                PS_release(psc)

            def st_c():
                for kc in range(KC):
                    S.op("dve", lambda e, kc=kc: e.scalar_tensor_tensor(
                        out=dst_fn(kc), in0=xT[:, kc, t0:t0 + tn], scalar=pvcol(gname, kc), in1=rstd[:, r, 0:tn],
                        op0=ALU.mult, op1=ALU.mult),
                        reads=[("xT", kc, j), ("nrstd", r), "pv"], writes=[dst_cells(kc)])

            return [st_a, st_b, st_c]

        def rms_tile(j, gname, dst_fn, dst_cells):
            for st_ in rms_stages(j, gname, dst_fn, dst_cells):
                st_()

        def norm_hT_tile(gname):
            def f(j):
                t0, tn = TT[j]
                rms_tile(j, gname, lambda kc: hT[:, kc, t0:t0 + tn], lambda kc: ("hT", kc, j))

            def stages(j):
                t0, tn = TT[j]
                return rms_stages(j, gname, lambda kc: hT[:, kc, t0:t0 + tn], lambda kc: ("hT", kc, j))
            f.stages = stages
            return f

        def ffn(l, which, next_norm=None):
            new_phase("ffn%d_l%d" % (which, l))
            S.alias_barrier(["w", "wh"])
            sg = carve([3, 512], F32)
            actb = carve([2 * FG, N], BF16)
            if next_norm == "final":
                next_norm = make_final_tile()
            groups = [list(range(g, min(g + FG, NFC))) for g in range(0, NFC, FG)]
            sgc = [0]

            def gu(fc, aslot, fl):
                sl = load_w(wgu_d[which][l, fc])
                wv = wsl[:, sl, :].rearrange("p (k c) -> p k c", k=KC)
                for j, (t0, tn) in enumerate(TT):
                    pg, pgc = PS()
                    pu, puc = PS()
                    for (pp, ppc, co) in ((pg, pgc, 0), (pu, puc, 128)):
                        for kc in range(KC):
                            S.op("pe", lambda e, pp=pp, kc=kc, co=co, wv=wv, t0=t0, tn=tn: e.matmul(
                                pp[:, 0:tn], lhsT=wv[:, kc, co:co + 128], rhs=hT[:, kc, t0:t0 + tn],
                                start=(kc == 0), stop=(kc == KC - 1)),
                                reads=[("w", sl), ("hT", kc, j)], writes=[ppc])
                    s = sgc[0] % 3
                    sgc[0] += 1
                    S.op("act", lambda e, pg=pg, s=s, tn=tn: e.activation(out=sg[:, s, 0:tn], in_=pg[:, 0:tn], func=AF.Silu),
                         reads=[pgc], writes=[C("sg", s)])
                    S.op("dve", lambda e, pu=pu, s=s, a=aslot * FG + fl, t0=t0, tn=tn: e.tensor_tensor(
                        out=actb[:, a, t0:t0 + tn], in0=pu[:, 0:tn], in1=sg[:, s, 0:tn], op=ALU.mult),
                        reads=[puc, C("sg", s)], writes=[C("act", aslot * FG + fl, j)])

            pend_norm = []

            def down(items, tail=None):
                chunks = []
                for (g, aslot) in items:
                    fcs = groups[g]
                    sls = [load_w(wdn_d[which][l, fc // 2]) for fc in fcs[::2]]
                    for i, fc in enumerate(fcs):
                        chunks.append((sls[i // 2], i % 2, aslot * FG + i))
                for j, (t0, tn) in enumerate(TT):
                    for dc in range(KC):
                        ps, psc = PS()
                        for n_, (sl, half, asl) in enumerate(chunks):
                            wv = wsl[:, sl, :].rearrange("p (f d) -> p f d", f=2)
                            S.op("pe", lambda e, ps=ps, wv=wv, half=half, asl=asl, dc=dc, t0=t0, tn=tn, n_=n_: e.matmul(
                                ps[:, 0:tn], lhsT=wv[:, half, dc * 128:(dc + 1) * 128], rhs=actb[:, asl, t0:t0 + tn],
                                start=(n_ == 0), stop=(n_ == len(chunks) - 1)),
                                reads=[("w", sl), C("act", asl, j)], writes=[psc])
                        S.op("dve", lambda e, ps=ps, dc=dc, t0=t0, tn=tn: e.scalar_tensor_tensor(
                            out=xT[:, dc, t0:t0 + tn], in0=ps[:, 0:tn], scalar=0.5, in1=xT[:, dc, t0:t0 + tn],
                            op0=ALU.mult, op1=ALU.add),
                            reads=[psc, ("xT", dc, j)], writes=[("xT", dc, j)])
                    if tail is not None:
                        if hasattr(tail, "stages"):
                            st_ = tail.stages(j)
                            st_[0]()
                            for f_ in pend_norm:
                                f_()
                            del pend_norm[:]
                            pend_norm.extend(st_[1:])
                        else:
                            tail(j)
                for f_ in pend_norm:
                    f_()
                del pend_norm[:]

            pending = None
            ng = len(groups)
            for g, fcs in enumerate(groups):
                aslot = g % 2
                for fl, fc in enumerate(fcs):
                    gu(fc, aslot, fl)
                    if fl == 0 and pending is not None and g < ng - 1:
                        down([pending])
                        pending = None
                if g < ng - 2:
                    pending = (g, aslot)
            down([(ng - 2, (ng - 2) % 2), (ng - 1, (ng - 1) % 2)], tail=next_norm)

        def make_final_tile():
            hf = carve([2, KC, 512], F32)
            ost = carve([2, D], F32)
            cnt = [0]

            def final_tile(j):
                t0, tn = TT[j]
                hs = j % 2
                rms_tile(j, "nfin", lambda kc: hf[:, hs, kc, 0:tn], lambda kc: C("hf", hs, kc))
                nb = (tn + 127) // 128
                rows = min(128, tn)
                for b in range(nb):
                    os_ = cnt[0] % 2
                    cnt[0] += 1
                    for half in range(2):
                        ps, psc = PS()
                        for q in range(4):
                            kc = half * 4 + q
                            S.op("pe", lambda e, ps=ps, q=q, kc=kc, b=b: e.transpose(
                                out=ps[0:rows, q * 128:(q + 1) * 128], in_=hf[:, hs, kc, b * rows:(b + 1) * rows], identity=ident[:]),
                                reads=[C("hf", hs, kc), "ident"], writes=[psc])
                        if half == 0:
                            S.op("act", lambda e, ps=ps, os_=os_: e.activation(
                                out=ost[0:rows, os_, 0:512], in_=ps[0:rows, :], func=AF.Copy),
                                reads=[psc], writes=[C("ost", os_, 0)])
                        else:
                            S.op("dve", lambda e, ps=ps, os_=os_: e.tensor_copy(
                                out=ost[0:rows, os_, 512:1024], in_=ps[0:rows, :]),
                                reads=[psc], writes=[C("ost", os_, 1)])
                    S.op("sp", lambda e, os_=os_, r0=t0 + b * rows: e.dma_start(
                        out=y_d[r0:r0 + rows, :], in_=ost[0:rows, os_, :]),
                        reads=[C("ost", os_, 0), C("ost", os_, 1)], dma_key=("ost", os_))
            return final_tile

        def cells(name, js=None):
            return [C(name, j) for j in (range(5) if js is None else js)]

        def A_(fn, reads, writes):
            S.op("act", fn, reads, writes)

        def V_(fn, reads, writes):
            S.op("dve", fn, reads, writes)

        def M_(fn, reads, writes):
            S.op("pe", fn, reads, writes)

        def proj_fm(sl, consume):
            wv = wslh[:, sl, 0:1024].rearrange("p (k c) -> p k c", k=KC)
            for j, (t0, tn) in enumerate(TT):
                ps, psc = PS()
                for kc in range(KC):
                    M_(lambda e, ps=ps, kc=kc, t0=t0, tn=tn, wv=wv: e.matmul(
                        ps[:, 0:tn], lhsT=wv[:, kc, :], rhs=hT[:, kc, t0:t0 + tn], start=(kc == 0), stop=(kc == KC - 1)),
                        [("wh", sl), ("hT", kc, j)], [psc])
                consume(j, t0, tn, ps, psc)

        def proj_tok(sl, evac):
            wv = wslh[:, sl, 0:1024].rearrange("p (k c) -> p k c", k=KC)
            for j, (t0, tn) in enumerate(TT):
                ps, psc = PS()
                rows = min(128, tn)
                nb = (tn + 127) // 128
                for b in range(nb):
                    for kc in range(KC):
                        M_(lambda e, ps=ps, kc=kc, b=b, rows=rows, t0=t0, wv=wv: e.matmul(
                            ps[0:rows, b * 128:(b + 1) * 128], lhsT=hT[:, kc, t0 + b * rows:t0 + (b + 1) * rows], rhs=wv[:, kc, :],
                            start=(kc == 0), stop=(kc == KC - 1)),
                            [("wh", sl), ("hT", kc, j)], [psc])
                evac(j, nb, rows, ps, psc)

        def to_tok(srcT, srcname, dst, dstname, scale_fn=None):
            for j, (t0, tn) in enumerate(TT):
                ps, psc = PS()
                psb_ = ps.bitcast(BF16)
                rows = min(128, tn)
                nb = (tn + 127) // 128
                for b in range(nb):
                    M_(lambda e, psb_=psb_, b=b, rows=rows, t0=t0: e.transpose(
                        out=psb_[0:rows, b * 128:(b + 1) * 128], in_=srcT[:, t0 + b * rows:t0 + (b + 1) * rows], identity=identb[:]),
                        [("scr", srcname, j), "identb"], [psc])
                A_(lambda e, psb_=psb_, j=j, nb=nb, rows=rows, sc_=(1.0 if scale_fn is None else scale_fn(j)): e.activation(
                    out=dst[0:rows, 4 * j:4 * j + nb, :], in_=psb_[0:rows, 0:nb * 128].rearrange("p (b c) -> p b c", b=nb), func=AF.Copy, scale=sc_),
                    [psc], [("scr", dstname, j)])

        def out_proj(l, kcs, src, src_cells, tail=None):
            sls = [load_wh(wout_d[l, kc]) for kc in kcs]
            for j, (t0, tn) in enumerate(TT):
                for dc in range(KC):
                    ps, psc = PS()
                    for i in range(len(kcs)):
                        M_(lambda e, ps=ps, i=i, dc=dc, t0=t0, tn=tn: e.matmul(
                            ps[:, 0:tn], lhsT=wslh[:, sls[i], dc * 128:(dc + 1) * 128], rhs=src(i, t0, tn),
                            start=(i == 0), stop=(i == len(kcs) - 1)),
                            [("wh", sls[i])] + src_cells(i, j), [psc])
                    V_(lambda e, ps=ps, dc=dc, t0=t0, tn=tn: e.scalar_tensor_tensor(
                        out=xT[:, dc, t0:t0 + tn], in0=ps[:, 0:tn], scalar=1.0, in1=xT[:, dc, t0:t0 + tn],
                        op0=ALU.mult, op1=ALU.add),
                        [psc, ("xT", dc, j)], [("xT", dc, j)])
                if tail is not None:
                    tail(j)

        def sv(buf):
            return buf[:, NPROMPT:N].rearrange("p (b t) -> p b t", t=4)

        def lru(l):
            for c in range(2):
                lru_chunk(l, c)

        def lru_chunk(l, c):
            if True:
                new_phase("lru%d_l%d" % (c, l))
                yl = carve([2, N], BF16)
                ohst = carve([256], F32)
                ocst = carve([256], F32)
                XPp = carve([NPROMPT + 3], F32)
                XPs = carve([16, 7], F32)
                xc = carve([N], F32)
                xcb = carve([N], BF16)
                Ab = carve([N], F32)
                Ub = carve([N], F32)
                Gb = carve([N], F32)
                HS = carve([N], F32)
                T1 = carve([N], F32)
                cvst = carve([256], F32)
                hst = carve([256], F32)
                h0T = carve([16], F32)
                hl = carve([32], F32)
                cv = carve([17, 3], F32)
                spn = carve([2], F32)
                u0 = 2 * c
                S.op("sp", lambda e: e.dma_start(out=cvst[0:48, :], in_=stcv_d[l]), writes=[C("cvst")], dma_key="cvst")
                S.op("sp", lambda e: e.dma_start(out=hst[0:16, :], in_=sth_d[l]), writes=[C("hst")], dma_key="hst")
                ps, psc = PS()
                M_(lambda e, ps=ps: e.transpose(out=ps[:, 0:48], in_=cvst[0:48, c * 128:(c + 1) * 128], identity=ident[0:48, 0:48]),
                   [C("cvst"), "ident"], [psc])
                M_(lambda e, ps=ps: e.transpose(out=ps[:, 64:80], in_=hst[0:16, c * 128:(c + 1) * 128], identity=ident[0:16, 0:16]),
                   [C("hst"), "ident"], [psc])
                A_(lambda e, ps=ps: e.activation(out=XPs[:, :, 0:3], in_=ps[:, 0:48].rearrange("p (b j) -> p b j", j=3), func=AF.Copy),
                   [psc], [C("XPs")])
                A_(lambda e, ps=ps: e.activation(out=h0T[:, :], in_=ps[:, 64:80], func=AF.Copy), [psc], [C("h0T")])
                V_(lambda e: e.memset(XPp[:, 0:3], 0.0), [], [C("XPp0")])
                A_(lambda e: e.activation(out=spn[:, :], in_=pvcol(("lam", l), 0, 2), func=AF.Exp, scale=-1.0), ["pv"], [C("spn")])
                A_(lambda e: e.activation(out=spn[:, :], in_=spn[:, :], func=AF.Ln, bias=1.0), [C("spn")], [C("spn")])
                V_(lambda e: e.tensor_scalar(out=spn[:, :], in0=spn[:, :], scalar1=-8.0, scalar2=None, op0=ALU.mult),
                   [C("spn")], [C("spn")])
                sl = load_wh(win_d[l, u0 + 0])

                def c_lx(j, t0, tn, ps, psc):
                    if j < 4:
                        A_(lambda e: e.activation(out=XPp[:, 3 + t0:3 + t0 + tn], in_=ps[:, 0:tn], func=AF.Copy), [psc], [C("XP", j)])
                    else:
                        A_(lambda e: e.activation(out=XPs[:, :, 3:7], in_=ps[:, 0:64].rearrange("p (b t) -> p b t", t=4), func=AF.Copy),
                           [psc], [C("XP", 4)])
                proj_fm(sl, c_lx)
                sl = load_wh(win_d[l, u0 + 1])

                def c_ly(j, t0, tn, ps, psc):
                    A_(lambda e: e.activation(out=Gb[:, t0:t0 + tn], in_=ps[:, 0:tn], func=AF.Gelu_apprx_tanh), [psc], [C("G", j)])
                proj_fm(sl, c_ly)
                wc = lambda jj: pvcol(("wconv", l), jj * 2 + c)
                xpr = [C("XP", j) for j in range(4)] + [C("XPp0")]
                V_(lambda e: e.tensor_scalar(out=xc[:, 0:NPROMPT], in0=XPp[:, 0:NPROMPT], scalar1=wc(0), scalar2=pvcol(("bconv", l), c),
                                             op0=ALU.mult, op1=ALU.add), xpr + ["pv"], cells("xc", range(4)))
                for jj in range(1, 4):
                    V_(lambda e, jj=jj: e.scalar_tensor_tensor(out=xc[:, 0:NPROMPT], in0=XPp[:, jj:jj + NPROMPT], scalar=wc(jj),
                                                               in1=xc[:, 0:NPROMPT], op0=ALU.mult, op1=ALU.add),
                       xpr + ["pv"] + cells("xc", range(4)), cells("xc", range(4)))
                V_(lambda e: e.tensor_scalar(out=sv(xc), in0=XPs[:, :, 0:4], scalar1=wc(0), scalar2=pvcol(("bconv", l), c),
                                             op0=ALU.mult, op1=ALU.add), [C("XP", 4), C("XPs"), "pv"], cells("xc", [4]))
                for jj in range(1, 4):
                    V_(lambda e, jj=jj: e.scalar_tensor_tensor(out=sv(xc), in0=XPs[:, :, jj:jj + 4], scalar=wc(jj), in1=sv(xc),
                                                               op0=ALU.mult, op1=ALU.add),
                       [C("XP", 4), C("XPs"), "pv"] + cells("xc", [4]), cells("xc", [4]))
                A_(lambda e: e.activation(out=xcb[:, :], in_=xc[:, :], func=AF.Copy), cells("xc"), cells("xcb"))
                slw = load_wh(wlru_d[l])
                for (gi, dst, dname, bname) in ((0, Ab, "A", "ba"), (1, Ub, "U", "bi")):
                    for j, (t0, tn) in enumerate(TT):
                        ps, psc = PS()
                        M_(lambda e, ps=ps, gi=gi, t0=t0, tn=tn: e.matmul(
                            ps[:, 0:tn], lhsT=wslh[:, slw, (gi * 2 + c) * 128:(gi * 2 + c + 1) * 128], rhs=xcb[:, t0:t0 + tn],
                            start=True, stop=True), [("wh", slw), C("xcb", j)], [psc])
                        A_(lambda e, ps=ps, dst=dst, bname=bname, t0=t0, tn=tn: e.activation(
                            out=dst[:, t0:t0 + tn], in_=ps[:, 0:tn], func=AF.Sigmoid, bias=pvcol((bname, l), c), scale=1.0),
                            [psc, "pv"], [("scr", dname, j)])
                A_(lambda e: e.activation(out=Ab[:, :], in_=Ab[:, :], func=AF.Exp, scale=spn[:, c:c + 1]),
                   cells("A") + [C("spn")], cells("A"))
                V_(lambda e: e.scalar_tensor_tensor(out=T1[:, :], in0=Ab[:, :], scalar=1.0, in1=Ab[:, :], op0=ALU.min, op1=ALU.mult),
                   cells("A"), cells("T1"))
                A_(lambda e: e.activation(out=T1[:, :], in_=T1[:, :], func=AF.Sqrt, bias=1.0, scale=-1.0), cells("T1"), cells("T1"))
                V_(lambda e: e.tensor_tensor(out=Ub[:, :], in0=Ub[:, :], in1=T1[:, :], op=ALU.mult), cells("U") + cells("T1"), cells("U"))
                V_(lambda e: e.tensor_tensor(out=Ub[:, :], in0=Ub[:, :], in1=xc[:, :], op=ALU.mult), cells("U") + cells("xc"), cells("U"))
                V_(lambda e: e.tensor_tensor_scan(out=HS[:, 0:NPROMPT], data0=Ab[:, 0:NPROMPT], data1=Ub[:, 0:NPROMPT], initial=0.0,
                                                  op0=ALU.mult, op1=ALU.add),
                   cells("A", range(4)) + cells("U", range(4)), cells("HS", range(4)))
                for b in range(16):
                    o = NPROMPT + 4 * b
                    V_(lambda e, b=b, o=o: e.tensor_tensor_scan(out=HS[:, o:o + 4], data0=Ab[:, o:o + 4], data1=Ub[:, o:o + 4],
                                                                initial=h0T[:, b:b + 1], op0=ALU.mult, op1=ALU.add),
                       cells("A", [4]) + cells("U", [4]) + [C("h0T")], cells("HS", [4]))
                V_(lambda e: e.tensor_tensor(out=yl[:, c, :], in0=HS[:, :], in1=Gb[:, :], op=ALU.mult),
                   cells("HS") + cells("G"), [C("yl", c, j) for j in range(5)])
                A_(lambda e: e.activation(out=hl[:, 0:1], in_=HS[:, NPROMPT - 1:NPROMPT], func=AF.Copy), cells("HS", [3]), [C("hl")])
                A_(lambda e: e.activation(out=hl[:, 1:17], in_=sv(HS)[:, :, 3], func=AF.Copy), cells("HS", [4]), [C("hl")])
                A_(lambda e: e.activation(out=cv[:, 0, :], in_=XPp[:, NPROMPT:NPROMPT + 3], func=AF.Copy), [C("XP", 3)], [C("cv")])
                A_(lambda e: e.activation(out=cv[:, 1:17, :], in_=XPs[:, :, 4:7], func=AF.Copy), [C("XP", 4)], [C("cv")])
                ps, psc = PS()
                M_(lambda e, ps=ps: e.transpose(out=ps[0:17, 0:128], in_=hl[:, 0:17], identity=ident[:]), [C("hl"), "ident"], [psc])
                M_(lambda e, ps=ps: e.transpose(out=ps[0:51, 128:256], in_=cv[:, :, :].rearrange("p s j -> p (s j)"), identity=ident[:]),
                   [C("cv"), "ident"], [psc])
                A_(lambda e, ps=ps: e.activation(out=ohst[0:17, c * 128:(c + 1) * 128], in_=ps[0:17, 0:128], func=AF.Copy), [psc], [C("ohst", c)])
                A_(lambda e, ps=ps: e.activation(out=ocst[0:51, c * 128:(c + 1) * 128], in_=ps[0:51, 128:256], func=AF.Copy), [psc], [C("ocst", c)])
                if c == 1:
                    S.op("sp", lambda e: e.dma_start(out=oh_d[l], in_=ohst[0:17, :]), reads=[C("ohst", 0), C("ohst", 1)], dma_key="o_h")
                    S.op("sp", lambda e: e.dma_start(out=ocv_d[l].rearrange("s j c -> (s j) c"), in_=ocst[0:51, :]),
                         reads=[C("ocst", 0), C("ocst", 1)], dma_key="o_cv")
                    out_proj(l, [6, 7], lambda i, t0, tn: yl[:, i, t0:t0 + tn], lambda i, j: [C("yl", i, j)])

        def unit_core(nh, qsel, kT, qfsel, Vx, Ktok, St0, Snew, scale_p, scale_s, has_den, post, tmps, chain1=False):
            Cst, Ctmp, Cbf, Pm, Pms, Km = tmps
            dsz = 128 // nh
            ow = 128 // nh
            if "core_p" not in SKIP:
                def emit_P(c):
                    t0 = c * 128
                    psP, psPc = PS(hold=True)
                    for h in range(nh):
                        M_(lambda e, psP=psP, h=h, t0=t0: e.matmul(
                            psP[:, h * 128:(h + 1) * 128], lhsT=kT[:, t0:t0 + 128], rhs=qsel(h)[:, t0:t0 + 128],
                            start=True, stop=True), [C("kT", c // 4), C("qT", c // 4)], [psPc])
                    return psP, psPc

                nextP = emit_P(0)
                for j in range(4):
                    psS, psSc = PS(hold=True)
                    for cc in range(4):
                        c = 4 * j + cc
                        for h in range(nh):
                            M_(lambda e, psS=psS, cc=cc, c=c, h=h: e.matmul(
                                psS[h * dsz:(h + 1) * dsz, cc * 128:(cc + 1) * 128], lhsT=Ktok[:, c, h * dsz:(h + 1) * dsz], rhs=Vx[:, c, h, :],
                                start=True, stop=True), [C("Ktok", j), C("Vx", j)], [psSc])
                    psO, psOc = PS(hold=True)
                    if has_den:
                        psD, psDc = PS(hold=True)
                    else:
                        psD, psDc = None, None
                    for cc in range(4):
                        c = 4 * j + cc
                        t0 = c * 128
                        psP, psPc = nextP
                        if c + 1 < 16:
                            nextP = emit_P(c + 1)
                        pslot = c % 4
                        V_(lambda e, psP=psP, pslot=pslot: e.tensor_tensor(
                            out=Pm[:, pslot, 0:nh, :], in0=psP[:, 0:nh * 128].rearrange("p (h t) -> p h t", h=nh), in1=maskc[:, 0:nh, :], op=ALU.mult),
                            [psPc, "cstb"], [C("Pm", pslot)])
                        PS_release(psPc)
                        slot = c % 3
                        src_ = psS[:, cc * 128:(cc + 1) * 128]
                        if chain1:
                            if c == 0:
                                V_(lambda e, src_=src_, slot=slot: e.tensor_copy(out=Cst[:, slot, :], in_=src_), [psSc], [C("Cst", slot)])
                            else:
                                V_(lambda e, src_=src_, slot=slot, c=c, ps_=(c - 1) % 3: e.scalar_tensor_tensor(
                                    out=Cst[:, slot, :], in0=Cst[:, ps_, :], scalar=scale_p(c), in1=src_, op0=ALU.mult, op1=ALU.add),
                                    [psSc, C("Cst", (c - 1) % 3)], [C("Cst", slot)])
                        else:
                            if c == 0:
                                V_(lambda e, src_=src_: e.tensor_copy(out=Ctmp[:, :], in_=src_), [psSc], [C("Ctmp")])
                            else:
                                V_(lambda e, src_=src_, ps_=(c - 1) % 3: e.tensor_tensor(out=Ctmp[:, :], in0=src_, in1=Cst[:, ps_, :], op=ALU.add),
                                   [psSc, C("Cst", (c - 1) % 3)], [C("Ctmp")])
                            A_(lambda e, c=c, slot=slot: e.activation(out=Cst[:, slot, :], in_=Ctmp[:, :], func=AF.Copy, scale=scale_p(c)),
                               [C("Ctmp")] + cells("eq", range(4)), [C("Cst", slot)])
                        if c < 15:
                            A_(lambda e, c=c, slot=slot: e.activation(out=Cbf[:, c + 1, :], in_=Cst[:, slot, :], func=AF.Copy),
                               [C("Cst", slot)], [C("Cbf", c + 1)])
                        for h in range(nh):
                            outs = [(psO, psOc, 0)] + ([(psD, psDc, 64)] if has_den else [])
                            for (pp, ppc, off) in outs:
                                M_(lambda e, pp=pp, h=h, c=c, cc=cc, pslot=pslot, off=off: e.matmul(
                                    pp[h * ow:(h + 1) * ow, cc * 128:(cc + 1) * 128], lhsT=Vx[:, c, h, off:off + ow], rhs=Pm[:, pslot, h, :],
                                    start=True, stop=(c == 0)), [C("Vx", j), C("Pm", pslot)], [ppc])
                                if c > 0:
                                    M_(lambda e, pp=pp, h=h, c=c, cc=cc, t0=t0, off=off: e.matmul(
                                        pp[h * ow:(h + 1) * ow, cc * 128:(cc + 1) * 128], lhsT=Cbf[:, c, off:off + ow],
                                        rhs=qsel(h)[:, t0:t0 + 128], start=False, stop=True),
                                        [C("Cbf", c), C("qT", j)], [ppc])
                        yield
                    PS_release(psSc)
                    post(j, psO, psOc, psD, psDc)
                    PS_release(psOc)
                    if has_den:
                        PS_release(psDc)
            if "core_s" not in SKIP:
                o = NPROMPT
                psP, psPc = PS()
                for h in range(nh):
                    M_(lambda e, h=h, psP=psP: e.matmul(
                        psP[0:64, h * 64:(h + 1) * 64], lhsT=kT[:, o:o + 64], rhs=qsel(h)[:, o:o + 64],
                        start=True, stop=True), [C("kT", 4), C("qT", 4)], [psPc])
                V_(lambda e, psP=psP: e.tensor_tensor(
                    out=Pms[0:64, 0:nh, :], in0=psP[0:64, 0:nh * 64].rearrange("p (h t) -> p h t", h=nh), in1=maskb[0:64, 0:nh, :], op=ALU.mult),
                    [psPc, "cstb"], [C("Pms")])
                psO, psOc = PS()
                if has_den:
                    psD, psDc = PS()
                else:
                    psD, psDc = None, None
                for h in range(nh):
                    outs = [(psO, psOc, 0)] + ([(psD, psDc, 64)] if has_den else [])
                    for (pp, ppc, off) in outs:
                        M_(lambda e, pp=pp, h=h, off=off: e.matmul(
                            pp[h * ow:(h + 1) * ow, 0:64], lhsT=Vx[0:64, 16, h, off:off + ow], rhs=Pms[0:64, h, :], start=True, stop=False),
                            [C("Vx", 4), C("Pms")], [ppc])
                        for b in range(16):
                            M_(lambda e, pp=pp, h=h, off=off, b=b: e.matmul(
                                pp[h * ow:(h + 1) * ow, 4 * b:4 * b + 4], lhsT=St0[:, b, off:off + ow],
                                rhs=qfsel(h)[:, 4 * b:4 * b + 4], start=False, stop=(b == 15)),
                                [C("St0"), C("qTf")], [ppc])
                post(4, psO, psOc, psD, psDc)
                yield
                for g in range(4):
                    psS, psSc = PS()
                    for bb in range(4):
                        b = 4 * g + bb
                        V_(lambda e, b=b, bb=bb: e.tensor_scalar(out=Km[0:64, bb, :], in0=Ktok[0:64, 16, :], scalar1=selb[0:64, b:b + 1], scalar2=None,
                                                                 op0=ALU.mult), [C("Ktok", 4), "cstf"], [C("Km", bb)])
                        for h in range(nh):
                            M_(lambda e, psS=psS, bb=bb, h=h: e.matmul(
                                psS[h * dsz:(h + 1) * dsz, bb * 128:(bb + 1) * 128], lhsT=Km[0:64, bb, h * dsz:(h + 1) * dsz], rhs=Vx[0:64, 16, h, :],
                                start=True, stop=True), [C("Km", bb), C("Vx", 4)], [psSc])
                    sc = scale_s(g)
                    if chain1:
                        V_(lambda e, psS=psS, g=g, sc=sc: e.scalar_tensor_tensor(
                            out=Snew[:, 4 * g:4 * g + 4, :], in0=St0[:, 4 * g:4 * g + 4, :], scalar=sc,
                            in1=psS[:, :].rearrange("p (b v) -> p b v", b=4), op0=ALU.mult, op1=ALU.add),
                            [psSc, C("St0")], [C("St0")])
                        yield
                        continue
                    V_(lambda e, psS=psS, g=g: e.tensor_tensor(out=Snew[:, 4 * g:4 * g + 4, :], in0=psS[:, :].rearrange("p (b v) -> p b v", b=4),
                                                               in1=St0[:, 4 * g:4 * g + 4, :], op=ALU.add),
                       [psSc, C("St0")], [C("St0")])
                    if isinstance(sc, float):
                        V_(lambda e, g=g, sc=sc: e.tensor_scalar(out=Snew[:, 4 * g:4 * g + 4, :], in0=Snew[:, 4 * g:4 * g + 4, :], scalar1=sc, scalar2=None,
                                                                 op0=ALU.mult), [C("St0")], [C("St0")])
                    else:
                        V_(lambda e, g=g, sc=sc: e.tensor_tensor(out=Snew[:, 4 * g:4 * g + 4, :], in0=Snew[:, 4 * g:4 * g + 4, :], in1=sc, op=ALU.mult),
                           [C("St0")] + cells("eq", [4]), [C("St0")])
            yield

        def carve_core_tmps(nh):
            Cst = carve([3, 128], F32)
            Ctmp = carve([128], F32)
            Cbf = carve([17, 128], BF16)
            Pm = carve([4, nh, 128], BF16)
            Pms = carve([nh, 64], BF16)
            Km = carve([4, 128], BF16)
            return (Cst, Ctmp, Cbf, Pm, Pms, Km)

        def head_norm_tile(j, t0, tn, Z, onesm, gcol, nt, dst_fn, dst_cells, extra_mul=None):
            Zb, Zq, sd = nt
            r = j % 2
            A_(lambda e: e.activation(out=Zb[:, r, 0:tn], in_=Z[:, t0:t0 + tn], func=AF.Copy), [C("Z", j)], [C("Zb", r)])
            psM, psMc = PS()
            M_(lambda e: e.matmul(psM[:, 0:tn], lhsT=onesm, rhs=Zb[:, r, 0:tn], start=True, stop=True), [C("Zb", r), "cstb"], [psMc])
            V_(lambda e: e.tensor_tensor(out=Z[:, t0:t0 + tn], in0=Z[:, t0:t0 + tn], in1=psM[:, 0:tn], op=ALU.subtract),
               [C("Z", j), psMc], [C("Z", j)])
            A_(lambda e: e.activation(out=Zq[:, r, 0:tn], in_=Z[:, t0:t0 + tn], func=AF.Square), [C("Z", j)], [C("Zq", r)])
            psV, psVc = PS()
            M_(lambda e: e.matmul(psV[:, 0:tn], lhsT=onesm, rhs=Zq[:, r, 0:tn], start=True, stop=True), [C("Zq", r), "cstb"], [psVc])
            A_(lambda e: e.activation(out=sd[:, r, 0:tn], in_=psV[:, 0:tn], func=AF.Sqrt, bias=EPS, scale=1.0), [psVc], [C("sd", r)])
            V_(lambda e: e.reciprocal(out=sd[:, r, 0:tn], in_=sd[:, r, 0:tn]), [C("sd", r)], [C("sd", r)])
            if extra_mul is None:
                V_(lambda e: e.scalar_tensor_tensor(out=dst_fn(t0, tn), in0=Z[:, t0:t0 + tn], scalar=gcol, in1=sd[:, r, 0:tn],
                                                    op0=ALU.mult, op1=ALU.mult), [C("Z", j), C("sd", r), "pv"], dst_cells(j))
            else:
                em, emc = extra_mul
                V_(lambda e: e.scalar_tensor_tensor(out=Z[:, t0:t0 + tn], in0=Z[:, t0:t0 + tn], scalar=gcol, in1=sd[:, r, 0:tn],
                                                    op0=ALU.mult, op1=ALU.mult), [C("Z", j), C("sd", r), "pv"], [C("Z", j)])
                V_(lambda e: e.tensor_tensor(out=dst_fn(t0, tn), in0=Z[:, t0:t0 + tn], in1=em, op=ALU.mult),
                   [C("Z", j)] + emc, dst_cells(j))

        def b_phase(l, wunit, kind, Z, onesm, gcol, kc_out, tail=None, bufs=None):
            if bufs is None:
                nsl = 5
                yT = carve([N], BF16)
                Zb = carve([5, 512], BF16)
                Zq = carve([5, 512], BF16)
                sd = carve([5, 512], F32)
                sgt = carve([5, 512], F32)
            else:
                nsl, yT, Zb, Zq, sd, sgt = bufs
            sl = load_wh(win_d[l, wunit])
            slo = load_wh(wout_d[l, kc_out])
            wv = wslh[:, sl, 0:1024].rearrange("p (k c) -> p k c", k=KC)
            gfun = AF.Sigmoid if kind == "ml" else AF.Silu

            def stages(j):
                t0, tn = TT[j]
                sj = j % nsl
                zc = C("Z", j)
                box = {}

                def s0():
                    ps, psc = PS()
                    for kc in range(KC):
                        M_(lambda e, kc=kc: e.matmul(ps[:, 0:tn], lhsT=wv[:, kc, :], rhs=hT[:, kc, t0:t0 + tn],
                                                     start=(kc == 0), stop=(kc == KC - 1)), [("wh", sl), ("hT", kc, j)], [psc])
                    A_(lambda e: e.activation(out=sgt[:, sj, 0:tn], in_=ps[:, 0:tn], func=gfun), [psc], [C("sgt", sj)])

                def s1():
                    if kind == "ml":
                        V_(lambda e: e.tensor_tensor(out=Z[:, t0:t0 + tn], in0=Z[:, t0:t0 + tn], in1=sgt[:, sj, 0:tn], op=ALU.mult),
                           [zc, C("sgt", sj)], [zc])
                    A_(lambda e: e.activation(out=Zb[:, sj, 0:tn], in_=Z[:, t0:t0 + tn], func=AF.Copy), [zc], [C("Zb", sj)])

                def s2():
                    box["m"] = PS(hold=True)
                    psM, psMc = box["m"]
                    M_(lambda e: e.matmul(psM[:, 0:tn], lhsT=onesm, rhs=Zb[:, sj, 0:tn], start=True, stop=True), [C("Zb", sj), "cstb"], [psMc])

                def s3():
                    psM, psMc = box["m"]
                    V_(lambda e: e.tensor_tensor(out=Z[:, t0:t0 + tn], in0=Z[:, t0:t0 + tn], in1=psM[:, 0:tn], op=ALU.subtract), [zc, psMc], [zc])
                    A_(lambda e: e.activation(out=Zq[:, sj, 0:tn], in_=Z[:, t0:t0 + tn], func=AF.Square), [zc], [C("Zq", sj)])
                    PS_release(psMc)

                def s4():
                    box["v"] = PS(hold=True)
                    psV, psVc = box["v"]
                    M_(lambda e: e.matmul(psV[:, 0:tn], lhsT=onesm, rhs=Zq[:, sj, 0:tn], start=True, stop=True), [C("Zq", sj), "cstb"], [psVc])

                def s5():
                    psV, psVc = box["v"]
                    A_(lambda e: e.activation(out=sd[:, sj, 0:tn], in_=psV[:, 0:tn], func=AF.Ln, bias=EPS, scale=1.0), [psVc], [C("sd", sj)])
                    A_(lambda e: e.activation(out=sd[:, sj, 0:tn], in_=sd[:, sj, 0:tn], func=AF.Exp, scale=-0.5), [C("sd", sj)], [C("sd", sj)])
                    PS_release(psVc)

                def s6():
                    if kind == "ml":
                        V_(lambda e: e.scalar_tensor_tensor(out=yT[:, t0:t0 + tn], in0=Z[:, t0:t0 + tn], scalar=gcol, in1=sd[:, sj, 0:tn],
                                                            op0=ALU.mult, op1=ALU.mult), [zc, C("sd", sj), "pv"], [C("yT", j)])
                    else:
                        V_(lambda e: e.scalar_tensor_tensor(out=Z[:, t0:t0 + tn], in0=Z[:, t0:t0 + tn], scalar=gcol, in1=sd[:, sj, 0:tn],
                                                            op0=ALU.mult, op1=ALU.mult), [zc, C("sd", sj), "pv"], [zc])
                        V_(lambda e: e.tensor_tensor(out=yT[:, t0:t0 + tn], in0=Z[:, t0:t0 + tn], in1=sgt[:, sj, 0:tn], op=ALU.mult),
                           [zc, C("sgt", sj)], [C("yT", j)])

                def s7():
                    for dc in range(KC):
                        ps, psc = PS()
                        M_(lambda e, ps=ps, dc=dc: e.matmul(ps[:, 0:tn], lhsT=wslh[:, slo, dc * 128:(dc + 1) * 128], rhs=yT[:, t0:t0 + tn],
                                                            start=True, stop=True), [("wh", slo), C("yT", j)], [psc])
                        if dc in POOL_DCS:
                            A_(lambda e, ps=ps: e.activation(out=sd[:, sj, 0:tn], in_=ps[:, 0:tn], func=AF.Copy), [psc], [C("sd", sj)])
                            S.op("pool", lambda e, dc=dc: e.tensor_tensor(out=xT[:, dc, t0:t0 + tn], in0=xT[:, dc, t0:t0 + tn],
                                                                         in1=sd[:, sj, 0:tn], op=ALU.add),
                                 [C("sd", sj), ("xT", dc, j)], [("xT", dc, j)])
                            continue
                        V_(lambda e, ps=ps, dc=dc: e.scalar_tensor_tensor(
                            out=xT[:, dc, t0:t0 + tn], in0=ps[:, 0:tn], scalar=1.0, in1=xT[:, dc, t0:t0 + tn], op0=ALU.mult, op1=ALU.add),
                            [psc, ("xT", dc, j)], [("xT", dc, j)])
                    if tail is not None and not hasattr(tail, "stages"):
                        tail(j)

                lst = [s0, s1, s2, s3, s4, s5, s6, s7]
                if tail is not None and hasattr(tail, "stages"):
                    lst += tail.stages(j)
                return lst

            sts = [stages(j) for j in range(5)]
            start = []
            for j in range(5):
                s_ = j if j == 0 else start[j - 1] + 1
                if j >= nsl:
                    s_ = max(s_, start[j - nsl] + 8)
                start.append(s_)
            nst = len(sts[0])
            for step in range(start[-1] + nst):
                for j in range(5):
                    s = step - start[j]
                    if 0 <= s < nst:
                        sts[j][s]()
                yield

        def mlstm(l, pr):
            new_phase("mlA%d_l%d" % (pr, l))
            Z = carve([N], F32)
            G1 = carve([N], F32)
            G2 = carve([N], F32)
            G3 = carve([N], F32)
            G4 = Z
            qT = carve([2, N], BF16)
            kT = carve([N], BF16)
            qTf = carve([2, 64], F32)
            Vx = carve([17, 2, 128], BF16)
            Ktok = carve([17, 128], BF16)
            St0 = carve([16, 128], F32)
            Snew = St0
            tmps = carve_core_tmps(2)
            dn = carve([2, 512], F32)
            M0 = carve([16], F32)
            m0T = carve([16], F32)
            msave = carve([32], F32)
            nst = carve([128], F32)
            n0st = carve([128], F32)
            u0 = 4 + 6 * pr
            hs = slice(2 * pr, 2 * pr + 2)
            S.op("sp", lambda e: e.dma_start(out=St0[:, :, 0:64], in_=stC_d[l, :, hs].rearrange("b h k v -> (h k) b v")),
                 writes=[C("St0")], dma_key="St0")
            S.op("sp", lambda e: e.dma_start(out=n0st[0:16, :], in_=stn_d[l, :, 128 * pr:128 * pr + 128]), writes=[C("n0st")], dma_key="n0st")
            S.op("sp", lambda e: e.dma_start(out=m0T[0:4, :], in_=stm_d[l].rearrange("b h -> h b"), allow_slow_non_contiguous=True),
                 writes=[C("m0T")], dma_key="m0T")
            ps, psc = PS()
            M_(lambda e, ps=ps: e.transpose(out=ps[:, 0:16], in_=n0st[0:16, :], identity=ident[0:16, 0:16]), [C("n0st"), "ident"], [psc])
            M_(lambda e, ps=ps: e.matmul(ps[:, 64:80], lhsT=sel4[0:4, pr, :], rhs=m0T[0:4, :], start=True, stop=True),
               [C("m0T"), "cstf"], [psc])
            V_(lambda e, ps=ps: e.tensor_copy(out=St0[:, :, 64:128], in_=ps[:, 0:16].unsqueeze(2).broadcast_to([128, 16, 64])),
               [psc], [C("St0")])
            V_(lambda e, ps=ps: e.tensor_copy(out=M0[:, :], in_=ps[:, 64:80]), [psc], [C("M0")])
            V_(lambda e: e.memset(Vx[:, :, :, 64:128], 1.0), [], cells("Vx"))
            V_(lambda e: e.memset(qT[:, :, :], 0.0), [], cells("qT"))
            V_(lambda e: e.memset(qTf[:, :, :], 0.0), [], [C("qTf")])
            sl = load_wh(win_d[l, u0 + 0])

            def c_li(j, t0, tn, ps, psc):
                A_(lambda e: e.activation(out=G1[:, t0:t0 + tn], in_=ps[:, 0:tn], func=AF.Identity, bias=pvcol(("bifi", l), pr), scale=1.0),
                   [psc, "pv"], [C("G1", j)])
            proj_fm(sl, c_li)
            sl = load_wh(win_d[l, u0 + 1])

            def c_lf(j, t0, tn, ps, psc):
                A_(lambda e: e.activation(out=G2[:, t0:t0 + tn], in_=ps[:, 0:tn], func=AF.Sigmoid, bias=pvcol(("biff", l), pr), scale=1.0),
                   [psc, "pv"], [C("G2", j)])
            proj_fm(sl, c_lf)
            sl = load_wh(win_d[l, u0 + 4])

            def e_v(j, nb, rows, ps, psc):
                A_(lambda e: e.activation(out=Vx[0:rows, 4 * j:4 * j + nb, :, 0:64],
                                          in_=ps[0:rows, 0:nb * 128].rearrange("p (b h v) -> p b h v", b=nb, h=2), func=AF.Copy),
                   [psc], [C("Vx", j)])
            proj_tok(sl, e_v)
            A_(lambda e: e.activation(out=G2[:, :], in_=G2[:, :], func=AF.Ln), cells("G2"), cells("G2"))
            V_(lambda e: e.tensor_tensor_scan(out=G3[:, 0:NPROMPT], data0=G2[:, 0:NPROMPT], data1=G1[:, 0:NPROMPT], initial=0.0,
                                              op0=ALU.add, op1=ALU.max), cells("G2", range(4)) + cells("G1", range(4)), cells("G3", range(4)))
            for t in range(4):
                prev = M0[:, :] if t == 0 else sv(G3)[:, :, t - 1]
                V_(lambda e, t=t, prev=prev: e.tensor_tensor(out=sv(G4)[:, :, t], in0=sv(G2)[:, :, t], in1=prev, op=ALU.add),
                   cells("G2", [4]) + cells("G3", [4]) + [C("M0")], cells("Z", [4]))
                V_(lambda e, t=t: e.tensor_tensor(out=sv(G3)[:, :, t], in0=sv(G4)[:, :, t], in1=sv(G1)[:, :, t], op=ALU.max),
                   cells("Z", [4]) + cells("G1", [4]), cells("G3", [4]))
            V_(lambda e: e.tensor_copy(out=msave[:, 0:1], in_=G3[:, NPROMPT - 1:NPROMPT]), cells("G3", [3]), [C("msave")])
            V_(lambda e: e.tensor_copy(out=msave[:, 1:17], in_=sv(G3)[:, :, 3]), cells("G3", [4]), [C("msave")])
            for c in range(16):
                t0 = c * 128
                V_(lambda e, t0=t0: e.tensor_tensor_scan(out=G4[:, t0:t0 + 128], data0=onesf[:, :], data1=G2[:, t0:t0 + 128], initial=0.0,
                                                         op0=ALU.mult, op1=ALU.add), cells("G2", [c // 4]) + ["cstf"], cells("Z", [c // 4]))
                if c > 0:
                    V_(lambda e, t0=t0: e.tensor_scalar(out=G4[:, t0:t0 + 128], in0=G4[:, t0:t0 + 128], scalar1=G3[:, t0 - 1:t0], scalar2=None,
                                                        op0=ALU.add), cells("Z", [c // 4]) + cells("G3", [(c * 128 - 1) // 512]), cells("Z", [c // 4]))
            V_(lambda e: e.tensor_copy(out=sv(G4)[:, :, 0], in_=sv(G2)[:, :, 0]), cells("G2", [4]), cells("Z", [4]))
            for t in range(1, 4):
                V_(lambda e, t=t: e.tensor_tensor(out=sv(G4)[:, :, t], in0=sv(G4)[:, :, t - 1], in1=sv(G2)[:, :, t], op=ALU.add),
                   cells("Z", [4]) + cells("G2", [4]), cells("Z", [4]))
            V_(lambda e: e.tensor_tensor(out=sv(G4), in0=sv(G4), in1=M0[:, :].unsqueeze(2).broadcast_to([128, 16, 4]), op=ALU.add),
               cells("Z", [4]) + [C("M0")], cells("Z", [4]))
            V_(lambda e: e.tensor_tensor(out=G2[:, :], in0=G4[:, :], in1=G3[:, :], op=ALU.subtract), cells("Z") + cells("G3"), cells("G2"))
            V_(lambda e: e.tensor_tensor(out=G1[:, :], in0=G4[:, :], in1=G1[:, :], op=ALU.subtract), cells("Z") + cells("G1"), cells("G1"))
            A_(lambda e: e.activation(out=G2[:, :], in_=G2[:, :], func=AF.Exp), cells("G2"), cells("G2") + cells("eq"))
            A_(lambda e: e.activation(out=G1[:, :], in_=G1[:, :], func=AF.Exp, scale=-1.0), cells("G1"), cells("G1"))
            A_(lambda e: e.activation(out=G3[:, :], in_=G3[:, :], func=AF.Exp, scale=-1.0), cells("G3") + [C("msave")], cells("G3"))
            sl = load_wh(win_d[l, u0 + 2])

            def c_q(j, t0, tn, ps, psc):
                for hh in range(2):
                    pq = slice(64 * hh, 64 * hh + 64)
                    V_(lambda e, hh=hh, pq=pq: e.tensor_tensor(out=qT[pq, hh, t0:t0 + tn], in0=ps[pq, 0:tn], in1=G2[pq, t0:t0 + tn], op=ALU.mult),
                       [psc, C("G2", j)], [C("qT", j)])
                    if j == 4:
                        V_(lambda e, hh=hh, pq=pq: e.tensor_tensor(out=qTf[pq, hh, :], in0=ps[pq, 0:64], in1=G2[pq, t0:t0 + 64], op=ALU.mult),
                           [psc, C("G2", j)], [C("qTf")])
            proj_fm(sl, c_q)
            sl = load_wh(win_d[l, u0 + 3])

            def c_k(j, t0, tn, ps, psc):
                V_(lambda e: e.scalar_tensor_tensor(out=kT[:, t0:t0 + tn], in0=ps[:, 0:tn], scalar=0.125, in1=G1[:, t0:t0 + tn],
                                                    op0=ALU.mult, op1=ALU.mult), [psc, C("G1", j)], [C("kT", j)])
            proj_fm(sl, c_k)
            to_tok(kT, "kT", Ktok, "Ktok")
            eqs = lambda c: G2[:, c * 128 + 127:c * 128 + 128]
            eqs_s = lambda g: G2[:, NPROMPT + 16 * g:NPROMPT + 16 * g + 16].rearrange("p (b t) -> p b t", t=4)[:, :, 3:4].broadcast_to([128, 4, 128])

            def post(j, psO, psOc, psD, psDc):
                t0, tn = TT[j]
                r = j % 2
                A_(lambda e: e.activation(out=dn[:, r, 0:tn], in_=psD[:, 0:tn], func=AF.Abs), [psDc], [C("dn", r)])
                V_(lambda e: e.tensor_tensor(out=dn[:, r, 0:tn], in0=dn[:, r, 0:tn], in1=G3[:, t0:t0 + tn], op=ALU.max),
                   [C("dn", r), C("G3", j)], [C("dn", r)])
                A_(lambda e: e.activation(out=dn[:, r, 0:tn], in_=dn[:, r, 0:tn], func=AF.Ln), [C("dn", r)], [C("dn", r)])
                A_(lambda e: e.activation(out=dn[:, r, 0:tn], in_=dn[:, r, 0:tn], func=AF.Exp, scale=-1.0), [C("dn", r)], [C("dn", r)])
                V_(lambda e: e.tensor_tensor(out=Z[:, t0:t0 + tn], in0=psO[:, 0:tn], in1=dn[:, r, 0:tn], op=ALU.mult),
                   [psOc, C("dn", r)], [C("Z", j)])
            if "m_core" not in SKIP:
                for _ in unit_core(2, lambda h: qT[:, h, :], kT, lambda h: qTf[:, h, :], Vx, Ktok, St0, Snew, eqs, eqs_s, True, post, tmps):
                    pass
            if "m_out" not in SKIP:
                Cst = tmps[0]
                fin = 15 % 3
                S.op("sp", lambda e: e.dma_start(out=oC_d[l, 0, hs].rearrange("h k v -> (h k) v"), in_=Cst[:, fin, 0:64]),
                     reads=[C("Cst", fin)], dma_key="o_C")
                S.op("sp", lambda e: e.dma_start(out=oC_d[l, 1:17, hs].rearrange("b h k v -> (h k) b v"), in_=Snew[:, :, 0:64]),
                     reads=[C("St0")], dma_key="o_C")
                ps, psc = PS()
                M_(lambda e, ps=ps: e.transpose(out=ps[0:32, 0:128], in_=Cst[:, fin, 64:96], identity=ident[:]), [C("Cst", fin), "ident"], [psc])
                M_(lambda e, ps=ps: e.transpose(out=ps[0:16, 128:256], in_=Snew[:, :, 64], identity=ident[:]),
                   [C("St0")] + ["ident"], [psc])
                A_(lambda e, ps=ps: e.activation(out=nst[0:1, :], in_=ps[0:1, 0:128], func=AF.Copy), [psc], [C("nst")])
                A_(lambda e, ps=ps: e.activation(out=n0st[0:16, :], in_=ps[0:16, 128:256], func=AF.Copy), [psc], [C("n0st")])
                S.op("sp", lambda e: e.dma_start(out=on_d[l, 0:1, 128 * pr:128 * pr + 128], in_=nst[0:1, :]), reads=[C("nst")], dma_key="o_n")
                S.op("sp", lambda e: e.dma_start(out=on_d[l, 1:17, 128 * pr:128 * pr + 128], in_=n0st[0:16, :]), reads=[C("n0st")], dma_key="o_n")
                for h in range(2):
                    S.op("sp", lambda e, h=h: e.dma_start(out=om_d[l, :, 2 * pr + h:2 * pr + h + 1].rearrange("s o -> o s"),
                                                          in_=msave[64 * h:64 * h + 1, 0:17], allow_slow_non_contiguous=True),
                         reads=[C("msave")], dma_key="o_m")
            new_phase("mlB%d_l%d" % (pr, l))
            Z = carve([N], F32)
            for _ in b_phase(l, u0 + 5, "ml", Z, bones64[:, :], pvcol(("gmn", l), pr), pr):
                pass

        def ret_A(l, h, A):
            (Z, cs, dec, T1, T2, qT, kT, qTf, Vx, Ktok, St0, tmps) = A
            Snew = St0
            u0 = 16 + 6 * h
            gam = 1.0 - 2.0 ** (-5.0 - h)
            S.op("sp", lambda e: e.dma_start(out=dec[:, :, :], in_=dect_d[h]), writes=[C("dec")], dma_key="dec")
            S.op("sp", lambda e: e.dma_start(out=St0[:, :, :], in_=stS_d[l, :, h].rearrange("b k v -> k b v")), writes=[C("St0")], dma_key="St0r")
            for (which, dstT, dname, uq) in ((0, qT, "qT", 0), (1, kT, "kT", 2)):
                sla = load_wh(win_d[l, u0 + uq])
                slb = load_wh(win_d[l, u0 + uq + 1])
                wva = wslh[:, sla, 0:1024].rearrange("p (k c) -> p k c", k=KC)
                wvb = wslh[:, slb, 0:1024].rearrange("p (k c) -> p k c", k=KC)
                for j, (t0, tn) in enumerate(TT):
                    pa, pac = PS()
                    pb, pbc = PS()
                    for (pp, ppc, wv_, sl_) in ((pa, pac, wva, sla), (pb, pbc, wvb, slb)):
                        for kc in range(KC):
                            M_(lambda e, pp=pp, kc=kc, t0=t0, tn=tn, wv_=wv_: e.matmul(
                                pp[:, 0:tn], lhsT=wv_[:, kc, :], rhs=hT[:, kc, t0:t0 + tn], start=(kc == 0), stop=(kc == KC - 1)),
                                [("wh", sl_), ("hT", kc, j)], [ppc])
                    V_(lambda e, pa=pa, t0=t0, tn=tn: e.tensor_tensor(out=T1[:, 0:tn], in0=pa[:, 0:tn], in1=cs[:, 0, t0:t0 + tn], op=ALU.mult),
                       [pac, C("cs")], [C("T1")])
                    V_(lambda e, pb=pb, t0=t0, tn=tn: e.tensor_tensor(out=T2[:, 0:tn], in0=pb[:, 0:tn], in1=cs[:, 1, t0:t0 + tn], op=ALU.mult),
                       [pbc, C("cs")], [C("T2")])
                    V_(lambda e, tn=tn: e.tensor_tensor(out=T1[:, 0:tn], in0=T1[:, 0:tn], in1=T2[:, 0:tn], op=ALU.add),
                       [C("T1"), C("T2")], [C("T1")])
                    if j < 4:
                        V_(lambda e, t0=t0, dstT=dstT, which=which: e.tensor_tensor(
                            out=dstT[:, t0:t0 + 512].rearrange("p (c t) -> p c t", c=4), in0=T1[:, :].rearrange("p (c t) -> p c t", c=4),
                            in1=dec[:, which, 0:128].unsqueeze(1).broadcast_to([128, 4, 128]), op=ALU.mult),
                            [C("T1"), C("dec")], [C(dname, j)])
                    else:
                        V_(lambda e, t0=t0, dstT=dstT, which=which: e.tensor_tensor(
                            out=dstT[:, t0:t0 + 64], in0=T1[:, 0:64], in1=dec[:, which, 128:192], op=ALU.mult),
                            [C("T1"), C("dec")], [C(dname, j)])
                        if which == 0:
                            V_(lambda e: e.tensor_tensor(out=qTf[:, :], in0=T1[:, 0:64], in1=dec[:, 0, 128:192], op=ALU.mult),
                               [C("T1"), C("dec")], [C("qTf")])
                    yield
            sl = load_wh(win_d[l, u0 + 4])

            def e_v(j, nb, rows, ps, psc):
                A_(lambda e: e.activation(out=Vx[0:rows, 4 * j:4 * j + nb, 0, :],
                                          in_=ps[0:rows, 0:nb * 128].rearrange("p (b v) -> p b v", b=nb), func=AF.Copy),
                   [psc], [C("Vx", j)])
            proj_tok(sl, e_v)
            yield
            to_tok(kT, "kT", Ktok, "Ktok", scale_fn=lambda j: float(gam ** 128) if j < 4 else float(gam ** 4))
            yield

            def post(j, psO, psOc, psD, psDc):
                t0, tn = TT[j]
                A_(lambda e: e.activation(out=Z[:, t0:t0 + tn], in_=psO[:, 0:tn], func=AF.Copy), [psOc], [C("Z", j)])
            yield from unit_core(1, lambda h_: qT, kT, lambda h_: qTf, Vx, Ktok, St0, Snew,
                                 lambda c: float(gam ** 128), lambda g: float(gam ** 4), False, post, tmps, chain1=True)
            Cst = tmps[0]
            fin = 15 % 3
            S.op("sp", lambda e: e.dma_start(out=oS_d[l, 0, h], in_=Cst[:, fin, :]), reads=[C("Cst", fin)], dma_key="o_S")
            S.op("sp", lambda e: e.dma_start(out=oS_d[l, 1:17, h].rearrange("b k v -> k b v"), in_=Snew[:, :, :]),
                 reads=[C("St0")], dma_key="o_S")
            yield

        def retention_section(l, tail=None):
            new_phase("ret_l%d" % l)
            for n_ in ("Z", "yT", "Zb", "Zq", "sd", "sgt"):
                arena_of[n_] = "scrB"
            for n_ in ("cs", "dec", "T1", "T2", "qT", "kT", "qTf", "Vx", "Ktok", "St0", "Cst", "Ctmp", "Cbf", "Pm", "Pms", "Km"):
                arena_of[n_] = "scrA"
            Z = carve([N], F32, "scrB")
            yT = carve([N], BF16, "scrB")
            Bb = (3, yT, carve([3, 512], BF16, "scrB"), carve([3, 512], BF16, "scrB"), carve([3, 512], F32, "scrB"),
                  carve([3, 512], BF16, "scrB"))
            cs = carve([2, N], F32, "scrA")
            dec = carve([2, 192], F32, "scrA")
            T1 = carve([512], F32, "scrA")
            T2 = carve([512], F32, "scrA")
            qT = carve([N], BF16, "scrA")
            kT = carve([N], BF16, "scrA")
            qTf = carve([64], F32, "scrA")
            Vx = carve([17, 1, 128], BF16, "scrA")
            Ktok = carve([17, 128], BF16, "scrA")
            St0 = carve([16, 128], F32, "scrA")
            tmps = (carve([3, 128], F32, "scrA"), carve([128], F32, "scrA"), carve([16, 128], BF16, "scrA"),
                    carve([4, 1, 128], BF16, "scrA"), carve([1, 64], BF16, "scrA"), carve([4, 128], BF16, "scrA"))
            A = (Z, cs, dec, T1, T2, qT, kT, qTf, Vx, Ktok, St0, tmps)
            S.op("sp", lambda e: e.dma_start(out=cs[:, :, :], in_=rope_d), writes=[C("cs")], dma_key="cs")

            def drive(gens):
                live = [[g, nm, pl] for g, nm, pl in gens if g is not None]
                while live:
                    for it in list(live):
                        S.scope = it[1]
                        state["pspool"] = it[2] if len(live) > 1 else None
                        try:
                            next(it[0])
                        except StopIteration:
                            live.remove(it)

            drive([(ret_A(l, 0, A), "rtA0_l%d" % l, None)])
            for h in range(4):
                gb = b_phase(l, 16 + 6 * h + 5, "rt", Z, ones128[:, :], pvcol(("grn", l), h), 2 + h,
                             tail=(tail if h == 3 else None), bufs=Bb)
                ga = ret_A(l, h + 1, A) if h < 3 else None
                drive([(gb, "rtB%d_l%d" % (h, l), "B"), (ga, "rtA%d_l%d" % (h + 1, l), "A")])
            state["pspool"] = None
            arena_of.clear()

        def mixer(l, next_norm=None):
            S.alias_barrier(["w", "wh"])
            if "lru" in PARTS:
                lru(l)
            if "mlstm" in PARTS:
                for pr in range(2):
                    mlstm(l, pr)
            if "ret" in PARTS:
                retention_section(l, tail=next_norm)

        load_x(tail=norm_hT_tile(("nf1", 0)))
        for l in range(DEPTH):
            ffn(l, 1, next_norm=norm_hT_tile(("nmx", l)))
            mixer(l, next_norm=norm_hT_tile(("nf2", l)))
            ffn(l, 2, next_norm=(norm_hT_tile(("nf1", l + 1)) if l + 1 < DEPTH else "final"))
        S.emit(final_wait_keys=[("ost", 0), ("ost", 1), "o_h", "o_cv", "o_C", "o_n", "o_m", "o_S"])
    return nc


_NC_CACHE = {}


def kernel(**inputs):
    inp = {k: np.asarray(v) for k, v in inputs.items()}
    shared, per_core = _host_prep(inp)
    if "nc" not in _NC_CACHE:
        _NC_CACHE["nc"] = build()
    nc = _NC_CACHE["nc"]
    in_maps = [dict(shared, **pc) for pc in per_core]
    res = run_bass_kernel_spmd(nc, in_maps, core_ids=list(range(8)))
    r = res.results
    y = np.stack([r[c]["y"] for c in range(8)])
    y_prompt = np.ascontiguousarray(y[:, :NPROMPT, :])
    y_sample = np.ascontiguousarray(y[:, NPROMPT:, :].reshape(128, 4, D))

    def split(name, tail):
        a = np.stack([r[c][name] for c in range(8)])
        p = np.ascontiguousarray(a[:, :, 0].transpose((1, 0) + tuple(range(2, a.ndim - 1)))).reshape((DEPTH, 8) + tail)
        s = a[:, :, 1:17].transpose((1, 0, 2) + tuple(range(3, a.ndim)))
        s = np.ascontiguousarray(s).reshape((DEPTH, 128) + tail)
        return p, s

    pC, sC = split("oC", (4, 64, 64))
    pn, sn = split("on", (4, 64))
    pm, sm = split("om", (4,))
    pS, sS = split("oS", (4, 128, 128))
    ph, sh = split("oh", (256,))
    pcv, scv = split("ocv", (3, 256))
    return (y_prompt, y_sample, pC, pn, pm, pS, ph, pcv, sC, sn, sm, sS, sh, scv)
```
